# Optimizing a Trainium2 kernel written in Bass

```python
import math
import jax, jax.numpy as jnp
from jax import lax
import numpy as np

D_MODEL = 2048
BATCH = 32
SEQ = 256
DEPTH = 4
DEC_BATCH = 8
DEC_SEQ = 4096
PAST_LEN = 512

GRID_W = 64
N_MIXERS = 2
N_A_LAYERS = (DEPTH + 1) // 2
N_B_LAYERS = DEPTH // 2
BLOCK = 128
A_HEADS = 16
A_KV_HEADS = 4
A_GROUPS = A_HEADS // A_KV_HEADS
A_HEAD_DIM = D_MODEL // A_HEADS
A_WINDOW = 128
B_HEADS = 8
B_QK_DIM = D_MODEL // (2 * B_HEADS)
B_V_DIM = 2 * B_QK_DIM
D_FF = 5632
CONV_WIDTH = 3
ROPE_BASE = 10000.0
EPS = 1e-6
NEG_INF = -1e30

kernel_name = 'hybrid_diffusion_window_sink_diff_attn_convglu'


def _rmsnorm(x, g):
    xf = x.astype(jnp.float32)
    y = xf * lax.rsqrt(jnp.mean(xf * xf, axis=-1, keepdims=True) + EPS)
    return (y * g.astype(jnp.float32)).astype(x.dtype)


def _modulation(cond, w_ada, b_ada):
    m = jax.nn.silu(cond) @ w_ada + b_ada
    return [t[:, None, :] for t in jnp.split(m, 6, axis=-1)]


def _adaln(x, g, shift, scale):
    return _rmsnorm(x, g) * (1 + scale) + shift


def _axial_angles(n, dim):
    rows = n // GRID_W
    row = jnp.repeat(jnp.arange(rows, dtype=jnp.float32), GRID_W)
    col = jnp.tile(jnp.arange(GRID_W, dtype=jnp.float32), rows)
    half = dim // 2
    inv = ROPE_BASE ** (-jnp.arange(0, half, 2, dtype=jnp.float32) / half)
    ang = jnp.concatenate([row[:, None] * inv, col[:, None] * inv], axis=-1)
    return jnp.cos(ang), jnp.sin(ang)


def _rotate(x, cos, sin):
    x1, x2 = jnp.split(x, 2, axis=-1)
    c = cos[None, :, None, :]
    s = sin[None, :, None, :]
    return jnp.concatenate([x1 * c - x2 * s, x2 * c + x1 * s], axis=-1)


def _axial_rope(x, cos, sin):
    half = x.shape[-1] // 2
    quarter = half // 2
    xf = x.astype(jnp.float32)
    out = jnp.concatenate([
        _rotate(xf[..., :half], cos[:, :quarter], sin[:, :quarter]),
        _rotate(xf[..., half:], cos[:, quarter:], sin[:, quarter:])], axis=-1)
    return out.astype(x.dtype)


def _to_blocks(q):
    b, n = q.shape[:2]
    return jnp.moveaxis(q.reshape((b, n // BLOCK, BLOCK) + q.shape[2:]), 1, 0)


def _from_blocks(o):
    nb, b, t, f = o.shape
    return jnp.moveaxis(o, 0, 1).reshape(b, nb * t, f)


def _a_project(h, w_qkv, gq, gk):
    b, n, _ = h.shape
    q, k, v = jnp.split(h @ w_qkv, [A_HEADS * A_HEAD_DIM, (A_HEADS + A_KV_HEADS) * A_HEAD_DIM], axis=-1)
    q = _rmsnorm(q.reshape(b, n, A_HEADS, A_HEAD_DIM), gq)
    k = _rmsnorm(k.reshape(b, n, A_KV_HEADS, A_HEAD_DIM), gk)
    v = v.reshape(b, n, A_KV_HEADS, A_HEAD_DIM)
    return q, k, v


def _gqa_sink_attend(q, k, v, sink, mask):
    b, tq = q.shape[:2]
    s = jnp.einsum('bqkgd,bskd->bkgqs', q.astype(jnp.float32), k.astype(jnp.float32)) * (A_HEAD_DIM ** -0.5)
    if mask is not None:
        s = jnp.where(mask, s, NEG_INF)
    sk = sink.astype(jnp.float32).reshape(1, A_KV_HEADS, A_GROUPS, 1, 1)
    m = jnp.maximum(jnp.max(s, axis=-1, keepdims=True), sk)
    p = jnp.exp(s - m)
    p = p / (jnp.sum(p, axis=-1, keepdims=True) + jnp.exp(sk - m))
    o = jnp.einsum('bkgqs,bskd->bqkgd', p, v.astype(jnp.float32))
    return o.reshape(b, tq, A_HEADS * A_HEAD_DIM).astype(v.dtype)


def _a_context_attention(q, k, v, sink):
    b, n = q.shape[:2]
    qb = _to_blocks(q.reshape(b, n, A_KV_HEADS, A_GROUPS, A_HEAD_DIM))
    o = lax.map(lambda q_blk: _gqa_sink_attend(q_blk, k, v, sink, None), qb)
    return _from_blocks(o)


def _a_latent_attention(q, k, v, k_ctx, v_ctx, sink):
    b, n = q.shape[:2]
    nb = n // BLOCK
    qb = _to_blocks(q.reshape(b, n, A_KV_HEADS, A_GROUPS, A_HEAD_DIM))
    pad = ((0, 0), (A_WINDOW, A_WINDOW), (0, 0), (0, 0))
    k_pad = jnp.pad(k, pad)
    v_pad = jnp.pad(v, pad)
    span = BLOCK + 2 * A_WINDOW
    q_off = jnp.arange(BLOCK)
    k_off = jnp.arange(span) - A_WINDOW
    ctx_mask = jnp.ones((BLOCK, k_ctx.shape[1]), dtype=bool)

    def one_block(args):
        j, q_blk = args
        start = j * BLOCK
        k_win = lax.dynamic_slice_in_dim(k_pad, start, span, axis=1)
        v_win = lax.dynamic_slice_in_dim(v_pad, start, span, axis=1)
        q_pos = start + q_off
        k_pos = start + k_off
        win_mask = ((jnp.abs(q_pos[:, None] - k_pos[None, :]) <= A_WINDOW)
                    & (k_pos >= 0)[None, :] & (k_pos < n)[None, :])
        mask = jnp.concatenate([win_mask, ctx_mask], axis=1)
        k_all = jnp.concatenate([k_win, k_ctx.astype(k_win.dtype)], axis=1)
        v_all = jnp.concatenate([v_win, v_ctx.astype(v_win.dtype)], axis=1)
        return _gqa_sink_attend(q_blk, k_all, v_all, sink, mask)

    o = lax.map(one_block, (jnp.arange(nb), qb))
    return _from_blocks(o)


def _b_project(h, w_qkv, gq, gk):
    b, n, _ = h.shape
    q, k, v = jnp.split(h @ w_qkv, 3, axis=-1)
    q = _rmsnorm(q.reshape(b, n, B_HEADS, 2, B_QK_DIM), gq)
    k = _rmsnorm(k.reshape(b, n, B_HEADS, 2, B_QK_DIM), gk)
    v = v.reshape(b, n, B_HEADS, B_V_DIM)
    return q, k, v


def _b_rope(x, cos, sin):
    b, n = x.shape[:2]
    return _axial_rope(x.reshape(b, n, B_HEADS * 2, B_QK_DIM), cos, sin).reshape(x.shape)


def _diff_attend(q, k, v, lam, subln_g, lambda_init):
    b, tq = q.shape[:2]
    s = jnp.einsum('bqhcd,bshcd->bhcqs', q.astype(jnp.float32), k.astype(jnp.float32)) * (B_QK_DIM ** -0.5)
    a = jax.nn.softmax(s, axis=-1)
    attn = a[:, :, 0] - lam * a[:, :, 1]
    o = jnp.einsum('bhqs,bshd->bqhd', attn, v.astype(jnp.float32))
    o = _rmsnorm(o, subln_g) * (1.0 - lambda_init)
    return o.reshape(b, tq, B_HEADS * B_V_DIM).astype(v.dtype)


def _b_attention(q, k_all, v_all, lam, subln_g, lambda_init):
    o = lax.map(lambda q_blk: _diff_attend(q_blk, k_all, v_all, lam, subln_g, lambda_init), _to_blocks(q))
    return _from_blocks(o)


def _conv_glu(h, w_up, conv_w, conv_b, w_down):
    g, v = jnp.split(h @ w_up, 2, axis=-1)
    n = g.shape[1]
    gp = jnp.pad(g, ((0, 0), (1, 1), (0, 0)))
    gc = gp[:, :n] * conv_w[0] + gp[:, 1:n + 1] * conv_w[1] + gp[:, 2:] * conv_w[2] + conv_b
    return (jax.nn.silu(gc) * v) @ w_down


def setup_inputs(seed: int = 0) -> dict:
    key = jax.random.key(seed)
    ks = jax.random.split(key, 32)
    f32 = jnp.float32

    def nrm(k, shape, scale):
        return jax.random.normal(k, shape, f32) * scale

    qkv_a = (A_HEADS + 2 * A_KV_HEADS) * A_HEAD_DIM
    return {
        'x_prompt': nrm(ks[0], (BATCH, SEQ, D_MODEL), 1.0),
        'x_sample': nrm(ks[1], (DEC_BATCH, DEC_SEQ, D_MODEL), 1.0),
        'cache_a_k': nrm(ks[2], (DEC_BATCH, N_A_LAYERS, PAST_LEN, A_KV_HEADS, A_HEAD_DIM), 1.0),
        'cache_a_v': nrm(ks[3], (DEC_BATCH, N_A_LAYERS, PAST_LEN, A_KV_HEADS, A_HEAD_DIM), 1.0),
        'cache_b_k': nrm(ks[4], (DEC_BATCH, N_B_LAYERS, PAST_LEN, B_HEADS, 2, B_QK_DIM), 1.0),
        'cache_b_v': nrm(ks[5], (DEC_BATCH, N_B_LAYERS, PAST_LEN, B_HEADS, B_V_DIM), 1.0),
        'c': nrm(ks[6], (DEC_BATCH, D_MODEL), 1.0),
        'c_ctx': nrm(ks[7], (D_MODEL,), 1.0),
        'ada_w': nrm(ks[8], (DEPTH, D_MODEL, 6 * D_MODEL), 0.5 * D_MODEL ** -0.5),
        'ada_b': nrm(ks[9], (DEPTH, 6 * D_MODEL), 0.02),
        'norm1_g': 1.0 + nrm(ks[10], (DEPTH, D_MODEL), 0.02),
        'norm2_g': 1.0 + nrm(ks[11], (DEPTH, D_MODEL), 0.02),
        'a_w_qkv': nrm(ks[12], (N_A_LAYERS, D_MODEL, qkv_a), D_MODEL ** -0.5),
        'a_q_norm': 1.0 + nrm(ks[13], (N_A_LAYERS, A_HEAD_DIM), 0.02),
        'a_k_norm': 1.0 + nrm(ks[14], (N_A_LAYERS, A_HEAD_DIM), 0.02),
        'a_sink': nrm(ks[15], (N_A_LAYERS, A_HEADS), 0.5),
        'a_w_o': nrm(ks[16], (N_A_LAYERS, A_HEADS * A_HEAD_DIM, D_MODEL), (A_HEADS * A_HEAD_DIM) ** -0.5),
        'b_w_qkv': nrm(ks[17], (N_B_LAYERS, D_MODEL, 3 * D_MODEL), D_MODEL ** -0.5),
        'b_q_norm': 1.0 + nrm(ks[18], (N_B_LAYERS, B_QK_DIM), 0.02),
        'b_k_norm': 1.0 + nrm(ks[19], (N_B_LAYERS, B_QK_DIM), 0.02),
        'b_lambda_q1': nrm(ks[20], (N_B_LAYERS, B_QK_DIM), 0.1),
        'b_lambda_k1': nrm(ks[21], (N_B_LAYERS, B_QK_DIM), 0.1),
        'b_lambda_q2': nrm(ks[22], (N_B_LAYERS, B_QK_DIM), 0.1),
        'b_lambda_k2': nrm(ks[23], (N_B_LAYERS, B_QK_DIM), 0.1),
        'b_subln': 1.0 + nrm(ks[24], (N_B_LAYERS, B_V_DIM), 0.02),
        'b_w_o': nrm(ks[25], (N_B_LAYERS, B_HEADS * B_V_DIM, D_MODEL), (B_HEADS * B_V_DIM) ** -0.5),
        'ffn_w_up': nrm(ks[26], (DEPTH, D_MODEL, 2 * D_FF), D_MODEL ** -0.5),
        'ffn_conv_w': nrm(ks[27], (DEPTH, CONV_WIDTH, D_FF), CONV_WIDTH ** -0.5),
        'ffn_conv_b': nrm(ks[28], (DEPTH, D_FF), 0.02),
        'ffn_w_down': nrm(ks[29], (DEPTH, D_FF, D_MODEL), D_FF ** -0.5),
    }


def reference(x_prompt, x_sample, cache_a_k, cache_a_v, cache_b_k, cache_b_v, c, c_ctx,
              ada_w, ada_b, norm1_g, norm2_g,
              a_w_qkv, a_q_norm, a_k_norm, a_sink, a_w_o,
              b_w_qkv, b_q_norm, b_k_norm, b_lambda_q1, b_lambda_k1, b_lambda_q2, b_lambda_k2, b_subln, b_w_o,
              ffn_w_up, ffn_conv_w, ffn_conv_b, ffn_w_down):
    n_lat = x_sample.shape[1]
    cos_a, sin_a = _axial_angles(n_lat, A_HEAD_DIM)
    cos_b, sin_b = _axial_angles(n_lat, B_QK_DIM)
    xp = x_prompt
    xs = x_sample
    a_k_list, a_v_list, b_k_list, b_v_list = [], [], [], []

    for i in range(DEPTH):
        sh1_p, sc1_p, g1_p, sh2_p, sc2_p, g2_p = _modulation(c_ctx[None, :], ada_w[i], ada_b[i])
        sh1_s, sc1_s, g1_s, sh2_s, sc2_s, g2_s = _modulation(c, ada_w[i], ada_b[i])
        hp = _adaln(xp, norm1_g[i], sh1_p, sc1_p)
        hs = _adaln(xs, norm1_g[i], sh1_s, sc1_s)
        j = i // N_MIXERS
        if i % N_MIXERS == 0:
            qp, kp, vp = _a_project(hp, a_w_qkv[j], a_q_norm[j], a_k_norm[j])
            op = _a_context_attention(qp, kp, vp, a_sink[j]) @ a_w_o[j]
            qs, ks_, vs = _a_project(hs, a_w_qkv[j], a_q_norm[j], a_k_norm[j])
            qs = _axial_rope(qs, cos_a, sin_a)
            ks_ = _axial_rope(ks_, cos_a, sin_a)
            os_ = _a_latent_attention(qs, ks_, vs, cache_a_k[:, j], cache_a_v[:, j], a_sink[j]) @ a_w_o[j]
            a_k_list.append(kp)
            a_v_list.append(vp)
        else:
            lambda_init = 0.8 - 0.6 * math.exp(-0.3 * i)
            lam = (jnp.exp(jnp.sum(b_lambda_q1[j].astype(jnp.float32) * b_lambda_k1[j].astype(jnp.float32)))
                   - jnp.exp(jnp.sum(b_lambda_q2[j].astype(jnp.float32) * b_lambda_k2[j].astype(jnp.float32)))
                   + lambda_init)
            qp, kp, vp = _b_project(hp, b_w_qkv[j], b_q_norm[j], b_k_norm[j])
            op = _b_attention(qp, kp, vp, lam, b_subln[j], lambda_init) @ b_w_o[j]
            qs, ks_, vs = _b_project(hs, b_w_qkv[j], b_q_norm[j], b_k_norm[j])
            qs = _b_rope(qs, cos_b, sin_b)
            ks_ = _b_rope(ks_, cos_b, sin_b)
            k_all = jnp.concatenate([ks_, cache_b_k[:, j].astype(ks_.dtype)], axis=1)
            v_all = jnp.concatenate([vs, cache_b_v[:, j].astype(vs.dtype)], axis=1)
            os_ = _b_attention(qs, k_all, v_all, lam, b_subln[j], lambda_init) @ b_w_o[j]
            b_k_list.append(kp)
            b_v_list.append(vp)
        xp = xp + g1_p * op
        xs = xs + g1_s * os_
        hp = _adaln(xp, norm2_g[i], sh2_p, sc2_p)
        hs = _adaln(xs, norm2_g[i], sh2_s, sc2_s)
        xp = xp + g2_p * _conv_glu(hp, ffn_w_up[i], ffn_conv_w[i], ffn_conv_b[i], ffn_w_down[i])
        xs = xs + g2_s * _conv_glu(hs, ffn_w_up[i], ffn_conv_w[i], ffn_conv_b[i], ffn_w_down[i])

    state_a_k = jnp.stack(a_k_list, axis=1)
    state_a_v = jnp.stack(a_v_list, axis=1)
    state_b_k = jnp.stack(b_k_list, axis=1)
    state_b_v = jnp.stack(b_v_list, axis=1)
    return (xp, xs, state_a_k, state_a_v, state_b_k, state_b_v)
```

```python
import math
from contextlib import ExitStack
import numpy as np
import concourse.bass as bass
import concourse.mybir as mybir
from concourse.bass_utils import run_bass_kernel_spmd

F32 = mybir.dt.float32
BF16 = mybir.dt.bfloat16
AF = mybir.ActivationFunctionType
ALU = mybir.AluOpType
ENGS = ("pe", "act", "dve", "pool", "sp")
EPS = 1e-6

CFG_FULL = dict(D=2048, DEPTH=4, NS=4096, PAST=512, NPS=4, SEQ=256, AH=16, AKV=4, BH=8,
                DFF=5632, GRID_W=64)


class Buf:
    __slots__ = ("wc", "wd", "rc", "rd", "prc", "prd", "was_read", "sem", "lastdma")

    def __init__(self):
        self.prc = {}
        self.prd = []
        self.wc = {}
        self.wd = []
        self.rc = {}
        self.rd = []
        self.was_read = False
        self.sem = None
        self.lastdma = None


class DSem:
    __slots__ = ("h", "count")

    def __init__(self, h):
        self.h = h
        self.count = 0


class Op:
    __slots__ = ("eng", "fn", "deps", "is_dma", "sem", "dval", "signal", "count")

    def __init__(self, eng, fn, is_dma):
        self.eng = eng
        self.fn = fn
        self.deps = []
        self.is_dma = is_dma
        self.sem = None
        self.dval = 0
        self.signal = False
        self.count = 0


class Ctx:
    def __init__(self, nc, stack, ndsem=56):
        self.nc = nc
        self.esem = {e: stack.enter_context(nc.semaphore("es_" + e)) for e in ENGS}
        self.ecount = {e: 0 for e in ENGS}
        self.dsems = [DSem(stack.enter_context(nc.semaphore("ds%d" % i))) for i in range(ndsem)]
        self.sw = self.dsems[:14]
        self.nph = 0
        self.stop = None
        self.sub = None
        self.hw = self.dsems[14:]


class Phase:
    def __init__(self, ctx):
        self.ctx = ctx
        self.ops = {e: [] for e in ENGS}
        self.nsw = 0
        self.nhw = 0
        self.homesem = {}
        self.nrec = 0
        self.limit = ctx.sub if (ctx.stop is not None and ctx.nph + 1 == ctx.stop) else None

    def _rec(self, o, reads, writes):
        eng, is_dma = o.eng, o.is_dma
        deps = o.deps
        for r in reads:
            for e2, d in r.wc.items():
                if not (eng == "pe" and e2 == "pe" and not is_dma):
                    deps.append(d)
            deps.extend(r.wd)
        for w in writes:
            if w.was_read:
                w.prc = w.rc
                w.prd = w.rd
                w.wc = {}
                w.wd = []
                w.rc = {}
                w.rd = []
                w.was_read = False
            for e2, d in w.prc.items():
                if is_dma or e2 != eng:
                    deps.append(d)
            deps.extend(w.prd)
        for r in reads:
            r.was_read = True
            if is_dma:
                r.rd.append(o)
            else:
                r.rc[eng] = o
        for w in writes:
            if is_dma:
                w.wd.append(o)
            else:
                w.wc[eng] = o
        self.ops[eng].append(o)
        return o

    def op(self, eng, fn, reads=(), writes=()):
        self.nrec += 1
        if self.limit is not None and self.nrec > self.limit:
            return None
        return self._rec(Op(eng, fn, False), reads, writes)

    def dma(self, queue, fn, reads, writes, home):
        self.nrec += 1
        if self.limit is not None and self.nrec > self.limit:
            return None
        o = Op(queue, fn, True)
        key = (id(home), queue == "pool")
        if key not in self.homesem:
            if queue == "pool":
                assert self.nsw < len(self.ctx.sw)
                self.homesem[key] = self.ctx.sw[self.nsw]
                self.nsw += 1
            else:
                assert self.nhw < len(self.ctx.hw)
                self.homesem[key] = self.ctx.hw[self.nhw]
                self.nhw += 1
            home.lastdma = None
        home.sem = self.homesem[key]
        if home.lastdma is not None:
            o.deps.append(home.lastdma)
        home.lastdma = o
        o.sem = home.sem
        o.sem.count += 16
        o.dval = o.sem.count
        return self._rec(o, reads, writes)

    def emit(self):
        ctx = self.ctx
        nc = ctx.nc
        ctx.nph += 1
        if ctx.stop is not None and ctx.nph > ctx.stop:
            for e in ENGS:
                for o in self.ops[e]:
                    if o.is_dma:
                        o.sem.count -= 16
            return
        start_counts = dict(ctx.ecount)
        start_d = [(s, s.count) for s in ctx.dsems]
        for e in ENGS:
            lst = self.ops[e]
            for o in lst:
                for d in o.deps:
                    if not d.is_dma:
                        d.signal = True
            for o in reversed(lst):
                if not o.is_dma:
                    o.signal = True
                    break
        for e in ENGS:
            c = ctx.ecount[e]
            for o in self.ops[e]:
                if o.signal and not o.is_dma:
                    c += 1
                    o.count = c
            ctx.ecount[e] = c
        pre_d = {}
        for e in ENGS:
            for o in self.ops[e]:
                if o.is_dma and id(o.sem) not in pre_d:
                    pre_d[id(o.sem)] = o.dval - 16
        esem = ctx.esem

        def run(en, e):
            known = {}
            for e2 in ENGS:
                if start_counts[e2] > 0:
                    e.wait_ge(esem[e2], start_counts[e2])
                    known[id(esem[e2])] = start_counts[e2]
            for s, cnt in start_d:
                v = pre_d.get(id(s), cnt)
                if v > 0:
                    e.wait_ge(s.h, v)
                    known[id(s.h)] = v
            for o in self.ops[en]:
                for d in o.deps:
                    if d.is_dma:
                        sem, val = d.sem.h, d.dval
                    else:
                        sem, val = esem[d.eng], d.count
                    k = id(sem)
                    if known.get(k, 0) < val:
                        e.wait_ge(sem, val)
                        known[k] = val
                ins = o.fn(e)
                if o.is_dma:
                    ins.then_inc(o.sem.h, 16)
                elif o.signal:
                    ins.then_inc(esem[en], 1)

        with nc.Block() as block:
            @block.sync
            def _(e):
                run("sp", e)

            @block.tensor
            def _(e):
                run("pe", e)

            @block.scalar
            def _(e):
                run("act", e)

            @block.vector
            def _(e):
                run("dve", e)

            @block.gpsimd
            def _(e):
                run("pool", e)


def final_wait(ctx):
    nc = ctx.nc
    with nc.Block() as block:
        @block.sync
        def _(e):
            for e2 in ENGS:
                if ctx.ecount[e2] > 0:
                    e.wait_ge(ctx.esem[e2], ctx.ecount[e2])
            for s in ctx.dsems:
                if s.count > 0:
                    e.wait_ge(s.h, s.count)


class T:
    __slots__ = ("t", "b")

    def __init__(self, t):
        self.t = t
        self.b = Buf()


def build(cfg):
    D, DEPTH, NS, PAST = cfg["D"], cfg["DEPTH"], cfg["NS"], cfg["PAST"]
    NPS, SEQ, AH, AKV, BH, DFF = cfg["NPS"], cfg["SEQ"], cfg["AH"], cfg["AKV"], cfg["BH"], cfg["DFF"]
    KC = D // 128
    FC = DFF // 128
    NP = NPS * SEQ
    NTOK = NS + NP
    NKV = NS + PAST + NP
    NCH = NKV // 128
    NA, NB = (DEPTH + 1) // 2, DEPTH // 2
    QA = (AH + 2 * AKV) * 128
    AG = AH // AKV
    assert SEQ == 256 and AG == 4 and NS % 512 == 0 and NP % 512 == 0 and D % 512 == 0 and DFF % 512 == 0
    QCH = max(AH, 2 * BH)
    VDA, VDB = AKV * 128, BH * 256
    VD = max(VDA, VDB)

    nc = bass.Bass("TRN2", target_bir_lowering=False)

    def din(name, shape):
        return nc.dram_tensor(name, list(shape), F32, kind="ExternalInput").ap()

    def dout(name, shape):
        return nc.dram_tensor(name, list(shape), F32, kind="ExternalOutput").ap()

    xs = din("xs", [NS, D]); xp = din("xp", [NP, D])
    cak = din("cak", [NA * PAST, VDA]); cav = din("cav", [NA * PAST, VDA])
    cbk = din("cbk", [max(NB, 1) * PAST, VDB]); cbv = din("cbv", [max(NB, 1) * PAST, VDB])
    cond = din("cond", [2, D])
    ada_w = din("ada_w", [DEPTH * D, 6 * D]); ada_b = din("ada_b", [DEPTH, 6 * D])
    n1g = din("norm1_g", [DEPTH, D]); n2g = din("norm2_g", [DEPTH, D])
    a_wqkv = din("a_w_qkv", [NA * D, QA]); a_qn = din("a_q_norm", [NA, 128]); a_kn = din("a_k_norm", [NA, 128])
    a_sink = din("a_sink", [NA, AH]); a_wo = din("a_w_o", [NA * D, D])
    b_wqkv = din("b_w_qkv", [max(NB, 1) * D, 3 * D]); b_qn = din("b_q_norm", [max(NB, 1), 128])
    b_kn = din("b_k_norm", [max(NB, 1), 128])
    b_l = [din("b_lambda_" + n, [max(NB, 1), 128]) for n in ("q1", "k1", "q2", "k2")]
    b_sub = din("b_subln", [max(NB, 1), 256]); b_wo = din("b_w_o", [max(NB, 1) * D, D])
    w_up = din("ffn_w_up", [DEPTH * D, 2 * DFF]); cw = din("ffn_conv_w", [DEPTH * 3, DFF])
    cbias = din("ffn_conv_b", [DEPTH, DFF]); w_dn = din("ffn_w_down", [DEPTH * DFF, D])
    ropec = din("rope_cos", [128, NS]); ropes = din("rope_sin", [128, NS])
    consts = din("consts", [128, 640])

    ys = dout("ys", [NS, D]); yp = dout("yp", [NP, D])
    sak = dout("sak", [NPS * NA * SEQ, VDA]); sav = dout("sav", [NPS * NA * SEQ, VDA])
    sbk = dout("sbk", [NPS * max(NB, 1) * SEQ, VDB]); sbv = dout("sbv", [NPS * max(NB, 1) * SEQ, VDB])

    XT = nc.dram_tensor("XT", [KC, 128, NTOK], F32).ap()
    X1T = nc.dram_tensor("X1T", [KC, 128, NTOK], F32).ap()
    QT = nc.dram_tensor("QT", [QCH, 128, NTOK], BF16).ap()
    KT = nc.dram_tensor("KT", [QCH, 128, NKV], BF16).ap()
    VS = nc.dram_tensor("VS", [NKV, VD], BF16).ap()
    OT = nc.dram_tensor("OT", [KC, 128, NTOK], BF16).ap()
    XTv = XT.rearrange("k p t -> p k t"); X1Tv = X1T.rearrange("k p t -> p k t")
    QTv = QT.rearrange("k p t -> p k t"); KTv = KT.rearrange("k p t -> p k t")
    OTv = OT.rearrange("k p t -> p k t")

    groups = [(False, 512 * g, 512 * g, g) for g in range(NS // 512)]
    groups += [(True, NS + 512 * g, NS + PAST + 512 * g, g) for g in range(NP // 512)]
    dbuf = {}

    def DB(name, key):
        k = (name, key)
        if k not in dbuf:
            dbuf[k] = Buf()
        return dbuf[k]

    with ExitStack() as gst:
        ctx = Ctx(nc, gst)
        ctx.stop = cfg.get("STOP")
        ctx.sub = cfg.get("SUB")

        nmc = [0]

        def sb(st, name, shape, dt):
            nmc[0] += 1
            return T(st.enter_context(nc.sbuf_tensor("%s_%d" % (name, nmc[0]), list(shape), dt)))

        cst = sb(gst, "cst", [128, 640], BF16)
        identb = cst.t[:, 0:128]; rotm = cst.t[:, 128:256]
        trip = cst.t[:, 256:384]; trin = cst.t[:, 384:512]; onesb = cst.t[:, 512:640]
        identf = sb(gst, "identf", [128, 128], F32)
        mod = sb(gst, "mod", [128, DEPTH, 2, 6 * KC], F32)
        Am = sb(gst, "Am", [128, DEPTH, 2, 2, KC], F32)
        psA = T(gst.enter_context(nc.psum_tensor("psA", [128, 2, 512], F32)))
        psB = T(gst.enter_context(nc.psum_tensor("psB", [128, 2, 512], F32)))
        psC = T(gst.enter_context(nc.psum_tensor("psC", [128, 512], F32)))
        psD = T(gst.enter_context(nc.psum_tensor("psD", [128, 512], F32)))
        psE = T(gst.enter_context(nc.psum_tensor("psE", [128, 512], F32)))
        psT = T(gst.enter_context(nc.psum_tensor("psT", [128, 1024], BF16)))
        bA = [Buf(), Buf()]; bB = [Buf(), Buf()]

        def newbufs():
            for t in (psA, psB, psC, psD, psE, psT, cst, identf, mod, Am):
                pass

        with ExitStack() as st:
            ph = Phase(ctx)
            ph.dma("pool", lambda e: e.dma_start(out=cst.t[:], in_=consts), [], [cst.b], cst.b)
            ph.dma("sp", lambda e: e.dma_start(out=identf.t[:], in_=consts[:, 0:128]), [], [identf.b], identf.b)
            cT_ = sb(st, "condT", [128, KC, 2], F32)
            scT = sb(st, "scT", [128, KC, 2], BF16)
            for c_ in range(2):
                ph.dma("sp", lambda e, c_=c_: e.dma_start(out=cT_.t[:, :, c_], in_=cond[c_, :].rearrange("(k p) -> p k", p=128),
                                                          allow_slow_non_contiguous=True), [], [cT_.b], cT_.b)
            ph.op("act", lambda e: e.activation(out=scT.t[:], in_=cT_.t[:], func=AF.Silu), [cT_.b], [scT.b])
            wsl = [sb(st, "mw%d" % i, [128, KC, 512], BF16) for i in range(4)]
            abT = sb(st, "abT", [128, 6 * KC], F32)
            ngT = sb(st, "ngT", [128, 2, KC], F32)
            tmpm = sb(st, "tmpm", [128, 2, KC], F32)
            wi = 0
            for L in range(DEPTH):
                for j6 in range(6):
                    ph.dma("sp", lambda e, L=L, j6=j6: e.dma_start(
                        out=abT.t[:, j6 * KC:(j6 + 1) * KC], in_=ada_b[L, j6 * D:(j6 + 1) * D].rearrange("(j p) -> p j", p=128),
                        allow_slow_non_contiguous=True), [], [abT.b], abT.b)
                ph.dma("sp", lambda e, L=L: e.dma_start(out=ngT.t[:, 0, :], in_=n1g[L, :].rearrange("(j p) -> p j", p=128),
                                                        allow_slow_non_contiguous=True), [], [ngT.b], ngT.b)
                ph.dma("sp", lambda e, L=L: e.dma_start(out=ngT.t[:, 1, :], in_=n2g[L, :].rearrange("(j p) -> p j", p=128),
                                                        allow_slow_non_contiguous=True), [], [ngT.b], ngT.b)
                Wl = ada_w[L * D:(L + 1) * D, :].rearrange("(k p) n -> p k n", p=128)
                for cb_ in range(6 * D // 512):
                    w = wsl[wi % 4]; wi += 1
                    ph.dma("pool", lambda e, w=w, cb_=cb_, Wl=Wl: e.dma_start(out=w.t[:], in_=Wl[:, :, cb_ * 512:(cb_ + 1) * 512]),
                           [], [w.b], w.b)
                    for m in range(4):
                        col = cb_ * 4 + m
                        for k in range(KC):
                            ph.op("pe", lambda e, w=w, m=m, k=k, col=col: e.matmul(
                                psC.t[:, col * 2:col * 2 + 2], lhsT=w.t[:, k, m * 128:(m + 1) * 128], rhs=scT.t[:, k, :],
                                start=(k == 0), stop=(k == KC - 1)), [w.b, scT.b], [psC.b])
                for c in range(2):
                    ph.op("dve", lambda e, L=L, c=c: e.tensor_tensor(
                        out=mod.t[:, L, c, :], in0=psC.t[:, 0:12 * KC].rearrange("p (j c) -> p j c", c=2)[:, :, c],
                        in1=abT.t[:], op=ALU.add), [psC.b, abT.b], [mod.b])
                    for n in range(2):
                        ph.op("dve", lambda e, L=L, c=c, n=n: e.tensor_scalar(
                            out=tmpm.t[:, n, :], in0=mod.t[:, L, c, (3 * n + 1) * KC:(3 * n + 2) * KC], scalar1=1.0, scalar2=None,
                            op0=ALU.add), [mod.b], [tmpm.b])
                        ph.op("dve", lambda e, L=L, c=c, n=n: e.tensor_tensor(
                            out=Am.t[:, L, c, n, :], in0=tmpm.t[:, n, :], in1=ngT.t[:, n, :], op=ALU.mult),
                            [tmpm.b, ngT.b], [Am.b])
            ph.emit()

        with ExitStack() as st:
            ph = Phase(ctx)
            xin = [sb(st, "xin%d" % i, [128, D], F32) for i in range(2)]
            xst = [sb(st, "xst%d" % i, [128, KC, 512], F32) for i in range(2)]
            pss = [(psA.t[:, 0, :], bA[0]), (psA.t[:, 1, :], bA[1]), (psB.t[:, 0, :], bB[0]), (psB.t[:, 1, :], bB[1]),
                   (psC.t[:], psC.b), (psD.t[:], psD.b), (psE.t[:], psE.b)]
            pi = 0; ti = 0
            for gi, (isp, t0, kv0, g) in enumerate(groups):
                src = xp if isp else xs
                r0 = t0 - NS if isp else t0
                xo = xst[gi % 2]
                for tt in range(4):
                    xi = xin[ti % 2]; ti += 1
                    ph.dma("sp", lambda e, xi=xi, src=src, r=r0 + tt * 128: e.dma_start(out=xi.t[:], in_=src[r:r + 128, :]),
                           [], [xi.b], xi.b)
                    for k4 in range(KC // 4):
                        pt, pb = pss[pi % 7]; pi += 1
                        for q in range(4):
                            k = k4 * 4 + q
                            ph.op("pe", lambda e, pt=pt, xi=xi, k=k, q=q: e.transpose(
                                pt[:, q * 128:(q + 1) * 128], in_=xi.t[:, k * 128:(k + 1) * 128], identity=identf.t[:]),
                                [xi.b, identf.b], [pb])
                        eng = "act" if (pi % 2) else "dve"
                        if eng == "act":
                            ph.op("act", lambda e, pt=pt, xo=xo, k4=k4, tt=tt: e.activation(
                                out=xo.t[:, k4 * 4:k4 * 4 + 4, tt * 128:(tt + 1) * 128],
                                in_=pt.rearrange("p (q t) -> p q t", q=4), func=AF.Copy), [pb], [xo.b])
                        else:
                            ph.op("dve", lambda e, pt=pt, xo=xo, k4=k4, tt=tt: e.tensor_copy(
                                out=xo.t[:, k4 * 4:k4 * 4 + 4, tt * 128:(tt + 1) * 128],
                                in_=pt.rearrange("p (q t) -> p q t", q=4)), [pb], [xo.b])
                ph.dma("sp", lambda e, xo=xo, t0=t0: e.dma_start(out=XTv[:, :, t0:t0 + 512], in_=xo.t[:]),
                       [xo.b], [DB("XT", gi)], xo.b)
            ph.emit()

        def norm_mod(ph, x, W, segs, L, c, n, sq, hT, rstd, tmpf, psn):
            ph.op("act", lambda e: e.activation(out=sq.t[:, :, 0:W], in_=x.t[:, :, 0:W], func=AF.Square), [x.b], [sq.b])
            for (c0, n_) in segs:
                for k in range(KC):
                    ph.op("pe", lambda e, k=k, c0=c0, n_=n_: e.matmul(
                        psn.t[:, 0:n_], lhsT=onesb, rhs=sq.t[:, k, c0:c0 + n_], start=(k == 0), stop=(k == KC - 1)),
                        [sq.b, cst.b], [psn.b])
                ph.op("act", lambda e, c0=c0, n_=n_: e.activation(
                    out=tmpf.t[:, c0:c0 + n_], in_=psn.t[:, 0:n_], func=AF.Sqrt, scale=1.0 / D, bias=EPS), [psn.b], [tmpf.b])
            ph.op("dve", lambda e: e.reciprocal(out=rstd.t[:, 0:W], in_=tmpf.t[:, 0:W]), [tmpf.b], [rstd.b])
            for k in range(KC):
                ph.op("dve", lambda e, k=k: e.scalar_tensor_tensor(
                    out=hT.t[:, k, 0:W], in0=x.t[:, k, 0:W], scalar=Am.t[:, L, c, n, k:k + 1], in1=rstd.t[:, 0:W],
                    op0=ALU.mult, op1=ALU.mult), [x.b, Am.b, rstd.b], [hT.b])
            for k in range(KC):
                ph.op("act", lambda e, k=k: e.activation(
                    out=hT.t[:, k, 0:W], in_=hT.t[:, k, 0:W], func=AF.Identity,
                    bias=mod.t[:, L, c, 3 * n * KC + k:3 * n * KC + k + 1], scale=1.0), [hT.b, mod.b], [hT.b])

        for L in range(DEPTH):
            isA = (L % 2 == 0)
            j = L // 2
            lam_init = 0.8 - 0.6 * math.exp(-0.3 * L)
            if isA:
                Wqkv = a_wqkv[j * D:(j + 1) * D, :].rearrange("(k p) n -> p k n", p=128)
                Wo = a_wo[j * D:(j + 1) * D, :].rearrange("(k p) n -> p k n", p=128)
                nq, nk, nv = AH, AKV, AKV
                qn_ap, kn_ap = a_qn, a_kn
                ck, cv = cak[j * PAST:(j + 1) * PAST, :], cav[j * PAST:(j + 1) * PAST, :]
                sk, sv = sak, sav
                VDl = VDA
                NLs = NA
            else:
                Wqkv = b_wqkv[j * D:(j + 1) * D, :].rearrange("(k p) n -> p k n", p=128)
                Wo = b_wo[j * D:(j + 1) * D, :].rearrange("(k p) n -> p k n", p=128)
                nq, nk, nv = 2 * BH, 2 * BH, 2 * BH
                qn_ap, kn_ap = b_qn, b_kn
                ck, cv = cbk[j * PAST:(j + 1) * PAST, :], cbv[j * PAST:(j + 1) * PAST, :]
                sk, sv = sbk, sbv
                VDl = VDB
                NLs = NB
            nchunks = nq + nk + nv
            Wup = w_up[L * D:(L + 1) * D, :].rearrange("(k p) n -> p k n", p=128)
            Wdn = w_dn[L * DFF:(L + 1) * DFF, :].rearrange("(m p) f -> p m f", p=128)

            with ExitStack() as st:
                ph = Phase(ctx)
                gqk = sb(st, "gqk", [128, 2], F32)
                ph.dma("sp", lambda e: e.dma_start(out=gqk.t[:, 0:1], in_=qn_ap[j, :].rearrange("(p o) -> p o", o=1)), [], [gqk.b], gqk.b)
                ph.dma("sp", lambda e: e.dma_start(out=gqk.t[:, 1:2], in_=kn_ap[j, :].rearrange("(p o) -> p o", o=1)), [], [gqk.b], gqk.b)
                gs = sb(st, "gs", [128, 2], F32)
                ph.op("act", lambda e: e.mul(out=gs.t[:, 0:1], in_=gqk.t[:, 0:1], mul=128.0 ** -0.5), [gqk.b], [gs.b])
                ph.op("act", lambda e: e.copy(out=gs.t[:, 1:2], in_=gqk.t[:, 1:2]), [gqk.b], [gs.b])
                ph.dma("pool", lambda e: e.dma_start(out=VS[NS:NS + PAST, 0:VDl], in_=cv), [], [DB("VS", "ctx")], DB("VS", "ctx"))
                ckin = [sb(st, "ckin%d" % i, [128, VDl], F32) for i in range(2)]
                kst_c = [sb(st, "kstc%d" % i, [128, 4, 128], BF16) for i in range(2)]
                kci = 0
                for cch in range(PAST // 128):
                    ci_ = ckin[cch % 2]
                    ph.dma("sp", lambda e, ci_=ci_, cch=cch: e.dma_start(out=ci_.t[:], in_=ck[cch * 128:(cch + 1) * 128, :]),
                           [], [ci_.b], ci_.b)
                    for h4 in range((nk + 3) // 4):
                        nh = min(4, nk - h4 * 4)
                        for q in range(nh):
                            h = h4 * 4 + q
                            ph.op("pe", lambda e, ci_=ci_, h=h, q=q: e.transpose(
                                psD.t[:, q * 128:(q + 1) * 128], in_=ci_.t[:, h * 128:(h + 1) * 128], identity=identf.t[:]),
                                [ci_.b, identf.b], [psD.b])
                        ks_ = kst_c[kci % 2]; kci += 1
                        ph.op("dve", lambda e, ks_=ks_, nh=nh: e.tensor_copy(
                            out=ks_.t[:, 0:nh, :], in_=psD.t[:, 0:nh * 128].rearrange("p (q t) -> p q t", q=nh)), [psD.b], [ks_.b])
                        ph.dma("sp", lambda e, ks_=ks_, h4=h4, cch=cch, nh=nh: e.dma_start(
                            out=KTv[:, h4 * 4:h4 * 4 + nh, NS + cch * 128:NS + (cch + 1) * 128], in_=ks_.t[:, 0:nh, :]),
                            [ks_.b], [DB("KT", "ctx")], ks_.b)

                xt = sb(st, "xt", [128, KC, 512], F32)
                sq = sb(st, "sq", [128, KC, 512], BF16)
                hT = sb(st, "hT", [128, KC, 512], BF16)
                rstdn = sb(st, "rstdn", [128, 512], F32); tmpn = sb(st, "tmpn", [128, 512], F32)
                wsl = [sb(st, "w%d" % i, [128, KC, 512], BF16) for i in range(4)]
                cosT = sb(st, "cosT", [128, 512], F32); sinT = sb(st, "sinT", [128, 512], F32)
                sqc = [sb(st, "sqc%d" % i, [128, 512], BF16) for i in range(2)]
                tq = [sb(st, "tq%d" % i, [128, 512], F32) for i in range(2)]
                rq = [sb(st, "rq%d" % i, [128, 512], F32) for i in range(2)]
                qg = [sb(st, "qg%d" % i, [128, 512], BF16) for i in range(2)]
                t1 = [sb(st, "t1%d" % i, [128, 512], F32) for i in range(2)]
                t2 = [sb(st, "t2%d" % i, [128, 512], F32) for i in range(2)]
                cTs = [sb(st, "cT%d" % i, [128, 512], BF16) for i in range(3)]
                vtok = [sb(st, "vtok%d" % i, [128, 4, 128], BF16) for i in range(2)]
                stg = [sb(st, "stg%d" % i, [128, 4, 128], F32) for i in range(2)]
                wi = 0; cn = 0; vi = 0; si = 0
                for gi, (isp, t0, kv0, g) in enumerate(groups):
                    c = 1 if isp else 0
                    ph.dma("sp", lambda e, t0=t0: e.dma_start(out=xt.t[:], in_=XTv[:, :, t0:t0 + 512]),
                           [DB("XT", gi)], [xt.b], xt.b)
                    norm_mod(ph, xt, 512, [(0, 512)], L, c, 0, sq, hT, rstdn, tmpn, psC)
                    if not isp:
                        ph.dma("sp", lambda e, t0=t0: e.dma_start(out=cosT.t[:], in_=ropec[:, t0:t0 + 512]), [], [cosT.b], cosT.b)
                        ph.dma("sp", lambda e, t0=t0: e.dma_start(out=sinT.t[:], in_=ropes[:, t0:t0 + 512]), [], [sinT.b], sinT.b)
                    w = None
                    for m in range(nchunks):
                        if m % 4 == 0:
                            w = wsl[wi % 4]; wi += 1
                            ncol = min(512, (nchunks - m) * 128)
                            ph.dma("pool", lambda e, w=w, m=m, ncol=ncol: e.dma_start(out=w.t[:, :, 0:ncol], in_=Wqkv[:, :, m * 128:m * 128 + ncol]),
                                   [], [w.b], w.b)
                        typ = "q" if m < nq else ("k" if m < nq + nk else "v")
                        hidx = m if typ == "q" else (m - nq if typ == "k" else m - nq - nk)
                        pm, pmb = (psA.t[:, cn % 2, :], bA[cn % 2])
                        x2 = cn % 2; cn += 1
                        for k in range(KC):
                            ph.op("pe", lambda e, w=w, m=m, k=k, pm=pm: e.matmul(
                                pm, lhsT=w.t[:, k, (m % 4) * 128:(m % 4 + 1) * 128], rhs=hT.t[:, k, :],
                                start=(k == 0), stop=(k == KC - 1)), [w.b, hT.b], [pmb])
                        cT = cTs[cn % 3]
                        if typ == "v":
                            ph.op("act", lambda e, cT=cT, pm=pm: e.activation(out=cT.t[:], in_=pm, func=AF.Copy), [pmb], [cT.b])
                        else:
                            gcol = 0 if typ == "q" else 1
                            ph.op("act", lambda e, x2=x2, pm=pm: e.activation(out=sqc[x2].t[:], in_=pm, func=AF.Square), [pmb], [sqc[x2].b])
                            ph.op("pe", lambda e, x2=x2: e.matmul(psB.t[:, 0, :], lhsT=onesb, rhs=sqc[x2].t[:], start=True, stop=True),
                                  [sqc[x2].b, cst.b], [bB[0]])
                            ph.op("act", lambda e, x2=x2: e.activation(out=tq[x2].t[:], in_=psB.t[:, 0, :], func=AF.Sqrt, scale=1.0 / 128, bias=EPS),
                                  [bB[0]], [tq[x2].b])
                            ph.op("dve", lambda e, x2=x2: e.reciprocal(out=rq[x2].t[:], in_=tq[x2].t[:]), [tq[x2].b], [rq[x2].b])
                            if isp:
                                ph.op("dve", lambda e, x2=x2, cT=cT, pm=pm, gcol=gcol: e.scalar_tensor_tensor(
                                    out=cT.t[:], in0=pm, scalar=gs.t[:, gcol:gcol + 1], in1=rq[x2].t[:], op0=ALU.mult, op1=ALU.mult),
                                    [pmb, gs.b, rq[x2].b], [cT.b])
                            else:
                                ph.op("act", lambda e, x2=x2, pm=pm, gcol=gcol: e.activation(
                                    out=qg[x2].t[:], in_=pm, func=AF.Copy, scale=gs.t[:, gcol:gcol + 1]), [pmb, gs.b], [qg[x2].b])
                                ph.op("pe", lambda e, x2=x2: e.matmul(psB.t[:, 1, :], lhsT=rotm, rhs=qg[x2].t[:], start=True, stop=True),
                                      [qg[x2].b, cst.b], [bB[1]])
                                ph.op("pool", lambda e, x2=x2: e.tensor_tensor(out=t1[x2].t[:], in0=qg[x2].t[:], in1=cosT.t[:], op=ALU.mult),
                                      [qg[x2].b, cosT.b], [t1[x2].b])
                                ph.op("dve", lambda e, x2=x2: e.tensor_tensor(out=t2[x2].t[:], in0=psB.t[:, 1, :], in1=sinT.t[:], op=ALU.mult),
                                      [bB[1], sinT.b], [t2[x2].b])
                                ph.op("pool", lambda e, x2=x2: e.tensor_tensor(out=t1[x2].t[:], in0=t1[x2].t[:], in1=t2[x2].t[:], op=ALU.add),
                                      [t1[x2].b, t2[x2].b], [t1[x2].b])
                                ph.op("dve", lambda e, x2=x2, cT=cT: e.tensor_tensor(out=cT.t[:], in0=t1[x2].t[:], in1=rq[x2].t[:], op=ALU.mult),
                                      [t1[x2].b, rq[x2].b], [cT.b])
                        if typ == "q":
                            ph.dma("sp", lambda e, cT=cT, hidx=hidx, t0=t0: e.dma_start(out=QT[hidx, :, t0:t0 + 512], in_=cT.t[:]),
                                   [cT.b], [DB("QT", gi)], cT.b)
                        elif typ == "k":
                            ph.dma("sp", lambda e, cT=cT, hidx=hidx, kv0=kv0: e.dma_start(out=KT[hidx, :, kv0:kv0 + 512], in_=cT.t[:]),
                                   [cT.b], [DB("KT", gi)], cT.b)
                        if typ == "v" or (typ == "k" and isp):
                            for tt in range(4):
                                ph.op("pe", lambda e, cT=cT, tt=tt: e.transpose(
                                    psT.t[:, tt * 128:(tt + 1) * 128], in_=cT.t[:, tt * 128:(tt + 1) * 128], identity=identb),
                                    [cT.b, cst.b], [psT.b])
                            if typ == "v":
                                vt = vtok[vi % 2]; vi += 1
                                ph.op("dve", lambda e, vt=vt: e.tensor_copy(out=vt.t[:], in_=psT.t[:, 0:512].rearrange("p (t f) -> p t f", t=4)),
                                      [psT.b], [vt.b])
                                ph.dma("sp", lambda e, vt=vt, hidx=hidx, kv0=kv0: e.dma_start(
                                    out=VS[kv0:kv0 + 512, hidx * 128:(hidx + 1) * 128].rearrange("(t p) f -> p t f", p=128), in_=vt.t[:]),
                                    [vt.b], [DB("VS", gi)], vt.b)
                            if isp:
                                sg_ = stg[si % 2]; si += 1
                                if typ == "v":
                                    ph.op("act", lambda e, sg_=sg_, vt=vt: e.activation(out=sg_.t[:], in_=vt.t[:], func=AF.Copy), [vt.b], [sg_.b])
                                else:
                                    ph.op("act", lambda e, sg_=sg_: e.activation(out=sg_.t[:], in_=psT.t[:, 0:512].rearrange("p (t f) -> p t f", t=4), func=AF.Copy),
                                          [psT.b], [sg_.b])
                                dst = sk if typ == "k" else sv
                                for s2 in range(2):
                                    sq_ = 2 * g + s2
                                    r0 = (sq_ * NLs + j) * SEQ
                                    ph.dma("sp", lambda e, sg_=sg_, dst=dst, r0=r0, hidx=hidx, s2=s2: e.dma_start(
                                        out=dst[r0:r0 + SEQ, hidx * 128:(hidx + 1) * 128].rearrange("(t p) f -> p t f", p=128),
                                        in_=sg_.t[:, 2 * s2:2 * s2 + 2, :]), [sg_.b], [], sg_.b)
                ph.emit()

            with ExitStack() as st:
                ph = Phase(ctx)
                nhg = AKV if isA else BH
                ncmap = 1 if isA else 2
                dva = 129 if isA else 257
                kt = [sb(st, "kt%d" % i, [128, ncmap, NKV], BF16) for i in range(2)]
                vt_ = [sb(st, "vt%d" % i, [128, NCH, dva], BF16) for i in range(2)]
                for v in vt_:
                    ph.op("pool", lambda e, v=v: e.memset(v.t[:, :, dva - 1:dva], 1.0), [], [v.b])
                qt = [sb(st, "qt%d" % i, [128, 4 if isA else 2, 512 if isA else 256], BF16) for i in range(2)]
                pT = [sb(st, "pT%d" % i, [128, 512], BF16) for i in range(3)]
                den = sb(st, "den", [128, 8], F32); rden = sb(st, "rden", [128, 8], F32)
                otok = [sb(st, "otok%d" % i, [128, 512], BF16) for i in range(2)]
                otmp = [sb(st, "otmp%d" % i, [128, 256], F32) for i in range(2)]
                junk = sb(st, "junk", [128, 256], F32)
                osb = [sb(st, "osb%d" % i, [128, 4, 512] if isA else [128, 2, 256], BF16) for i in range(2)]
                allkv = [DB("KT", gi) for gi in range(len(groups))] + [DB("KT", "ctx")]
                allv = [DB("VS", gi) for gi in range(len(groups))] + [DB("VS", "ctx")]
                if isA:
                    psS = [(psA.t[:, 0, :], bA[0]), (psA.t[:, 1, :], bA[1]), (psE.t[:], psE.b)]
                else:
                    psS = [(psA.t[:, 0, :], bA[0]), (psA.t[:, 1, :], bA[1]), (psB.t[:, 0, :], bB[0])]
                if isA:
                    esk = sb(st, "esk", [128, AH], F32)
                    ph.dma("sp", lambda e: e.dma_start(out=esk.t[:], in_=a_sink[j:j + 1, :].partition_broadcast(128)), [], [esk.b], esk.b)
                    ph.op("act", lambda e: e.activation(out=esk.t[:], in_=esk.t[:], func=AF.Exp), [esk.b], [esk.b])
                    accs = [(psB.t[:, 0, 0:129], bB[0]), (psB.t[:, 1, 0:129], bB[1]), (psC.t[:, 0:129], psC.b), (psD.t[:, 0:129], psD.b)]
                else:
                    lv = sb(st, "lv", [128, 4, 128], F32)
                    for q in range(4):
                        ph.dma("sp", lambda e, q=q: e.dma_start(out=lv.t[:, q, :], in_=b_l[q][j:j + 1, :].partition_broadcast(128)), [], [lv.b], lv.b)
                    lp = sb(st, "lp", [128, 2, 128], F32); ls = sb(st, "ls", [128, 2], F32); nlam = sb(st, "nlam", [128, 1], F32)
                    ph.op("dve", lambda e: e.tensor_tensor(out=lp.t[:, 0, :], in0=lv.t[:, 0, :], in1=lv.t[:, 1, :], op=ALU.mult), [lv.b], [lp.b])
                    ph.op("dve", lambda e: e.tensor_tensor(out=lp.t[:, 1, :], in0=lv.t[:, 2, :], in1=lv.t[:, 3, :], op=ALU.mult), [lv.b], [lp.b])
                    ph.op("dve", lambda e: e.reduce_sum(out=ls.t[:], in_=lp.t[:], axis=mybir.AxisListType.X), [lp.b], [ls.b])
                    ph.op("act", lambda e: e.activation(out=ls.t[:], in_=ls.t[:], func=AF.Exp), [ls.b], [ls.b])
                    ph.op("dve", lambda e: e.tensor_tensor(out=nlam.t[:], in0=ls.t[:, 1:2], in1=ls.t[:, 0:1], op=ALU.subtract), [ls.b], [nlam.b])
                    ph.op("dve", lambda e: e.tensor_scalar(out=nlam.t[:], in0=nlam.t[:], scalar1=-lam_init, scalar2=None, op0=ALU.add), [nlam.b], [nlam.b])
                    subT = sb(st, "subT", [128, 2], F32)
                    ph.dma("sp", lambda e: e.dma_start(out=subT.t[:], in_=b_sub[j, :].rearrange("(c p) -> p c", p=128), allow_slow_non_contiguous=True),
                           [], [subT.b], subT.b)
                    ph.op("act", lambda e: e.mul(out=subT.t[:], in_=subT.t[:], mul=1.0 - lam_init), [subT.b], [subT.b])
                    accs = [(psB.t[:, 1, 0:257], bB[1]), (psC.t[:, 0:257], psC.b), (psD.t[:, 0:257], psD.b), (psE.t[:, 0:257], psE.b)]
                si = 0; pi = 0; oi = 0; qi = 0
                for hg in range(nhg):
                    ktile = kt[hg % 2]; vtile = vt_[hg % 2]
                    ph.dma("sp", lambda e, ktile=ktile, hg=hg: e.dma_start(out=ktile.t[:], in_=KTv[:, hg * ncmap:(hg + 1) * ncmap, :]),
                           allkv, [ktile.b], ktile.b)
                    dvw = dva - 1
                    ph.dma("sp", lambda e, vtile=vtile, hg=hg, dvw=dvw: e.dma_start(
                        out=vtile.t[:, :, 0:dvw], in_=VS[:, hg * dvw:(hg + 1) * dvw].rearrange("(c p) f -> p c f", p=128)),
                        allv, [vtile.b], vtile.b)
                    units = []
                    if isA:
                        for gi, (isp, t0, kv0, g) in enumerate(groups):
                            blocks = []
                            for blk in range(4):
                                if isp:
                                    cb0 = (kv0 + (blk // 2) * 256) // 128
                                    chunks = [(cb0, None), (cb0 + 1, None)]
                                else:
                                    jb = g * 4 + blk
                                    chunks = []
                                    if jb > 0:
                                        chunks.append((jb - 1, trip))
                                    chunks.append((jb, None))
                                    if jb < NS // 128 - 1:
                                        chunks.append((jb + 1, trin))
                                    chunks += [(NS // 128 + c_, None) for c_ in range(PAST // 128)]
                                blocks.append(chunks)
                            units.append((gi, t0, blocks))
                    else:
                        for gi, (isp, t0, kv0, g) in enumerate(groups):
                            for hf in range(2):
                                if isp:
                                    cb0 = (kv0 + hf * 256) // 128
                                    chunks = [(cb0, None), (cb0 + 1, None)]
                                else:
                                    chunks = [(c_, None) for c_ in range((NS + PAST) // 128)]
                                units.append((gi, t0 + hf * 256, [chunks]))
                    for (gi, t0, blocks) in units:
                        qtile = qt[qi % 2]; qi += 1
                        ob = osb[oi % 2]; oi += 1
                        if isA:
                            ph.dma("sp", lambda e, qtile=qtile, hg=hg, t0=t0: e.dma_start(out=qtile.t[:], in_=QTv[:, hg * 4:hg * 4 + 4, t0:t0 + 512]),
                                   [DB("QT", gi)], [qtile.b], qtile.b)
                        else:
                            ph.dma("sp", lambda e, qtile=qtile, hg=hg, t0=t0: e.dma_start(out=qtile.t[:], in_=QTv[:, hg * 2:hg * 2 + 2, t0:t0 + 256]),
                                   [DB("QT", gi)], [qtile.b], qtile.b)
                        for blk, chunks in enumerate(blocks):
                            nchk = len(chunks)
                            for cidx, (ci, msk) in enumerate(chunks):
                                ps_, psb_ = psS[si % 3]; si += 1
                                p_ = pT[pi % 3]; pi += 1
                                if isA:
                                    ph.op("pe", lambda e, ps_=ps_, ktile=ktile, ci=ci, qtile=qtile, blk=blk: e.matmul(
                                        ps_.rearrange("p (g t) -> p g t", g=4), lhsT=ktile.t[:, 0, ci * 128:(ci + 1) * 128],
                                        rhs=qtile.t[:, :, blk * 128:(blk + 1) * 128], start=True, stop=True),
                                        [ktile.b, qtile.b], [psb_])
                                else:
                                    for c_ in range(2):
                                        ph.op("pe", lambda e, ps_=ps_, ktile=ktile, ci=ci, qtile=qtile, c_=c_: e.matmul(
                                            ps_[:, c_ * 256:(c_ + 1) * 256], lhsT=ktile.t[:, c_, ci * 128:(ci + 1) * 128],
                                            rhs=qtile.t[:, c_, :], start=True, stop=True), [ktile.b, qtile.b], [psb_])
                                ph.op("act", lambda e, ps_=ps_, p_=p_: e.activation(out=p_.t[:], in_=ps_, func=AF.Exp), [psb_], [p_.b])
                                if msk is not None:
                                    ph.op("dve", lambda e, p_=p_, msk=msk: e.tensor_tensor(
                                        out=p_.t[:].rearrange("p (g t) -> p g t", g=4), in0=p_.t[:].rearrange("p (g t) -> p g t", g=4),
                                        in1=msk.unsqueeze(1).to_broadcast([128, 4, 128]), op=ALU.mult), [p_.b, cst.b], [p_.b])
                                for a_ in range(4):
                                    ac, acb = accs[a_]
                                    ph.op("pe", lambda e, ac=ac, p_=p_, a_=a_, vtile=vtile, ci=ci, cidx=cidx, nchk=nchk: e.matmul(
                                        ac, lhsT=p_.t[:, a_ * 128:(a_ + 1) * 128], rhs=vtile.t[:, ci, :],
                                        start=(cidx == 0), stop=(cidx == nchk - 1)), [p_.b, vtile.b], [acb])
                            if isA:
                                ot_ = otok[blk % 2]
                                for a_ in range(4):
                                    ac, acb = accs[a_]
                                    ph.op("dve", lambda e, ac=ac, a_=a_, hg=hg: e.tensor_tensor(
                                        out=den.t[:, a_:a_ + 1], in0=ac[:, 128:129],
                                        in1=esk.t[:, hg * 4 + a_:hg * 4 + a_ + 1], op=ALU.add), [acb, esk.b], [den.b])
                                ph.op("dve", lambda e: e.reciprocal(out=rden.t[:, 0:4], in_=den.t[:, 0:4]), [den.b], [rden.b])
                                for a_ in range(4):
                                    ac, acb = accs[a_]
                                    ph.op("dve", lambda e, ac=ac, a_=a_, ot_=ot_: e.tensor_scalar(
                                        out=ot_.t[:, a_ * 128:(a_ + 1) * 128], in0=ac[:, 0:128], scalar1=rden.t[:, a_:a_ + 1], scalar2=None,
                                        op0=ALU.mult), [acb, rden.b], [ot_.b])
                                for a_ in range(4):
                                    ph.op("pe", lambda e, a_=a_, ot_=ot_: e.transpose(
                                        psT.t[:, a_ * 128:(a_ + 1) * 128], in_=ot_.t[:, a_ * 128:(a_ + 1) * 128], identity=identb),
                                        [ot_.b, cst.b], [psT.b])
                                ph.op("act", lambda e, ob=ob, blk=blk: e.activation(
                                    out=ob.t[:, :, blk * 128:(blk + 1) * 128], in_=psT.t[:, 0:512].rearrange("p (g t) -> p g t", g=4), func=AF.Copy),
                                    [psT.b], [ob.b])
                            else:
                                for a_ in range(4):
                                    ac, acb = accs[a_]
                                    ph.op("dve", lambda e, ac=ac, a_=a_: e.tensor_copy(out=den.t[:, a_:a_ + 1], in_=ac[:, 256:257]), [acb], [den.b])
                                ph.op("dve", lambda e: e.reciprocal(out=rden.t[:, 0:4], in_=den.t[:, 0:4]), [den.b], [rden.b])
                                ph.op("dve", lambda e: e.tensor_scalar(out=rden.t[:, 2:4], in0=rden.t[:, 2:4], scalar1=nlam.t[:, 0:1], scalar2=None, op0=ALU.mult),
                                      [rden.b, nlam.b], [rden.b])
                                for qb in range(2):
                                    om = otmp[qb]
                                    a0, a0b = accs[qb]; a1, a1b = accs[2 + qb]
                                    ph.op("dve", lambda e, om=om, a0=a0, qb=qb: e.tensor_scalar(
                                        out=om.t[:], in0=a0[:, 0:256], scalar1=rden.t[:, qb:qb + 1], scalar2=None, op0=ALU.mult), [a0b, rden.b], [om.b])
                                    ph.op("dve", lambda e, om=om, a1=a1, qb=qb: e.scalar_tensor_tensor(
                                        out=om.t[:], in0=a1[:, 0:256], scalar=rden.t[:, 2 + qb:3 + qb], in1=om.t[:], op0=ALU.mult, op1=ALU.add),
                                        [a1b, rden.b, om.b], [om.b])
                                    ph.op("act", lambda e, om=om, qb=qb: e.activation(out=junk.t[:], in_=om.t[:], func=AF.Square, accum_out=den.t[:, 4 + qb:5 + qb]),
                                          [om.b], [junk.b, den.b])
                                    ph.op("act", lambda e, qb=qb: e.activation(out=den.t[:, 6 + qb:7 + qb], in_=den.t[:, 4 + qb:5 + qb], func=AF.Sqrt, scale=1.0 / 256, bias=EPS),
                                          [den.b], [den.b])
                                    ph.op("dve", lambda e, qb=qb: e.reciprocal(out=rden.t[:, 6 + qb:7 + qb], in_=den.t[:, 6 + qb:7 + qb]), [den.b], [rden.b])
                                    ot_ = otok[qb]
                                    ph.op("dve", lambda e, om=om, qb=qb, ot_=ot_: e.tensor_scalar(
                                        out=ot_.t[:, 0:256], in0=om.t[:], scalar1=rden.t[:, 6 + qb:7 + qb], scalar2=None, op0=ALU.mult), [om.b, rden.b], [ot_.b])
                                    for cc in range(2):
                                        ph.op("pe", lambda e, ot_=ot_, cc=cc, qb=qb: e.transpose(
                                            psT.t[:, (qb * 2 + cc) * 128:(qb * 2 + cc + 1) * 128], in_=ot_.t[:, cc * 128:(cc + 1) * 128], identity=identb),
                                            [ot_.b, cst.b], [psT.b])
                                    for cc in range(2):
                                        ph.op("act", lambda e, ob=ob, cc=cc, qb=qb: e.activation(
                                            out=ob.t[:, cc, qb * 128:(qb + 1) * 128], in_=psT.t[:, (qb * 2 + cc) * 128:(qb * 2 + cc + 1) * 128],
                                            func=AF.Copy, scale=subT.t[:, cc:cc + 1]), [psT.b, subT.b], [ob.b])
                        if isA:
                            ph.dma("sp", lambda e, ob=ob, hg=hg, t0=t0: e.dma_start(out=OTv[:, hg * 4:hg * 4 + 4, t0:t0 + 512], in_=ob.t[:]),
                                   [ob.b], [DB("OT", gi)], ob.b)
                        else:
                            ph.dma("sp", lambda e, ob=ob, hg=hg, t0=t0: e.dma_start(out=OTv[:, hg * 2:hg * 2 + 2, t0:t0 + 256], in_=ob.t[:]),
                                   [ob.b], [DB("OT", gi)], ob.b)
                ph.emit()

            with ExitStack() as st:
                ph = Phase(ctx)
                xt2 = [sb(st, "xt%d" % i, [128, KC, 512], F32) for i in range(2)]
                ot2 = [sb(st, "ot%d" % i, [128, KC, 512], BF16) for i in range(2)]
                wsl = [sb(st, "w%d" % i, [128, KC, 512], BF16) for i in range(4)]
                pss = [(psA.t[:, 0, :], bA[0]), (psA.t[:, 1, :], bA[1]), (psB.t[:, 0, :], bB[0]), (psB.t[:, 1, :], bB[1])]
                wi = 0; pi = 0
                for gi, (isp, t0, kv0, g) in enumerate(groups):
                    c = 1 if isp else 0
                    xt = xt2[gi % 2]; ot = ot2[gi % 2]
                    ph.dma("sp", lambda e, xt=xt, t0=t0: e.dma_start(out=xt.t[:], in_=XTv[:, :, t0:t0 + 512]), [DB("XT", gi)], [xt.b], xt.b)
                    ph.dma("sp", lambda e, ot=ot, t0=t0: e.dma_start(out=ot.t[:], in_=OTv[:, :, t0:t0 + 512]), [DB("OT", gi)], [ot.b], ot.b)
                    for pc in range(D // 512):
                        w = wsl[wi % 4]; wi += 1
                        ph.dma("pool", lambda e, w=w, pc=pc: e.dma_start(out=w.t[:], in_=Wo[:, :, pc * 512:(pc + 1) * 512]), [], [w.b], w.b)
                        for m in range(4):
                            f = pc * 4 + m
                            pm, pmb = pss[pi % 4]; pi += 1
                            for k in range(KC):
                                ph.op("pe", lambda e, w=w, m=m, k=k, pm=pm, ot=ot: e.matmul(
                                    pm, lhsT=w.t[:, k, m * 128:(m + 1) * 128], rhs=ot.t[:, k, :], start=(k == 0), stop=(k == KC - 1)),
                                    [w.b, ot.b], [pmb])
                            ph.op("dve", lambda e, pm=pm, xt=xt, f=f, c=c: e.scalar_tensor_tensor(
                                out=xt.t[:, f, :], in0=pm, scalar=mod.t[:, L, c, 2 * KC + f:2 * KC + f + 1], in1=xt.t[:, f, :],
                                op0=ALU.mult, op1=ALU.add), [pmb, mod.b, xt.b], [xt.b])
                    ph.dma("sp", lambda e, xt=xt, t0=t0: e.dma_start(out=X1Tv[:, :, t0:t0 + 512], in_=xt.t[:]), [xt.b], [DB("X1T", gi)], xt.b)
                ph.emit()

            with ExitStack() as st:
                ph = Phase(ctx)
                x1 = sb(st, "x1", [128, KC, 516], F32)
                sq = sb(st, "sq", [128, KC, 516], BF16)
                hT = sb(st, "hT", [128, KC, 516], BF16)
                rstdn = sb(st, "rstdn", [128, 516], F32); tmpn = sb(st, "tmpn", [128, 516], F32)
                aT = sb(st, "aT", [128, FC, 512], BF16)
                wsl = [sb(st, "w%d" % i, [128, KC, 512], BF16) for i in range(4)]
                cwT = sb(st, "cwT", [128, 4, FC], F32)
                for m0 in range(0, FC, 11):
                    m1 = min(FC, m0 + 11)
                    for q in range(3):
                        ph.dma("sp", lambda e, q=q, m0=m0, m1=m1: e.dma_start(
                            out=cwT.t[:, q, m0:m1], in_=cw[L * 3 + q, m0 * 128:m1 * 128].rearrange("(m p) -> p m", p=128),
                            allow_slow_non_contiguous=True), [], [cwT.b], cwT.b)
                    ph.dma("sp", lambda e, m0=m0, m1=m1: e.dma_start(
                        out=cwT.t[:, 3, m0:m1], in_=cbias[L, m0 * 128:m1 * 128].rearrange("(m p) -> p m", p=128),
                        allow_slow_non_contiguous=True), [], [cwT.b], cwT.b)
                tmpc = [sb(st, "tmpc%d" % i, [128, 2, 256], F32) for i in range(2)]
                sgl = [sb(st, "sgl%d" % i, [128, 2, 256], F32) for i in range(2)]
                x1v = x1.t[:].rearrange("p k (s t) -> p k s t", s=2)
                hTv = hT.t[:].rearrange("p k (s t) -> p k s t", s=2)
                aTv = aT.t[:].rearrange("p m (s t) -> p m s t", s=2)
                wi = 0; ci_ = 0
                kpieces = [(k0, min(k0 + 16, FC)) for k0 in range(0, FC, 16)]
                for gi, (isp, t0, kv0, g) in enumerate(groups):
                    c = 1 if isp else 0
                    for s in range(2):
                        ts_ = t0 + s * 256
                        if isp:
                            left = right = False
                        else:
                            left = ts_ > 0
                            right = ts_ + 256 < NS
                        lo = ts_ - (1 if left else 0)
                        n_ = 256 + (1 if left else 0) + (1 if right else 0)
                        off = s * 258 + (0 if left else 1)
                        if not left:
                            ph.op("pool", lambda e, s=s: e.memset(x1.t[:, :, s * 258:s * 258 + 1], 0.0), [], [x1.b])
                        if not right:
                            ph.op("pool", lambda e, s=s: e.memset(x1.t[:, :, s * 258 + 257:s * 258 + 258], 0.0), [], [x1.b])
                        rb = [DB("X1T", gi)]
                        if left and (ts_ % 512 == 0):
                            rb.append(DB("X1T", gi - 1))
                        if right and ((ts_ + 256) % 512 == 0):
                            rb.append(DB("X1T", gi + 1))
                        ph.dma("sp", lambda e, lo=lo, n_=n_, off=off: e.dma_start(out=x1.t[:, :, off:off + n_], in_=X1Tv[:, :, lo:lo + n_]),
                               rb, [x1.b], x1.b)
                    norm_mod(ph, x1, 516, [(0, 258), (258, 258)], L, c, 1, sq, hT, rstdn, tmpn, psC)
                    for s in range(2):
                        ts_ = t0 + s * 256
                        left = (not isp) and ts_ > 0
                        right = (not isp) and (ts_ + 256 < NS)
                        if not left:
                            ph.op("pool", lambda e, s=s: e.memset(hT.t[:, :, s * 258:s * 258 + 1], 0.0), [hT.b], [hT.b])
                        if not right:
                            ph.op("pool", lambda e, s=s: e.memset(hT.t[:, :, s * 258 + 257:s * 258 + 258], 0.0), [hT.b], [hT.b])
                    for pc in range(DFF // 512):
                        wg = wsl[wi % 4]; wv = wsl[(wi + 1) % 4]; wi += 2
                        ph.dma("pool", lambda e, wg=wg, pc=pc: e.dma_start(out=wg.t[:], in_=Wup[:, :, pc * 512:(pc + 1) * 512]), [], [wg.b], wg.b)
                        ph.dma("pool", lambda e, wv=wv, pc=pc: e.dma_start(out=wv.t[:], in_=Wup[:, :, DFF + pc * 512:DFF + (pc + 1) * 512]), [], [wv.b], wv.b)
                        for m in range(4):
                            fm = pc * 4 + m
                            for (w_, ps_, pbs) in ((wg, psA, bA), (wv, psB, bB)):
                                for s in range(2):
                                    for k in range(KC):
                                        ph.op("pe", lambda e, w_=w_, ps_=ps_, s=s, k=k, m=m: e.matmul(
                                            ps_.t[:, s, 0:258], lhsT=w_.t[:, k, m * 128:(m + 1) * 128], rhs=hTv[:, k, s, :],
                                            start=(k == 0), stop=(k == KC - 1)), [w_.b, hT.b], [pbs[s]])
                            tc_ = tmpc[ci_ % 2]; sg_ = sgl[ci_ % 2]; ci_ += 1
                            ph.op("act", lambda e, tc_=tc_, fm=fm: e.activation(
                                out=tc_.t[:], in_=psA.t[:, :, 1:257], func=AF.Identity, scale=cwT.t[:, 1, fm:fm + 1], bias=cwT.t[:, 3, fm:fm + 1]),
                                [bA[0], bA[1], cwT.b], [tc_.b])
                            ph.op("dve", lambda e, tc_=tc_, fm=fm: e.scalar_tensor_tensor(
                                out=tc_.t[:], in0=psA.t[:, :, 0:256], scalar=cwT.t[:, 0, fm:fm + 1], in1=tc_.t[:], op0=ALU.mult, op1=ALU.add),
                                [bA[0], bA[1], cwT.b, tc_.b], [tc_.b])
                            ph.op("dve", lambda e, tc_=tc_, fm=fm: e.scalar_tensor_tensor(
                                out=tc_.t[:], in0=psA.t[:, :, 2:258], scalar=cwT.t[:, 2, fm:fm + 1], in1=tc_.t[:], op0=ALU.mult, op1=ALU.add),
                                [bA[0], bA[1], cwT.b, tc_.b], [tc_.b])
                            ph.op("act", lambda e, tc_=tc_, sg_=sg_: e.activation(out=sg_.t[:], in_=tc_.t[:], func=AF.Silu), [tc_.b], [sg_.b])
                            ph.op("dve", lambda e, sg_=sg_, fm=fm: e.tensor_tensor(
                                out=aTv[:, fm, :, :], in0=sg_.t[:], in1=psB.t[:, :, 1:257], op=ALU.mult), [sg_.b, bB[0], bB[1]], [aT.b])
                    pso = [(psA.t[:, 0, :], bA[0]), (psA.t[:, 1, :], bA[1]), (psB.t[:, 0, :], bB[0]), (psB.t[:, 1, :], bB[1])]
                    for pb_ in range(D // 512):
                        for (k0, k1) in kpieces:
                            w = wsl[wi % 4]; wi += 1
                            ph.dma("pool", lambda e, w=w, k0=k0, k1=k1, pb_=pb_: e.dma_start(
                                out=w.t[:, 0:k1 - k0, :], in_=Wdn[:, k0:k1, pb_ * 512:(pb_ + 1) * 512]), [], [w.b], w.b)
                            for o_ in range(4):
                                po, pob = pso[o_]
                                for k in range(k0, k1):
                                    ph.op("pe", lambda e, w=w, po=po, k=k, k0=k0, o_=o_: e.matmul(
                                        po, lhsT=w.t[:, k - k0, o_ * 128:(o_ + 1) * 128], rhs=aT.t[:, k, :],
                                        start=(k == 0), stop=(k == FC - 1)), [w.b, aT.b], [pob])
                        for o_ in range(4):
                            f = pb_ * 4 + o_
                            po, pob = pso[o_]
                            ph.op("dve", lambda e, po=po, f=f, c=c: e.scalar_tensor_tensor(
                                out=x1v[:, f, :, 1:257], in0=po.rearrange("p (s t) -> p s t", s=2), scalar=mod.t[:, L, c, 5 * KC + f:5 * KC + f + 1],
                                in1=x1v[:, f, :, 1:257], op0=ALU.mult, op1=ALU.add), [pob, mod.b, x1.b], [x1.b])
                    for s in range(2):
                        ph.dma("sp", lambda e, s=s, t0=t0: e.dma_start(out=XTv[:, :, t0 + s * 256:t0 + (s + 1) * 256], in_=x1v[:, :, s, 1:257]),
                               [x1.b], [DB("XT", gi)], x1.b)
                ph.emit()

        with ExitStack() as st:
            ph = Phase(ctx)
            xin = [sb(st, "fx%d" % i, [128, KC, 512], F32) for i in range(2)]
            yst = [sb(st, "yst%d" % i, [128, D], F32) for i in range(2)]
            pss = [(psA.t[:, 0, :], bA[0]), (psA.t[:, 1, :], bA[1]), (psB.t[:, 0, :], bB[0]), (psB.t[:, 1, :], bB[1]),
                   (psC.t[:], psC.b), (psD.t[:], psD.b), (psE.t[:], psE.b)]
            pi = 0; yi = 0
            for gi, (isp, t0, kv0, g) in enumerate(groups):
                xi = xin[gi % 2]
                dst = yp if isp else ys
                r0 = t0 - NS if isp else t0
                ph.dma("sp", lambda e, xi=xi, t0=t0: e.dma_start(out=xi.t[:], in_=XTv[:, :, t0:t0 + 512]), [DB("XT", gi)], [xi.b], xi.b)
                for tt in range(4):
                    yo = yst[yi % 2]; yi += 1
                    for k4 in range(KC // 4):
                        pt, pb = pss[pi % 7]; pi += 1
                        for q in range(4):
                            k = k4 * 4 + q
                            ph.op("pe", lambda e, pt=pt, xi=xi, k=k, q=q, tt=tt: e.transpose(
                                pt[:, q * 128:(q + 1) * 128], in_=xi.t[:, k, tt * 128:(tt + 1) * 128], identity=identf.t[:]),
                                [xi.b, identf.b], [pb])
                        if pi % 2:
                            ph.op("act", lambda e, pt=pt, yo=yo, k4=k4: e.activation(out=yo.t[:, k4 * 512:(k4 + 1) * 512], in_=pt, func=AF.Copy), [pb], [yo.b])
                        else:
                            ph.op("dve", lambda e, pt=pt, yo=yo, k4=k4: e.tensor_copy(out=yo.t[:, k4 * 512:(k4 + 1) * 512], in_=pt), [pb], [yo.b])
                    ph.dma("sp", lambda e, yo=yo, dst=dst, r=r0 + tt * 128: e.dma_start(out=dst[r:r + 128, :], in_=yo.t[:]), [yo.b], [], yo.b)
            ph.emit()
        final_wait(ctx)
    return nc


def _rope_tables(NS, GRID_W):
    half = 64
    t = np.arange(NS)
    row = (t // GRID_W).astype(np.float32)
    col = (t % GRID_W).astype(np.float32)
    inv = (10000.0 ** (-np.arange(0, half, 2, dtype=np.float32) / half)).astype(np.float32)
    ang = np.concatenate([row[:, None] * inv, col[:, None] * inv], axis=-1).astype(np.float32)
    cos, sin = np.cos(ang), np.sin(ang)
    C = np.zeros((128, NS), np.float32); S = np.zeros((128, NS), np.float32)
    for d in range(128):
        hf = d // 64; r = d % 64
        fi = hf * 32 + (r % 32)
        C[d] = cos[:, fi]
        S[d] = -sin[:, fi] if r < 32 else sin[:, fi]
    return C, S


def _consts():
    c = np.zeros((128, 640), np.float32)
    c[:, 0:128] = np.eye(128, dtype=np.float32)
    for dp in range(128):
        r = dp % 64
        partner = dp + 32 if r < 32 else dp - 32
        c[partner, 128 + dp] = 1.0
    tk = np.arange(128)[:, None]; tq = np.arange(128)[None, :]
    c[:, 256:384] = (tq <= tk).astype(np.float32)
    c[:, 384:512] = (tk <= tq).astype(np.float32)
    c[:, 512:640] = 1.0
    return c


def make_in_maps(cfg, inputs, ncores):
    D, DEPTH, NS, PAST, NPS, SEQ = cfg["D"], cfg["DEPTH"], cfg["NS"], cfg["PAST"], cfg["NPS"], cfg["SEQ"]
    NA, NB = (DEPTH + 1) // 2, DEPTH // 2
    f = lambda a: np.ascontiguousarray(np.asarray(a, dtype=np.float32))
    C, S = _rope_tables(NS, cfg["GRID_W"])
    shared = {
        "ada_w": f(inputs["ada_w"]).reshape(DEPTH * D, 6 * D), "ada_b": f(inputs["ada_b"]),
        "norm1_g": f(inputs["norm1_g"]), "norm2_g": f(inputs["norm2_g"]),
        "a_w_qkv": f(inputs["a_w_qkv"]).reshape(NA * D, -1), "a_q_norm": f(inputs["a_q_norm"]), "a_k_norm": f(inputs["a_k_norm"]),
        "a_sink": f(inputs["a_sink"]), "a_w_o": f(inputs["a_w_o"]).reshape(NA * D, D),
        "b_w_qkv": f(inputs["b_w_qkv"]).reshape(NB * D, 3 * D), "b_q_norm": f(inputs["b_q_norm"]), "b_k_norm": f(inputs["b_k_norm"]),
        "b_lambda_q1": f(inputs["b_lambda_q1"]), "b_lambda_k1": f(inputs["b_lambda_k1"]),
        "b_lambda_q2": f(inputs["b_lambda_q2"]), "b_lambda_k2": f(inputs["b_lambda_k2"]),
        "b_subln": f(inputs["b_subln"]), "b_w_o": f(inputs["b_w_o"]).reshape(NB * D, D),
        "ffn_w_up": f(inputs["ffn_w_up"]).reshape(DEPTH * D, -1), "ffn_conv_w": f(inputs["ffn_conv_w"]).reshape(DEPTH * 3, -1),
        "ffn_conv_b": f(inputs["ffn_conv_b"]), "ffn_w_down": f(inputs["ffn_w_down"]).reshape(-1, D),
        "rope_cos": C, "rope_sin": S, "consts": _consts(),
    }
    xs = f(inputs["x_sample"]); xp = f(inputs["x_prompt"])
    cak = f(inputs["cache_a_k"]); cav = f(inputs["cache_a_v"]); cbk = f(inputs["cache_b_k"]); cbv = f(inputs["cache_b_v"])
    cc = f(inputs["c"]); cctx = f(inputs["c_ctx"])
    maps = []
    for i in range(ncores):
        m = dict(shared)
        m["xs"] = xs[i]
        m["xp"] = xp[i * NPS:(i + 1) * NPS].reshape(NPS * SEQ, D)
        m["cak"] = cak[i].reshape(NA * PAST, -1); m["cav"] = cav[i].reshape(NA * PAST, -1)
        m["cbk"] = cbk[i].reshape(NB * PAST, -1); m["cbv"] = cbv[i].reshape(NB * PAST, -1)
        m["cond"] = np.ascontiguousarray(np.stack([cc[i], cctx], axis=0))
        maps.append(m)
    return maps


def gather(cfg, results, ncores):
    D, DEPTH, NS, NPS, SEQ = cfg["D"], cfg["DEPTH"], cfg["NS"], cfg["NPS"], cfg["SEQ"]
    NA, NB = (DEPTH + 1) // 2, DEPTH // 2
    ys = np.stack([results[i]["ys"] for i in range(ncores)], 0)
    yp = np.concatenate([results[i]["yp"].reshape(NPS, SEQ, D) for i in range(ncores)], 0)
    sak = np.concatenate([results[i]["sak"].reshape(NPS, NA, SEQ, cfg["AKV"], 128) for i in range(ncores)], 0)
    sav = np.concatenate([results[i]["sav"].reshape(NPS, NA, SEQ, cfg["AKV"], 128) for i in range(ncores)], 0)
    sbk = np.concatenate([results[i]["sbk"].reshape(NPS, NB, SEQ, cfg["BH"], 2, 128) for i in range(ncores)], 0)
    sbv = np.concatenate([results[i]["sbv"].reshape(NPS, NB, SEQ, cfg["BH"], 256) for i in range(ncores)], 0)
    return (yp.astype(np.float32), ys.astype(np.float32), sak.astype(np.float32), sav.astype(np.float32),
            sbk.astype(np.float32), sbv.astype(np.float32))


def run(cfg, inputs, ncores=8):
    nc = build(cfg)
    maps = make_in_maps(cfg, inputs, ncores)
    res = run_bass_kernel_spmd(nc, maps, core_ids=list(range(ncores)))
    return gather(cfg, res.results, ncores)


def kernel(**inputs):
    return run(CFG_FULL, inputs, 8)
```

```python
import math
from contextlib import ExitStack
import numpy as np
import concourse.bass as bass
import concourse.mybir as mybir
from concourse.bass_utils import run_bass_kernel_spmd

F32 = mybir.dt.float32
BF16 = mybir.dt.bfloat16
AF = mybir.ActivationFunctionType
ALU = mybir.AluOpType
ENGS = ("pe", "act", "dve", "pool", "sp")
EPS = 1e-6

CFG_FULL = dict(D=2048, DEPTH=4, NS=4096, PAST=512, NPS=4, SEQ=256, AH=16, AKV=4, BH=8,
                DFF=5632, GRID_W=64)


class Buf:
    __slots__ = ("wc", "wd", "rc", "rd", "prc", "prd", "was_read", "sem", "lastdma")

    def __init__(self):
        self.prc = {}
        self.prd = []
        self.wc = {}
        self.wd = []
        self.rc = {}
        self.rd = []
        self.was_read = False
        self.sem = None
        self.lastdma = None


class DSem:
    __slots__ = ("h", "count")

    def __init__(self, h):
        self.h = h
        self.count = 0


class Op:
    __slots__ = ("eng", "fn", "deps", "is_dma", "sem", "dval", "signal", "count")

    def __init__(self, eng, fn, is_dma):
        self.eng = eng
        self.fn = fn
        self.deps = []
        self.is_dma = is_dma
        self.sem = None
        self.dval = 0
        self.signal = False
        self.count = 0


class Ctx:
    def __init__(self, nc, stack, ndsem=56):
        self.nc = nc
        self.esem = {e: stack.enter_context(nc.semaphore("es_" + e)) for e in ENGS}
        self.ecount = {e: 0 for e in ENGS}
        self.dsems = [DSem(stack.enter_context(nc.semaphore("ds%d" % i))) for i in range(ndsem)]
        self.sw = self.dsems[:14]
        self.nph = 0
        self.stop = None
        self.sub = None
        self.hw = self.dsems[14:]


class Phase:
    def __init__(self, ctx):
        self.ctx = ctx
        self.ops = {e: [] for e in ENGS}
        self.nsw = 0
        self.nhw = 0
        self.homesem = {}
        self.nrec = 0
        self.limit = ctx.sub if (ctx.stop is not None and ctx.nph + 1 == ctx.stop) else None

    def _rec(self, o, reads, writes):
        eng, is_dma = o.eng, o.is_dma
        deps = o.deps
        for r in reads:
            for e2, d in r.wc.items():
                if not (eng == "pe" and e2 == "pe" and not is_dma):
                    deps.append(d)
            deps.extend(r.wd)
        for w in writes:
            if w.was_read:
                w.prc = w.rc
                w.prd = w.rd
                w.wc = {}
                w.wd = []
                w.rc = {}
                w.rd = []
                w.was_read = False
            for e2, d in w.prc.items():
                if is_dma or e2 != eng:
                    deps.append(d)
            deps.extend(w.prd)
        for r in reads:
            r.was_read = True
            if is_dma:
                r.rd.append(o)
            else:
                r.rc[eng] = o
        for w in writes:
            if is_dma:
                w.wd.append(o)
            else:
                w.wc[eng] = o
        self.ops[eng].append(o)
        return o

    def op(self, eng, fn, reads=(), writes=()):
        self.nrec += 1
        if self.limit is not None and self.nrec > self.limit:
            return None
        return self._rec(Op(eng, fn, False), reads, writes)

    def dma(self, queue, fn, reads, writes, home):
        self.nrec += 1
        if self.limit is not None and self.nrec > self.limit:
            return None
        o = Op(queue, fn, True)
        key = (id(home), queue == "pool")
        if key not in self.homesem:
            if queue == "pool":
                assert self.nsw < len(self.ctx.sw)
                self.homesem[key] = self.ctx.sw[self.nsw]
                self.nsw += 1
            else:
                assert self.nhw < len(self.ctx.hw)
                self.homesem[key] = self.ctx.hw[self.nhw]
                self.nhw += 1
            home.lastdma = None
        home.sem = self.homesem[key]
        if home.lastdma is not None:
            o.deps.append(home.lastdma)
        home.lastdma = o
        o.sem = home.sem
        o.sem.count += 16
        o.dval = o.sem.count
        return self._rec(o, reads, writes)

    def emit(self):
        ctx = self.ctx
        nc = ctx.nc
        ctx.nph += 1
        if ctx.stop is not None and ctx.nph > ctx.stop:
            for e in ENGS:
                for o in self.ops[e]:
                    if o.is_dma:
                        o.sem.count -= 16
            return
        start_counts = dict(ctx.ecount)
        start_d = [(s, s.count) for s in ctx.dsems]
        for e in ENGS:
            lst = self.ops[e]
            for o in lst:
                for d in o.deps:
                    if not d.is_dma:
                        d.signal = True
            for o in reversed(lst):
                if not o.is_dma:
                    o.signal = True
                    break
        for e in ENGS:
            c = ctx.ecount[e]
            for o in self.ops[e]:
                if o.signal and not o.is_dma:
                    c += 1
                    o.count = c
            ctx.ecount[e] = c
        pre_d = {}
        for e in ENGS:
            for o in self.ops[e]:
                if o.is_dma and id(o.sem) not in pre_d:
                    pre_d[id(o.sem)] = o.dval - 16
        esem = ctx.esem

        def run(en, e):
            known = {}
            for e2 in ENGS:
                if start_counts[e2] > 0:
                    e.wait_ge(esem[e2], start_counts[e2])
                    known[id(esem[e2])] = start_counts[e2]
            for s, cnt in start_d:
                v = pre_d.get(id(s), cnt)
                if v > 0:
                    e.wait_ge(s.h, v)
                    known[id(s.h)] = v
            for o in self.ops[en]:
                for d in o.deps:
                    if d.is_dma:
                        sem, val = d.sem.h, d.dval
                    else:
                        sem, val = esem[d.eng], d.count
                    k = id(sem)
                    if known.get(k, 0) < val:
                        e.wait_ge(sem, val)
                        known[k] = val
                ins = o.fn(e)
                if o.is_dma:
                    ins.then_inc(o.sem.h, 16)
                elif o.signal:
                    ins.then_inc(esem[en], 1)

        with nc.Block() as block:
            @block.sync
            def _(e):
                run("sp", e)

            @block.tensor
            def _(e):
                run("pe", e)

            @block.scalar
            def _(e):
                run("act", e)

            @block.vector
            def _(e):
                run("dve", e)

            @block.gpsimd
            def _(e):
                run("pool", e)


def final_wait(ctx):
    nc = ctx.nc
    with nc.Block() as block:
        @block.sync
        def _(e):
            for e2 in ENGS:
                if ctx.ecount[e2] > 0:
                    e.wait_ge(ctx.esem[e2], ctx.ecount[e2])
            for s in ctx.dsems:
                if s.count > 0:
                    e.wait_ge(s.h, s.count)


class T:
    __slots__ = ("t", "b")

    def __init__(self, t):
        self.t = t
        self.b = Buf()


def build(cfg):
    D, DEPTH, NS, PAST = cfg["D"], cfg["DEPTH"], cfg["NS"], cfg["PAST"]
    NPS, SEQ, AH, AKV, BH, DFF = cfg["NPS"], cfg["SEQ"], cfg["AH"], cfg["AKV"], cfg["BH"], cfg["DFF"]
    KC = D // 128
    FC = DFF // 128
    NP = NPS * SEQ
    NTOK = NS + NP
    NKV = NS + PAST + NP
    NCH = NKV // 128
    NA, NB = (DEPTH + 1) // 2, DEPTH // 2
    QA = (AH + 2 * AKV) * 128
    AG = AH // AKV
    assert SEQ == 256 and AG == 4 and NS % 512 == 0 and NP % 512 == 0 and D % 512 == 0 and DFF % 512 == 0
    QCH = max(AH, 2 * BH)
    VDA, VDB = AKV * 128, BH * 256
    VD = max(VDA, VDB)

    nc = bass.Bass("TRN2", target_bir_lowering=False)

    def din(name, shape):
        return nc.dram_tensor(name, list(shape), F32, kind="ExternalInput").ap()

    def dout(name, shape):
        return nc.dram_tensor(name, list(shape), F32, kind="ExternalOutput").ap()

    xs = din("xs", [NS, D]); xp = din("xp", [NP, D])
    cak = din("cak", [NA * PAST, VDA]); cav = din("cav", [NA * PAST, VDA])
    cbk = din("cbk", [max(NB, 1) * PAST, VDB]); cbv = din("cbv", [max(NB, 1) * PAST, VDB])
    cond = din("cond", [2, D])
    ada_w = din("ada_w", [DEPTH * D, 6 * D]); ada_b = din("ada_b", [DEPTH, 6 * D])
    n1g = din("norm1_g", [DEPTH, D]); n2g = din("norm2_g", [DEPTH, D])
    a_wqkv = din("a_w_qkv", [NA * D, QA]); a_qn = din("a_q_norm", [NA, 128]); a_kn = din("a_k_norm", [NA, 128])
    a_sink = din("a_sink", [NA, AH]); a_wo = din("a_w_o", [NA * D, D])
    b_wqkv = din("b_w_qkv", [max(NB, 1) * D, 3 * D]); b_qn = din("b_q_norm", [max(NB, 1), 128])
    b_kn = din("b_k_norm", [max(NB, 1), 128])
    b_l = [din("b_lambda_" + n, [max(NB, 1), 128]) for n in ("q1", "k1", "q2", "k2")]
    b_sub = din("b_subln", [max(NB, 1), 256]); b_wo = din("b_w_o", [max(NB, 1) * D, D])
    w_up = din("ffn_w_up", [DEPTH * D, 2 * DFF]); cw = din("ffn_conv_w", [DEPTH * 3, DFF])
    cbias = din("ffn_conv_b", [DEPTH, DFF]); w_dn = din("ffn_w_down", [DEPTH * DFF, D])
    ropec = din("rope_cos", [128, NS]); ropes = din("rope_sin", [128, NS])
    consts = din("consts", [128, 640])

    ys = dout("ys", [NS, D]); yp = dout("yp", [NP, D])
    sak = dout("sak", [NPS * NA * SEQ, VDA]); sav = dout("sav", [NPS * NA * SEQ, VDA])
    sbk = dout("sbk", [NPS * max(NB, 1) * SEQ, VDB]); sbv = dout("sbv", [NPS * max(NB, 1) * SEQ, VDB])

    XT = nc.dram_tensor("XT", [KC, 128, NTOK], F32).ap()
    X1T = nc.dram_tensor("X1T", [KC, 128, NTOK], F32).ap()
    QT = nc.dram_tensor("QT", [QCH, 128, NTOK], BF16).ap()
    KT = nc.dram_tensor("KT", [QCH, 128, NKV], BF16).ap()
    VS = nc.dram_tensor("VS", [NKV, VD], BF16).ap()
    OT = nc.dram_tensor("OT", [KC, 128, NTOK], BF16).ap()
    XTv = XT.rearrange("k p t -> p k t"); X1Tv = X1T.rearrange("k p t -> p k t")
    QTv = QT.rearrange("k p t -> p k t"); KTv = KT.rearrange("k p t -> p k t")
    OTv = OT.rearrange("k p t -> p k t")

    groups = [(False, 512 * g, 512 * g, g) for g in range(NS // 512)]
    groups += [(True, NS + 512 * g, NS + PAST + 512 * g, g) for g in range(NP // 512)]
    dbuf = {}

    def DB(name, key):
        k = (name, key)
        if k not in dbuf:
            dbuf[k] = Buf()
        return dbuf[k]

    with ExitStack() as gst:
        ctx = Ctx(nc, gst)
        ctx.stop = cfg.get("STOP")
        ctx.sub = cfg.get("SUB")

        nmc = [0]

        def sb(st, name, shape, dt):
            nmc[0] += 1
            return T(st.enter_context(nc.sbuf_tensor("%s_%d" % (name, nmc[0]), list(shape), dt)))

        cst = sb(gst, "cst", [128, 640], BF16)
        identb = cst.t[:, 0:128]; rotm = cst.t[:, 128:256]
        trip = cst.t[:, 256:384]; trin = cst.t[:, 384:512]; onesb = cst.t[:, 512:640]
        identf = sb(gst, "identf", [128, 128], F32)
        mod = sb(gst, "mod", [128, DEPTH, 2, 6 * KC], F32)
        Am = sb(gst, "Am", [128, DEPTH, 2, 2, KC], F32)
        psA = T(gst.enter_context(nc.psum_tensor("psA", [128, 2, 512], F32)))
        psB = T(gst.enter_context(nc.psum_tensor("psB", [128, 2, 512], F32)))
        psC = T(gst.enter_context(nc.psum_tensor("psC", [128, 512], F32)))
        psD = T(gst.enter_context(nc.psum_tensor("psD", [128, 512], F32)))
        psE = T(gst.enter_context(nc.psum_tensor("psE", [128, 512], F32)))
        psT = T(gst.enter_context(nc.psum_tensor("psT", [128, 1024], BF16)))
        bA = [Buf(), Buf()]; bB = [Buf(), Buf()]

        def newbufs():
            for t in (psA, psB, psC, psD, psE, psT, cst, identf, mod, Am):
                pass

        with ExitStack() as st:
            ph = Phase(ctx)
            ph.dma("pool", lambda e: e.dma_start(out=cst.t[:], in_=consts), [], [cst.b], cst.b)
            ph.dma("sp", lambda e: e.dma_start(out=identf.t[:], in_=consts[:, 0:128]), [], [identf.b], identf.b)
            cT_ = sb(st, "condT", [128, KC, 2], F32)
            scT = sb(st, "scT", [128, KC, 2], BF16)
            for c_ in range(2):
                ph.dma("sp", lambda e, c_=c_: e.dma_start(out=cT_.t[:, :, c_], in_=cond[c_, :].rearrange("(k p) -> p k", p=128),
                                                          allow_slow_non_contiguous=True), [], [cT_.b], cT_.b)
            ph.op("act", lambda e: e.activation(out=scT.t[:], in_=cT_.t[:], func=AF.Silu), [cT_.b], [scT.b])
            wsl = [sb(st, "mw%d" % i, [128, KC, 512], BF16) for i in range(4)]
            abT = sb(st, "abT", [128, 6 * KC], F32)
            ngT = sb(st, "ngT", [128, 2, KC], F32)
            tmpm = sb(st, "tmpm", [128, 2, KC], F32)
            wi = 0
            for L in range(DEPTH):
                for j6 in range(6):
                    ph.dma("sp", lambda e, L=L, j6=j6: e.dma_start(
                        out=abT.t[:, j6 * KC:(j6 + 1) * KC], in_=ada_b[L, j6 * D:(j6 + 1) * D].rearrange("(j p) -> p j", p=128),
                        allow_slow_non_contiguous=True), [], [abT.b], abT.b)
                ph.dma("sp", lambda e, L=L: e.dma_start(out=ngT.t[:, 0, :], in_=n1g[L, :].rearrange("(j p) -> p j", p=128),
                                                        allow_slow_non_contiguous=True), [], [ngT.b], ngT.b)
                ph.dma("sp", lambda e, L=L: e.dma_start(out=ngT.t[:, 1, :], in_=n2g[L, :].rearrange("(j p) -> p j", p=128),
                                                        allow_slow_non_contiguous=True), [], [ngT.b], ngT.b)
                Wl = ada_w[L * D:(L + 1) * D, :].rearrange("(k p) n -> p k n", p=128)
                for cb_ in range(6 * D // 512):
                    w = wsl[wi % 4]; wi += 1
                    ph.dma("pool", lambda e, w=w, cb_=cb_, Wl=Wl: e.dma_start(out=w.t[:], in_=Wl[:, :, cb_ * 512:(cb_ + 1) * 512]),
                           [], [w.b], w.b)
                    for m in range(4):
                        col = cb_ * 4 + m
                        for k in range(KC):
                            ph.op("pe", lambda e, w=w, m=m, k=k, col=col: e.matmul(
                                psC.t[:, col * 2:col * 2 + 2], lhsT=w.t[:, k, m * 128:(m + 1) * 128], rhs=scT.t[:, k, :],
                                start=(k == 0), stop=(k == KC - 1)), [w.b, scT.b], [psC.b])
                for c in range(2):
                    ph.op("dve", lambda e, L=L, c=c: e.tensor_tensor(
                        out=mod.t[:, L, c, :], in0=psC.t[:, 0:12 * KC].rearrange("p (j c) -> p j c", c=2)[:, :, c],
                        in1=abT.t[:], op=ALU.add), [psC.b, abT.b], [mod.b])
                    for n in range(2):
                        ph.op("dve", lambda e, L=L, c=c, n=n: e.tensor_scalar(
                            out=tmpm.t[:, n, :], in0=mod.t[:, L, c, (3 * n + 1) * KC:(3 * n + 2) * KC], scalar1=1.0, scalar2=None,
                            op0=ALU.add), [mod.b], [tmpm.b])
                        ph.op("dve", lambda e, L=L, c=c, n=n: e.tensor_tensor(
                            out=Am.t[:, L, c, n, :], in0=tmpm.t[:, n, :], in1=ngT.t[:, n, :], op=ALU.mult),
                            [tmpm.b, ngT.b], [Am.b])
            ph.emit()

        with ExitStack() as st:
            ph = Phase(ctx)
            xin = [sb(st, "xin%d" % i, [128, D], F32) for i in range(2)]
            xst = [sb(st, "xst%d" % i, [128, KC, 512], F32) for i in range(2)]
            pss = [(psA.t[:, 0, :], bA[0]), (psA.t[:, 1, :], bA[1]), (psB.t[:, 0, :], bB[0]), (psB.t[:, 1, :], bB[1]),
                   (psC.t[:], psC.b), (psD.t[:], psD.b), (psE.t[:], psE.b)]
            pi = 0; ti = 0
            for gi, (isp, t0, kv0, g) in enumerate(groups):
                src = xp if isp else xs
                r0 = t0 - NS if isp else t0
                xo = xst[gi % 2]
                for tt in range(4):
                    xi = xin[ti % 2]; ti += 1
                    ph.dma("sp", lambda e, xi=xi, src=src, r=r0 + tt * 128: e.dma_start(out=xi.t[:], in_=src[r:r + 128, :]),
                           [], [xi.b], xi.b)
                    for k4 in range(KC // 4):
                        pt, pb = pss[pi % 7]; pi += 1
                        for q in range(4):
                            k = k4 * 4 + q
                            ph.op("pe", lambda e, pt=pt, xi=xi, k=k, q=q: e.transpose(
                                pt[:, q * 128:(q + 1) * 128], in_=xi.t[:, k * 128:(k + 1) * 128], identity=identf.t[:]),
                                [xi.b, identf.b], [pb])
                        eng = "act" if (pi % 2) else "dve"
                        if eng == "act":
                            ph.op("act", lambda e, pt=pt, xo=xo, k4=k4, tt=tt: e.activation(
                                out=xo.t[:, k4 * 4:k4 * 4 + 4, tt * 128:(tt + 1) * 128],
                                in_=pt.rearrange("p (q t) -> p q t", q=4), func=AF.Copy), [pb], [xo.b])
                        else:
                            ph.op("dve", lambda e, pt=pt, xo=xo, k4=k4, tt=tt: e.tensor_copy(
                                out=xo.t[:, k4 * 4:k4 * 4 + 4, tt * 128:(tt + 1) * 128],
                                in_=pt.rearrange("p (q t) -> p q t", q=4)), [pb], [xo.b])
                ph.dma("sp", lambda e, xo=xo, t0=t0: e.dma_start(out=XTv[:, :, t0:t0 + 512], in_=xo.t[:]),
                       [xo.b], [DB("XT", gi)], xo.b)
            ph.emit()

        def norm_mod(ph, x, W, segs, L, c, n, sq, hT, rstd, tmpf, psn):
            ph.op("act", lambda e: e.activation(out=sq.t[:, :, 0:W], in_=x.t[:, :, 0:W], func=AF.Square), [x.b], [sq.b])
            for (c0, n_) in segs:
                for k in range(KC):
                    ph.op("pe", lambda e, k=k, c0=c0, n_=n_: e.matmul(
                        psn.t[:, 0:n_], lhsT=onesb, rhs=sq.t[:, k, c0:c0 + n_], start=(k == 0), stop=(k == KC - 1)),
                        [sq.b, cst.b], [psn.b])
                ph.op("act", lambda e, c0=c0, n_=n_: e.activation(
                    out=tmpf.t[:, c0:c0 + n_], in_=psn.t[:, 0:n_], func=AF.Sqrt, scale=1.0 / D, bias=EPS), [psn.b], [tmpf.b])
            ph.op("dve", lambda e: e.reciprocal(out=rstd.t[:, 0:W], in_=tmpf.t[:, 0:W]), [tmpf.b], [rstd.b])
            for k in range(KC):
                ph.op("dve", lambda e, k=k: e.scalar_tensor_tensor(
                    out=hT.t[:, k, 0:W], in0=x.t[:, k, 0:W], scalar=Am.t[:, L, c, n, k:k + 1], in1=rstd.t[:, 0:W],
                    op0=ALU.mult, op1=ALU.mult), [x.b, Am.b, rstd.b], [hT.b])
            for k in range(KC):
                ph.op("act", lambda e, k=k: e.activation(
                    out=hT.t[:, k, 0:W], in_=hT.t[:, k, 0:W], func=AF.Identity,
                    bias=mod.t[:, L, c, 3 * n * KC + k:3 * n * KC + k + 1], scale=1.0), [hT.b, mod.b], [hT.b])

        for L in range(DEPTH):
            isA = (L % 2 == 0)
            j = L // 2
            lam_init = 0.8 - 0.6 * math.exp(-0.3 * L)
            if isA:
                Wqkv = a_wqkv[j * D:(j + 1) * D, :].rearrange("(k p) n -> p k n", p=128)
                Wo = a_wo[j * D:(j + 1) * D, :].rearrange("(k p) n -> p k n", p=128)
                nq, nk, nv = AH, AKV, AKV
                qn_ap, kn_ap = a_qn, a_kn
                ck, cv = cak[j * PAST:(j + 1) * PAST, :], cav[j * PAST:(j + 1) * PAST, :]
                sk, sv = sak, sav
                VDl = VDA
                NLs = NA
            else:
                Wqkv = b_wqkv[j * D:(j + 1) * D, :].rearrange("(k p) n -> p k n", p=128)
                Wo = b_wo[j * D:(j + 1) * D, :].rearrange("(k p) n -> p k n", p=128)
                nq, nk, nv = 2 * BH, 2 * BH, 2 * BH
                qn_ap, kn_ap = b_qn, b_kn
                ck, cv = cbk[j * PAST:(j + 1) * PAST, :], cbv[j * PAST:(j + 1) * PAST, :]
                sk, sv = sbk, sbv
                VDl = VDB
                NLs = NB
            nchunks = nq + nk + nv
            Wup = w_up[L * D:(L + 1) * D, :].rearrange("(k p) n -> p k n", p=128)
            Wdn = w_dn[L * DFF:(L + 1) * DFF, :].rearrange("(m p) f -> p m f", p=128)

            with ExitStack() as st:
                ph = Phase(ctx)
                gqk = sb(st, "gqk", [128, 2], F32)
                ph.dma("sp", lambda e: e.dma_start(out=gqk.t[:, 0:1], in_=qn_ap[j, :].rearrange("(p o) -> p o", o=1)), [], [gqk.b], gqk.b)
                ph.dma("sp", lambda e: e.dma_start(out=gqk.t[:, 1:2], in_=kn_ap[j, :].rearrange("(p o) -> p o", o=1)), [], [gqk.b], gqk.b)
                gs = sb(st, "gs", [128, 2], F32)
                ph.op("act", lambda e: e.mul(out=gs.t[:, 0:1], in_=gqk.t[:, 0:1], mul=128.0 ** -0.5), [gqk.b], [gs.b])
                ph.op("act", lambda e: e.copy(out=gs.t[:, 1:2], in_=gqk.t[:, 1:2]), [gqk.b], [gs.b])
                ph.dma("pool", lambda e: e.dma_start(out=VS[NS:NS + PAST, 0:VDl], in_=cv), [], [DB("VS", "ctx")], DB("VS", "ctx"))
                ckin = [sb(st, "ckin%d" % i, [128, VDl], F32) for i in range(2)]
                kst_c = [sb(st, "kstc%d" % i, [128, 4, 128], BF16) for i in range(2)]
                kci = 0
                for cch in range(PAST // 128):
                    ci_ = ckin[cch % 2]
                    ph.dma("sp", lambda e, ci_=ci_, cch=cch: e.dma_start(out=ci_.t[:], in_=ck[cch * 128:(cch + 1) * 128, :]),
                           [], [ci_.b], ci_.b)
                    for h4 in range((nk + 3) // 4):
                        nh = min(4, nk - h4 * 4)
                        for q in range(nh):
                            h = h4 * 4 + q
                            ph.op("pe", lambda e, ci_=ci_, h=h, q=q: e.transpose(
                                psD.t[:, q * 128:(q + 1) * 128], in_=ci_.t[:, h * 128:(h + 1) * 128], identity=identf.t[:]),
                                [ci_.b, identf.b], [psD.b])
                        ks_ = kst_c[kci % 2]; kci += 1
                        ph.op("dve", lambda e, ks_=ks_, nh=nh: e.tensor_copy(
                            out=ks_.t[:, 0:nh, :], in_=psD.t[:, 0:nh * 128].rearrange("p (q t) -> p q t", q=nh)), [psD.b], [ks_.b])
                        ph.dma("sp", lambda e, ks_=ks_, h4=h4, cch=cch, nh=nh: e.dma_start(
                            out=KTv[:, h4 * 4:h4 * 4 + nh, NS + cch * 128:NS + (cch + 1) * 128], in_=ks_.t[:, 0:nh, :]),
                            [ks_.b], [DB("KT", "ctx")], ks_.b)

                xt = sb(st, "xt", [128, KC, 512], F32)
                sq = sb(st, "sq", [128, KC, 512], BF16)
                hT = sb(st, "hT", [128, KC, 512], BF16)
                rstdn = sb(st, "rstdn", [128, 512], F32); tmpn = sb(st, "tmpn", [128, 512], F32)
                wsl = [sb(st, "w%d" % i, [128, KC, 512], BF16) for i in range(4)]
                cosT = sb(st, "cosT", [128, 512], F32); sinT = sb(st, "sinT", [128, 512], F32)
                sqc = [sb(st, "sqc%d" % i, [128, 512], BF16) for i in range(2)]
                tq = [sb(st, "tq%d" % i, [128, 512], F32) for i in range(2)]
                rq = [sb(st, "rq%d" % i, [128, 512], F32) for i in range(2)]
                qg = [sb(st, "qg%d" % i, [128, 512], BF16) for i in range(2)]
                t1 = [sb(st, "t1%d" % i, [128, 512], F32) for i in range(2)]
                t2 = [sb(st, "t2%d" % i, [128, 512], F32) for i in range(2)]
                cTs = [sb(st, "cT%d" % i, [128, 512], BF16) for i in range(3)]
                vtok = [sb(st, "vtok%d" % i, [128, 4, 128], BF16) for i in range(2)]
                stg = [sb(st, "stg%d" % i, [128, 4, 128], F32) for i in range(2)]
                wi = 0; cn = 0; vi = 0; si = 0
                for gi, (isp, t0, kv0, g) in enumerate(groups):
                    c = 1 if isp else 0
                    ph.dma("sp", lambda e, t0=t0: e.dma_start(out=xt.t[:], in_=XTv[:, :, t0:t0 + 512]),
                           [DB("XT", gi)], [xt.b], xt.b)
                    norm_mod(ph, xt, 512, [(0, 512)], L, c, 0, sq, hT, rstdn, tmpn, psC)
                    if not isp:
                        ph.dma("sp", lambda e, t0=t0: e.dma_start(out=cosT.t[:], in_=ropec[:, t0:t0 + 512]), [], [cosT.b], cosT.b)
                        ph.dma("sp", lambda e, t0=t0: e.dma_start(out=sinT.t[:], in_=ropes[:, t0:t0 + 512]), [], [sinT.b], sinT.b)
                    w = None
                    for m in range(nchunks):
                        if m % 4 == 0:
                            w = wsl[wi % 4]; wi += 1
                            ncol = min(512, (nchunks - m) * 128)
                            ph.dma("pool", lambda e, w=w, m=m, ncol=ncol: e.dma_start(out=w.t[:, :, 0:ncol], in_=Wqkv[:, :, m * 128:m * 128 + ncol]),
                                   [], [w.b], w.b)
                        typ = "q" if m < nq else ("k" if m < nq + nk else "v")
                        hidx = m if typ == "q" else (m - nq if typ == "k" else m - nq - nk)
                        pm, pmb = (psA.t[:, cn % 2, :], bA[cn % 2])
                        x2 = cn % 2; cn += 1
                        for k in range(KC):
                            ph.op("pe", lambda e, w=w, m=m, k=k, pm=pm: e.matmul(
                                pm, lhsT=w.t[:, k, (m % 4) * 128:(m % 4 + 1) * 128], rhs=hT.t[:, k, :],
                                start=(k == 0), stop=(k == KC - 1)), [w.b, hT.b], [pmb])
                        cT = cTs[cn % 3]
                        if typ == "v":
                            ph.op("act", lambda e, cT=cT, pm=pm: e.activation(out=cT.t[:], in_=pm, func=AF.Copy), [pmb], [cT.b])
                        else:
                            gcol = 0 if typ == "q" else 1
                            ph.op("act", lambda e, x2=x2, pm=pm: e.activation(out=sqc[x2].t[:], in_=pm, func=AF.Square), [pmb], [sqc[x2].b])
                            ph.op("pe", lambda e, x2=x2: e.matmul(psB.t[:, 0, :], lhsT=onesb, rhs=sqc[x2].t[:], start=True, stop=True),
                                  [sqc[x2].b, cst.b], [bB[0]])
                            ph.op("act", lambda e, x2=x2: e.activation(out=tq[x2].t[:], in_=psB.t[:, 0, :], func=AF.Sqrt, scale=1.0 / 128, bias=EPS),
                                  [bB[0]], [tq[x2].b])
                            ph.op("dve", lambda e, x2=x2: e.reciprocal(out=rq[x2].t[:], in_=tq[x2].t[:]), [tq[x2].b], [rq[x2].b])
                            if isp:
                                ph.op("dve", lambda e, x2=x2, cT=cT, pm=pm, gcol=gcol: e.scalar_tensor_tensor(
                                    out=cT.t[:], in0=pm, scalar=gs.t[:, gcol:gcol + 1], in1=rq[x2].t[:], op0=ALU.mult, op1=ALU.mult),
                                    [pmb, gs.b, rq[x2].b], [cT.b])
                            else:
                                ph.op("act", lambda e, x2=x2, pm=pm, gcol=gcol: e.activation(
                                    out=qg[x2].t[:], in_=pm, func=AF.Copy, scale=gs.t[:, gcol:gcol + 1]), [pmb, gs.b], [qg[x2].b])
                                ph.op("pe", lambda e, x2=x2: e.matmul(psB.t[:, 1, :], lhsT=rotm, rhs=qg[x2].t[:], start=True, stop=True),
                                      [qg[x2].b, cst.b], [bB[1]])
                                ph.op("pool", lambda e, x2=x2: e.tensor_tensor(out=t1[x2].t[:], in0=qg[x2].t[:], in1=cosT.t[:], op=ALU.mult),
                                      [qg[x2].b, cosT.b], [t1[x2].b])
                                ph.op("dve", lambda e, x2=x2: e.tensor_tensor(out=t2[x2].t[:], in0=psB.t[:, 1, :], in1=sinT.t[:], op=ALU.mult),
                                      [bB[1], sinT.b], [t2[x2].b])
                                ph.op("pool", lambda e, x2=x2: e.tensor_tensor(out=t1[x2].t[:], in0=t1[x2].t[:], in1=t2[x2].t[:], op=ALU.add),
                                      [t1[x2].b, t2[x2].b], [t1[x2].b])
                                ph.op("dve", lambda e, x2=x2, cT=cT: e.tensor_tensor(out=cT.t[:], in0=t1[x2].t[:], in1=rq[x2].t[:], op=ALU.mult),
                                      [t1[x2].b, rq[x2].b], [cT.b])
                        if typ == "q":
                            ph.dma("sp", lambda e, cT=cT, hidx=hidx, t0=t0: e.dma_start(out=QT[hidx, :, t0:t0 + 512], in_=cT.t[:]),
                                   [cT.b], [DB("QT", gi)], cT.b)
                        elif typ == "k":
                            ph.dma("sp", lambda e, cT=cT, hidx=hidx, kv0=kv0: e.dma_start(out=KT[hidx, :, kv0:kv0 + 512], in_=cT.t[:]),
                                   [cT.b], [DB("KT", gi)], cT.b)
                        if typ == "v" or (typ == "k" and isp):
                            for tt in range(4):
                                ph.op("pe", lambda e, cT=cT, tt=tt: e.transpose(
                                    psT.t[:, tt * 128:(tt + 1) * 128], in_=cT.t[:, tt * 128:(tt + 1) * 128], identity=identb),
                                    [cT.b, cst.b], [psT.b])
                            if typ == "v":
                                vt = vtok[vi % 2]; vi += 1
                                ph.op("dve", lambda e, vt=vt: e.tensor_copy(out=vt.t[:], in_=psT.t[:, 0:512].rearrange("p (t f) -> p t f", t=4)),
                                      [psT.b], [vt.b])
                                ph.dma("sp", lambda e, vt=vt, hidx=hidx, kv0=kv0: e.dma_start(
                                    out=VS[kv0:kv0 + 512, hidx * 128:(hidx + 1) * 128].rearrange("(t p) f -> p t f", p=128), in_=vt.t[:]),
                                    [vt.b], [DB("VS", gi)], vt.b)
                            if isp:
                                sg_ = stg[si % 2]; si += 1
                                if typ == "v":
                                    ph.op("act", lambda e, sg_=sg_, vt=vt: e.activation(out=sg_.t[:], in_=vt.t[:], func=AF.Copy), [vt.b], [sg_.b])
                                else:
                                    ph.op("act", lambda e, sg_=sg_: e.activation(out=sg_.t[:], in_=psT.t[:, 0:512].rearrange("p (t f) -> p t f", t=4), func=AF.Copy),
                                          [psT.b], [sg_.b])
                                dst = sk if typ == "k" else sv
                                for s2 in range(2):
                                    sq_ = 2 * g + s2
                                    r0 = (sq_ * NLs + j) * SEQ
                                    ph.dma("sp", lambda e, sg_=sg_, dst=dst, r0=r0, hidx=hidx, s2=s2: e.dma_start(
                                        out=dst[r0:r0 + SEQ, hidx * 128:(hidx + 1) * 128].rearrange("(t p) f -> p t f", p=128),
                                        in_=sg_.t[:, 2 * s2:2 * s2 + 2, :]), [sg_.b], [], sg_.b)
                ph.emit()

            with ExitStack() as st:
                ph = Phase(ctx)
                nhg = AKV if isA else BH
                ncmap = 1 if isA else 2
                dva = 129 if isA else 257
                kt = [sb(st, "kt%d" % i, [128, ncmap, NKV], BF16) for i in range(2)]
                vt_ = [sb(st, "vt%d" % i, [128, NCH, dva], BF16) for i in range(2)]
                for v in vt_:
                    ph.op("pool", lambda e, v=v: e.memset(v.t[:, :, dva - 1:dva], 1.0), [], [v.b])
                qt = [sb(st, "qt%d" % i, [128, 4 if isA else 2, 512 if isA else 256], BF16) for i in range(2)]
                pT = [sb(st, "pT%d" % i, [128, 512], BF16) for i in range(3)]
                den = sb(st, "den", [128, 8], F32); rden = sb(st, "rden", [128, 8], F32)
                otok = [sb(st, "otok%d" % i, [128, 512], BF16) for i in range(2)]
                otmp = [sb(st, "otmp%d" % i, [128, 256], F32) for i in range(2)]
                junk = sb(st, "junk", [128, 256], F32)
                osb = [sb(st, "osb%d" % i, [128, 4, 512] if isA else [128, 2, 256], BF16) for i in range(2)]
                allkv = [DB("KT", gi) for gi in range(len(groups))] + [DB("KT", "ctx")]
                allv = [DB("VS", gi) for gi in range(len(groups))] + [DB("VS", "ctx")]
                if isA:
                    psS = [(psA.t[:, 0, :], bA[0]), (psA.t[:, 1, :], bA[1]), (psE.t[:], psE.b)]
                else:
                    psS = [(psA.t[:, 0, :], bA[0]), (psA.t[:, 1, :], bA[1]), (psB.t[:, 0, :], bB[0])]
                if isA:
                    esk = sb(st, "esk", [128, AH], F32)
                    ph.dma("sp", lambda e: e.dma_start(out=esk.t[:], in_=a_sink[j:j + 1, :].partition_broadcast(128)), [], [esk.b], esk.b)
                    ph.op("act", lambda e: e.activation(out=esk.t[:], in_=esk.t[:], func=AF.Exp), [esk.b], [esk.b])
                    accs = [(psB.t[:, 0, 0:129], bB[0]), (psB.t[:, 1, 0:129], bB[1]), (psC.t[:, 0:129], psC.b), (psD.t[:, 0:129], psD.b)]
                else:
                    lv = sb(st, "lv", [128, 4, 128], F32)
                    for q in range(4):
                        ph.dma("sp", lambda e, q=q: e.dma_start(out=lv.t[:, q, :], in_=b_l[q][j:j + 1, :].partition_broadcast(128)), [], [lv.b], lv.b)
                    lp = sb(st, "lp", [128, 2, 128], F32); ls = sb(st, "ls", [128, 2], F32); nlam = sb(st, "nlam", [128, 1], F32)
                    ph.op("dve", lambda e: e.tensor_tensor(out=lp.t[:, 0, :], in0=lv.t[:, 0, :], in1=lv.t[:, 1, :], op=ALU.mult), [lv.b], [lp.b])
                    ph.op("dve", lambda e: e.tensor_tensor(out=lp.t[:, 1, :], in0=lv.t[:, 2, :], in1=lv.t[:, 3, :], op=ALU.mult), [lv.b], [lp.b])
                    ph.op("dve", lambda e: e.reduce_sum(out=ls.t[:], in_=lp.t[:], axis=mybir.AxisListType.X), [lp.b], [ls.b])
                    ph.op("act", lambda e: e.activation(out=ls.t[:], in_=ls.t[:], func=AF.Exp), [ls.b], [ls.b])
                    ph.op("dve", lambda e: e.tensor_tensor(out=nlam.t[:], in0=ls.t[:, 1:2], in1=ls.t[:, 0:1], op=ALU.subtract), [ls.b], [nlam.b])
                    ph.op("dve", lambda e: e.tensor_scalar(out=nlam.t[:], in0=nlam.t[:], scalar1=-lam_init, scalar2=None, op0=ALU.add), [nlam.b], [nlam.b])
                    subT = sb(st, "subT", [128, 2], F32)
                    ph.dma("sp", lambda e: e.dma_start(out=subT.t[:], in_=b_sub[j, :].rearrange("(c p) -> p c", p=128), allow_slow_non_contiguous=True),
                           [], [subT.b], subT.b)
                    ph.op("act", lambda e: e.mul(out=subT.t[:], in_=subT.t[:], mul=1.0 - lam_init), [subT.b], [subT.b])
                    accs = [(psB.t[:, 1, 0:257], bB[1]), (psC.t[:, 0:257], psC.b), (psD.t[:, 0:257], psD.b), (psE.t[:, 0:257], psE.b)]
                si = 0; pi = 0; oi = 0; qi = 0; ai = 0
                if not isA:
                    otokB = [[sb(st, "otkB%d%d" % (u, q), [128, 256], BF16) for q in range(2)] for u in range(2)]
                    otmpB = [[sb(st, "otmB%d%d" % (u, q), [128, 256], F32) for q in range(2)] for u in range(2)]
                dq = []

                def step():
                    due = []; keep = []
                    for it in dq:
                        it[0] -= 1
                        (due if it[0] <= 0 else keep).append(it)
                    dq[:] = keep
                    for it in due:
                        it[1]()

                def flush():
                    while dq:
                        step()

                for hg in range(nhg):
                    ktile = kt[hg % 2]; vtile = vt_[hg % 2]
                    ph.dma("sp", lambda e, ktile=ktile, hg=hg: e.dma_start(out=ktile.t[:], in_=KTv[:, hg * ncmap:(hg + 1) * ncmap, :]),
                           allkv, [ktile.b], ktile.b)
                    dvw = dva - 1
                    ph.dma("sp", lambda e, vtile=vtile, hg=hg, dvw=dvw: e.dma_start(
                        out=vtile.t[:, :, 0:dvw], in_=VS[:, hg * dvw:(hg + 1) * dvw].rearrange("(c p) f -> p c f", p=128)),
                        allv, [vtile.b], vtile.b)
                    units = []
                    if isA:
                        for gi, (isp, t0, kv0, g) in enumerate(groups):
                            blocks = []
                            for blk in range(4):
                                if isp:
                                    cb0 = (kv0 + (blk // 2) * 256) // 128
                                    chunks = [(cb0, None), (cb0 + 1, None)]
                                else:
                                    jb = g * 4 + blk
                                    chunks = []
                                    if jb > 0:
                                        chunks.append((jb - 1, trip))
                                    chunks.append((jb, None))
                                    if jb < NS // 128 - 1:
                                        chunks.append((jb + 1, trin))
                                    chunks += [(NS // 128 + c_, None) for c_ in range(PAST // 128)]
                                blocks.append(chunks)
                            units.append((gi, t0, blocks))
                    else:
                        for gi, (isp, t0, kv0, g) in enumerate(groups):
                            for hf in range(2):
                                if isp:
                                    cb0 = (kv0 + hf * 256) // 128
                                    chunks = [(cb0, None), (cb0 + 1, None)]
                                else:
                                    chunks = [(c_, None) for c_ in range((NS + PAST) // 128)]
                                units.append((gi, t0 + hf * 256, [chunks]))
                    for (gi, t0, blocks) in units:
                        qtile = qt[qi % 2]; qi += 1
                        ob = osb[oi % 2]; oi += 1
                        if isA:
                            ph.dma("sp", lambda e, qtile=qtile, hg=hg, t0=t0: e.dma_start(out=qtile.t[:], in_=QTv[:, hg * 4:hg * 4 + 4, t0:t0 + 512]),
                                   [DB("QT", gi)], [qtile.b], qtile.b)
                        else:
                            ph.dma("sp", lambda e, qtile=qtile, hg=hg, t0=t0: e.dma_start(out=qtile.t[:], in_=QTv[:, hg * 2:hg * 2 + 2, t0:t0 + 256]),
                                   [DB("QT", gi)], [qtile.b], qtile.b)
                        for blk, chunks in enumerate(blocks):
                            nchk = len(chunks)
                            accset = accs
                            ai += 1
                            for cidx, (ci, msk) in enumerate(chunks):
                                ps_, psb_ = psS[si % 3]; si += 1
                                p_ = pT[pi % 3]; pi += 1
                                if isA:
                                    ph.op("pe", lambda e, ps_=ps_, ktile=ktile, ci=ci, qtile=qtile, blk=blk: e.matmul(
                                        ps_.rearrange("p (g t) -> p g t", g=4), lhsT=ktile.t[:, 0, ci * 128:(ci + 1) * 128],
                                        rhs=qtile.t[:, :, blk * 128:(blk + 1) * 128], start=True, stop=True),
                                        [ktile.b, qtile.b], [psb_])
                                else:
                                    for c_ in range(2):
                                        ph.op("pe", lambda e, ps_=ps_, ktile=ktile, ci=ci, qtile=qtile, c_=c_: e.matmul(
                                            ps_[:, c_ * 256:(c_ + 1) * 256], lhsT=ktile.t[:, c_, ci * 128:(ci + 1) * 128],
                                            rhs=qtile.t[:, c_, :], start=True, stop=True), [ktile.b, qtile.b], [psb_])
                                ph.op("act", lambda e, ps_=ps_, p_=p_: e.activation(out=p_.t[:], in_=ps_, func=AF.Exp), [psb_], [p_.b])
                                if msk is not None:
                                    ph.op("dve", lambda e, p_=p_, msk=msk: e.tensor_tensor(
                                        out=p_.t[:].rearrange("p (g t) -> p g t", g=4), in0=p_.t[:].rearrange("p (g t) -> p g t", g=4),
                                        in1=msk.unsqueeze(1).to_broadcast([128, 4, 128]), op=ALU.mult), [p_.b, cst.b], [p_.b])
                                step()

                                def pv(p_=p_, ci=ci, cidx=cidx, nchk=nchk, accset=accset, vtile=vtile):
                                    for a_ in range(4):
                                        ac, acb = accset[a_]
                                        ph.op("pe", lambda e, ac=ac, a_=a_: e.matmul(
                                            ac, lhsT=p_.t[:, a_ * 128:(a_ + 1) * 128], rhs=vtile.t[:, ci, :],
                                            start=(cidx == 0), stop=(cidx == nchk - 1)), [p_.b, vtile.b], [acb])
                                dq.append([1, pv])
                            lastblk = (blk == len(blocks) - 1)
                            if isA:
                                ot_ = otok[ai % 2]

                                def fin_a(accset=accset, ot_=ot_, hg=hg):
                                    for a_ in range(4):
                                        ac, acb = accset[a_]
                                        ph.op("dve", lambda e, ac=ac, a_=a_: e.tensor_tensor(
                                            out=den.t[:, a_:a_ + 1], in0=ac[:, 128:129],
                                            in1=esk.t[:, hg * 4 + a_:hg * 4 + a_ + 1], op=ALU.add), [acb, esk.b], [den.b])
                                    ph.op("dve", lambda e: e.reciprocal(out=rden.t[:, 0:4], in_=den.t[:, 0:4]), [den.b], [rden.b])
                                    for a_ in range(4):
                                        ac, acb = accset[a_]
                                        ph.op("dve", lambda e, ac=ac, a_=a_: e.tensor_scalar(
                                            out=ot_.t[:, a_ * 128:(a_ + 1) * 128], in0=ac[:, 0:128], scalar1=rden.t[:, a_:a_ + 1], scalar2=None,
                                            op0=ALU.mult), [acb, rden.b], [ot_.b])

                                def fin_b(ot_=ot_, ob=ob, blk=blk, lastblk=lastblk, hg=hg, t0=t0, gi=gi):
                                    for a_ in range(4):
                                        ph.op("pe", lambda e, a_=a_: e.transpose(
                                            psT.t[:, a_ * 128:(a_ + 1) * 128], in_=ot_.t[:, a_ * 128:(a_ + 1) * 128], identity=identb),
                                            [ot_.b, cst.b], [psT.b])
                                    ph.op("act", lambda e: e.activation(
                                        out=ob.t[:, :, blk * 128:(blk + 1) * 128], in_=psT.t[:, 0:512].rearrange("p (g t) -> p g t", g=4), func=AF.Copy),
                                        [psT.b], [ob.b])
                                    if lastblk:
                                        ph.dma("sp", lambda e: e.dma_start(out=OTv[:, hg * 4:hg * 4 + 4, t0:t0 + 512], in_=ob.t[:]),
                                               [ob.b], [DB("OT", gi)], ob.b)
                            else:
                                otk = otokB[ai % 2]; otm = otmpB[ai % 2]

                                def fin_a(accset=accset, otk=otk, otm=otm):
                                    for a_ in range(4):
                                        ac, acb = accset[a_]
                                        ph.op("dve", lambda e, ac=ac, a_=a_: e.tensor_copy(out=den.t[:, a_:a_ + 1], in_=ac[:, 256:257]), [acb], [den.b])
                                    ph.op("dve", lambda e: e.reciprocal(out=rden.t[:, 0:4], in_=den.t[:, 0:4]), [den.b], [rden.b])
                                    ph.op("dve", lambda e: e.tensor_scalar(out=rden.t[:, 2:4], in0=rden.t[:, 2:4], scalar1=nlam.t[:, 0:1], scalar2=None, op0=ALU.mult),
                                          [rden.b, nlam.b], [rden.b])
                                    for qb in range(2):
                                        om = otm[qb]
                                        a0, a0b = accset[qb]; a1, a1b = accset[2 + qb]
                                        ph.op("dve", lambda e, om=om, a0=a0, qb=qb: e.tensor_scalar(
                                            out=om.t[:], in0=a0[:, 0:256], scalar1=rden.t[:, qb:qb + 1], scalar2=None, op0=ALU.mult), [a0b, rden.b], [om.b])
                                        ph.op("dve", lambda e, om=om, a1=a1, qb=qb: e.scalar_tensor_tensor(
                                            out=om.t[:], in0=a1[:, 0:256], scalar=rden.t[:, 2 + qb:3 + qb], in1=om.t[:], op0=ALU.mult, op1=ALU.add),
                                            [a1b, rden.b, om.b], [om.b])
                                        ph.op("act", lambda e, om=om, qb=qb: e.activation(out=junk.t[:], in_=om.t[:], func=AF.Square, accum_out=den.t[:, 4 + qb:5 + qb]),
                                              [om.b], [junk.b, den.b])
                                        ph.op("act", lambda e, qb=qb: e.activation(out=den.t[:, 6 + qb:7 + qb], in_=den.t[:, 4 + qb:5 + qb], func=AF.Sqrt, scale=1.0 / 256, bias=EPS),
                                              [den.b], [den.b])
                                        ph.op("dve", lambda e, qb=qb: e.reciprocal(out=rden.t[:, 6 + qb:7 + qb], in_=den.t[:, 6 + qb:7 + qb]), [den.b], [rden.b])
                                        ot_ = otk[qb]
                                        ph.op("dve", lambda e, om=om, qb=qb, ot_=ot_: e.tensor_scalar(
                                            out=ot_.t[:, 0:256], in0=om.t[:], scalar1=rden.t[:, 6 + qb:7 + qb], scalar2=None, op0=ALU.mult), [om.b, rden.b], [ot_.b])

                                def fin_b(otk=otk, ob=ob, hg=hg, t0=t0, gi=gi):
                                    for qb in range(2):
                                        ot_ = otk[qb]
                                        for cc in range(2):
                                            ph.op("pe", lambda e, ot_=ot_, cc=cc, qb=qb: e.transpose(
                                                psT.t[:, (qb * 2 + cc) * 128:(qb * 2 + cc + 1) * 128], in_=ot_.t[:, cc * 128:(cc + 1) * 128], identity=identb),
                                                [ot_.b, cst.b], [psT.b])
                                    for qb in range(2):
                                        for cc in range(2):
                                            ph.op("act", lambda e, cc=cc, qb=qb: e.activation(
                                                out=ob.t[:, cc, qb * 128:(qb + 1) * 128], in_=psT.t[:, (qb * 2 + cc) * 128:(qb * 2 + cc + 1) * 128],
                                                func=AF.Copy, scale=subT.t[:, cc:cc + 1]), [psT.b, subT.b], [ob.b])
                                    ph.dma("sp", lambda e: e.dma_start(out=OTv[:, hg * 2:hg * 2 + 2, t0:t0 + 256], in_=ob.t[:]),
                                           [ob.b], [DB("OT", gi)], ob.b)
                            dq.append([1, fin_a])
                            dq.append([3, fin_b])
                    flush()
                ph.emit()

            with ExitStack() as st:
                ph = Phase(ctx)
                xt2 = [sb(st, "xt%d" % i, [128, KC, 512], F32) for i in range(2)]
                ot2 = [sb(st, "ot%d" % i, [128, KC, 512], BF16) for i in range(2)]
                wsl = [sb(st, "w%d" % i, [128, KC, 512], BF16) for i in range(4)]
                pss = [(psA.t[:, 0, :], bA[0]), (psA.t[:, 1, :], bA[1]), (psB.t[:, 0, :], bB[0]), (psB.t[:, 1, :], bB[1])]
                wi = 0; pi = 0
                for gi, (isp, t0, kv0, g) in enumerate(groups):
                    c = 1 if isp else 0
                    xt = xt2[gi % 2]; ot = ot2[gi % 2]
                    ph.dma("sp", lambda e, xt=xt, t0=t0: e.dma_start(out=xt.t[:], in_=XTv[:, :, t0:t0 + 512]), [DB("XT", gi)], [xt.b], xt.b)
                    ph.dma("sp", lambda e, ot=ot, t0=t0: e.dma_start(out=ot.t[:], in_=OTv[:, :, t0:t0 + 512]), [DB("OT", gi)], [ot.b], ot.b)
                    for pc in range(D // 512):
                        w = wsl[wi % 4]; wi += 1
                        ph.dma("pool", lambda e, w=w, pc=pc: e.dma_start(out=w.t[:], in_=Wo[:, :, pc * 512:(pc + 1) * 512]), [], [w.b], w.b)
                        for m in range(4):
                            f = pc * 4 + m
                            pm, pmb = pss[pi % 4]; pi += 1
                            for k in range(KC):
                                ph.op("pe", lambda e, w=w, m=m, k=k, pm=pm, ot=ot: e.matmul(
                                    pm, lhsT=w.t[:, k, m * 128:(m + 1) * 128], rhs=ot.t[:, k, :], start=(k == 0), stop=(k == KC - 1)),
                                    [w.b, ot.b], [pmb])
                            ph.op("dve", lambda e, pm=pm, xt=xt, f=f, c=c: e.scalar_tensor_tensor(
                                out=xt.t[:, f, :], in0=pm, scalar=mod.t[:, L, c, 2 * KC + f:2 * KC + f + 1], in1=xt.t[:, f, :],
                                op0=ALU.mult, op1=ALU.add), [pmb, mod.b, xt.b], [xt.b])
                    ph.dma("sp", lambda e, xt=xt, t0=t0: e.dma_start(out=X1Tv[:, :, t0:t0 + 512], in_=xt.t[:]), [xt.b], [DB("X1T", gi)], xt.b)
                ph.emit()

            with ExitStack() as st:
                ph = Phase(ctx)
                x1 = sb(st, "x1", [128, KC, 516], F32)
                sq = sb(st, "sq", [128, KC, 516], BF16)
                hT = sb(st, "hT", [128, KC, 516], BF16)
                rstdn = sb(st, "rstdn", [128, 516], F32); tmpn = sb(st, "tmpn", [128, 516], F32)
                aT = sb(st, "aT", [128, FC, 512], BF16)
                wsl = [sb(st, "w%d" % i, [128, KC, 512], BF16) for i in range(4)]
                cwT = sb(st, "cwT", [128, 4, FC], F32)
                for m0 in range(0, FC, 11):
                    m1 = min(FC, m0 + 11)
                    for q in range(3):
                        ph.dma("sp", lambda e, q=q, m0=m0, m1=m1: e.dma_start(
                            out=cwT.t[:, q, m0:m1], in_=cw[L * 3 + q, m0 * 128:m1 * 128].rearrange("(m p) -> p m", p=128),
                            allow_slow_non_contiguous=True), [], [cwT.b], cwT.b)
                    ph.dma("sp", lambda e, m0=m0, m1=m1: e.dma_start(
                        out=cwT.t[:, 3, m0:m1], in_=cbias[L, m0 * 128:m1 * 128].rearrange("(m p) -> p m", p=128),
                        allow_slow_non_contiguous=True), [], [cwT.b], cwT.b)
                tmpc = [sb(st, "tmpc%d" % i, [128, 2, 256], F32) for i in range(2)]
                sgl = [sb(st, "sgl%d" % i, [128, 2, 256], F32) for i in range(2)]
                x1v = x1.t[:].rearrange("p k (s t) -> p k s t", s=2)
                hTv = hT.t[:].rearrange("p k (s t) -> p k s t", s=2)
                aTv = aT.t[:].rearrange("p m (s t) -> p m s t", s=2)
                wi = 0; ci_ = 0
                kpieces = [(k0, min(k0 + 16, FC)) for k0 in range(0, FC, 16)]
                for gi, (isp, t0, kv0, g) in enumerate(groups):
                    c = 1 if isp else 0
                    for s in range(2):
                        ts_ = t0 + s * 256
                        if isp:
                            left = right = False
                        else:
                            left = ts_ > 0
                            right = ts_ + 256 < NS
                        lo = ts_ - (1 if left else 0)
                        n_ = 256 + (1 if left else 0) + (1 if right else 0)
                        off = s * 258 + (0 if left else 1)
                        if not left:
                            ph.op("pool", lambda e, s=s: e.memset(x1.t[:, :, s * 258:s * 258 + 1], 0.0), [], [x1.b])
                        if not right:
                            ph.op("pool", lambda e, s=s: e.memset(x1.t[:, :, s * 258 + 257:s * 258 + 258], 0.0), [], [x1.b])
                        rb = [DB("X1T", gi)]
                        if left and (ts_ % 512 == 0):
                            rb.append(DB("X1T", gi - 1))
                        if right and ((ts_ + 256) % 512 == 0):
                            rb.append(DB("X1T", gi + 1))
                        ph.dma("sp", lambda e, lo=lo, n_=n_, off=off: e.dma_start(out=x1.t[:, :, off:off + n_], in_=X1Tv[:, :, lo:lo + n_]),
                               rb, [x1.b], x1.b)
                    norm_mod(ph, x1, 516, [(0, 258), (258, 258)], L, c, 1, sq, hT, rstdn, tmpn, psC)
                    for s in range(2):
                        ts_ = t0 + s * 256
                        left = (not isp) and ts_ > 0
                        right = (not isp) and (ts_ + 256 < NS)
                        if not left:
                            ph.op("pool", lambda e, s=s: e.memset(hT.t[:, :, s * 258:s * 258 + 1], 0.0), [hT.b], [hT.b])
                        if not right:
                            ph.op("pool", lambda e, s=s: e.memset(hT.t[:, :, s * 258 + 257:s * 258 + 258], 0.0), [hT.b], [hT.b])
                    for pc in range(DFF // 512):
                        wg = wsl[wi % 4]; wv = wsl[(wi + 1) % 4]; wi += 2
                        ph.dma("pool", lambda e, wg=wg, pc=pc: e.dma_start(out=wg.t[:], in_=Wup[:, :, pc * 512:(pc + 1) * 512]), [], [wg.b], wg.b)
                        ph.dma("pool", lambda e, wv=wv, pc=pc: e.dma_start(out=wv.t[:], in_=Wup[:, :, DFF + pc * 512:DFF + (pc + 1) * 512]), [], [wv.b], wv.b)
                        for m in range(4):
                            fm = pc * 4 + m
                            for (w_, ps_, pbs) in ((wg, psA, bA), (wv, psB, bB)):
                                for s in range(2):
                                    for k in range(KC):
                                        ph.op("pe", lambda e, w_=w_, ps_=ps_, s=s, k=k, m=m: e.matmul(
                                            ps_.t[:, s, 0:258], lhsT=w_.t[:, k, m * 128:(m + 1) * 128], rhs=hTv[:, k, s, :],
                                            start=(k == 0), stop=(k == KC - 1)), [w_.b, hT.b], [pbs[s]])
                            tc_ = tmpc[ci_ % 2]; sg_ = sgl[ci_ % 2]; ci_ += 1
                            ph.op("act", lambda e, tc_=tc_, fm=fm: e.activation(
                                out=tc_.t[:], in_=psA.t[:, :, 1:257], func=AF.Identity, scale=cwT.t[:, 1, fm:fm + 1], bias=cwT.t[:, 3, fm:fm + 1]),
                                [bA[0], bA[1], cwT.b], [tc_.b])
                            ph.op("dve", lambda e, tc_=tc_, fm=fm: e.scalar_tensor_tensor(
                                out=tc_.t[:], in0=psA.t[:, :, 0:256], scalar=cwT.t[:, 0, fm:fm + 1], in1=tc_.t[:], op0=ALU.mult, op1=ALU.add),
                                [bA[0], bA[1], cwT.b, tc_.b], [tc_.b])
                            ph.op("dve", lambda e, tc_=tc_, fm=fm: e.scalar_tensor_tensor(
                                out=tc_.t[:], in0=psA.t[:, :, 2:258], scalar=cwT.t[:, 2, fm:fm + 1], in1=tc_.t[:], op0=ALU.mult, op1=ALU.add),
                                [bA[0], bA[1], cwT.b, tc_.b], [tc_.b])
                            ph.op("act", lambda e, tc_=tc_, sg_=sg_: e.activation(out=sg_.t[:], in_=tc_.t[:], func=AF.Silu), [tc_.b], [sg_.b])
                            ph.op("dve", lambda e, sg_=sg_, fm=fm: e.tensor_tensor(
                                out=aTv[:, fm, :, :], in0=sg_.t[:], in1=psB.t[:, :, 1:257], op=ALU.mult), [sg_.b, bB[0], bB[1]], [aT.b])
                    pso = [(psA.t[:, 0, :], bA[0]), (psA.t[:, 1, :], bA[1]), (psB.t[:, 0, :], bB[0]), (psB.t[:, 1, :], bB[1])]
                    for pb_ in range(D // 512):
                        for (k0, k1) in kpieces:
                            w = wsl[wi % 4]; wi += 1
                            ph.dma("pool", lambda e, w=w, k0=k0, k1=k1, pb_=pb_: e.dma_start(
                                out=w.t[:, 0:k1 - k0, :], in_=Wdn[:, k0:k1, pb_ * 512:(pb_ + 1) * 512]), [], [w.b], w.b)
                            for o_ in range(4):
                                po, pob = pso[o_]
                                for k in range(k0, k1):
                                    ph.op("pe", lambda e, w=w, po=po, k=k, k0=k0, o_=o_: e.matmul(
                                        po, lhsT=w.t[:, k - k0, o_ * 128:(o_ + 1) * 128], rhs=aT.t[:, k, :],
                                        start=(k == 0), stop=(k == FC - 1)), [w.b, aT.b], [pob])
                        for o_ in range(4):
                            f = pb_ * 4 + o_
                            po, pob = pso[o_]
                            ph.op("dve", lambda e, po=po, f=f, c=c: e.scalar_tensor_tensor(
                                out=x1v[:, f, :, 1:257], in0=po.rearrange("p (s t) -> p s t", s=2), scalar=mod.t[:, L, c, 5 * KC + f:5 * KC + f + 1],
                                in1=x1v[:, f, :, 1:257], op0=ALU.mult, op1=ALU.add), [pob, mod.b, x1.b], [x1.b])
                    for s in range(2):
                        ph.dma("sp", lambda e, s=s, t0=t0: e.dma_start(out=XTv[:, :, t0 + s * 256:t0 + (s + 1) * 256], in_=x1v[:, :, s, 1:257]),
                               [x1.b], [DB("XT", gi)], x1.b)
                ph.emit()

        with ExitStack() as st:
            ph = Phase(ctx)
            xin = [sb(st, "fx%d" % i, [128, KC, 512], F32) for i in range(2)]
            yst = [sb(st, "yst%d" % i, [128, D], F32) for i in range(2)]
            pss = [(psA.t[:, 0, :], bA[0]), (psA.t[:, 1, :], bA[1]), (psB.t[:, 0, :], bB[0]), (psB.t[:, 1, :], bB[1]),
                   (psC.t[:], psC.b), (psD.t[:], psD.b), (psE.t[:], psE.b)]
            pi = 0; yi = 0
            for gi, (isp, t0, kv0, g) in enumerate(groups):
                xi = xin[gi % 2]
                dst = yp if isp else ys
                r0 = t0 - NS if isp else t0
                ph.dma("sp", lambda e, xi=xi, t0=t0: e.dma_start(out=xi.t[:], in_=XTv[:, :, t0:t0 + 512]), [DB("XT", gi)], [xi.b], xi.b)
                for tt in range(4):
                    yo = yst[yi % 2]; yi += 1
                    for k4 in range(KC // 4):
                        pt, pb = pss[pi % 7]; pi += 1
                        for q in range(4):
                            k = k4 * 4 + q
                            ph.op("pe", lambda e, pt=pt, xi=xi, k=k, q=q, tt=tt: e.transpose(
                                pt[:, q * 128:(q + 1) * 128], in_=xi.t[:, k, tt * 128:(tt + 1) * 128], identity=identf.t[:]),
                                [xi.b, identf.b], [pb])
                        if pi % 2:
                            ph.op("act", lambda e, pt=pt, yo=yo, k4=k4: e.activation(out=yo.t[:, k4 * 512:(k4 + 1) * 512], in_=pt, func=AF.Copy), [pb], [yo.b])
                        else:
                            ph.op("dve", lambda e, pt=pt, yo=yo, k4=k4: e.tensor_copy(out=yo.t[:, k4 * 512:(k4 + 1) * 512], in_=pt), [pb], [yo.b])
                    ph.dma("sp", lambda e, yo=yo, dst=dst, r=r0 + tt * 128: e.dma_start(out=dst[r:r + 128, :], in_=yo.t[:]), [yo.b], [], yo.b)
            ph.emit()
        final_wait(ctx)
    return nc


def _rope_tables(NS, GRID_W):
    half = 64
    t = np.arange(NS)
    row = (t // GRID_W).astype(np.float32)
    col = (t % GRID_W).astype(np.float32)
    inv = (10000.0 ** (-np.arange(0, half, 2, dtype=np.float32) / half)).astype(np.float32)
    ang = np.concatenate([row[:, None] * inv, col[:, None] * inv], axis=-1).astype(np.float32)
    cos, sin = np.cos(ang), np.sin(ang)
    C = np.zeros((128, NS), np.float32); S = np.zeros((128, NS), np.float32)
    for d in range(128):
        hf = d // 64; r = d % 64
        fi = hf * 32 + (r % 32)
        C[d] = cos[:, fi]
        S[d] = -sin[:, fi] if r < 32 else sin[:, fi]
    return C, S


def _consts():
    c = np.zeros((128, 640), np.float32)
    c[:, 0:128] = np.eye(128, dtype=np.float32)
    for dp in range(128):
        r = dp % 64
        partner = dp + 32 if r < 32 else dp - 32
        c[partner, 128 + dp] = 1.0
    tk = np.arange(128)[:, None]; tq = np.arange(128)[None, :]
    c[:, 256:384] = (tq <= tk).astype(np.float32)
    c[:, 384:512] = (tk <= tq).astype(np.float32)
    c[:, 512:640] = 1.0
    return c


def make_in_maps(cfg, inputs, ncores):
    D, DEPTH, NS, PAST, NPS, SEQ = cfg["D"], cfg["DEPTH"], cfg["NS"], cfg["PAST"], cfg["NPS"], cfg["SEQ"]
    NA, NB = (DEPTH + 1) // 2, DEPTH // 2
    f = lambda a: np.ascontiguousarray(np.asarray(a, dtype=np.float32))
    C, S = _rope_tables(NS, cfg["GRID_W"])
    shared = {
        "ada_w": f(inputs["ada_w"]).reshape(DEPTH * D, 6 * D), "ada_b": f(inputs["ada_b"]),
        "norm1_g": f(inputs["norm1_g"]), "norm2_g": f(inputs["norm2_g"]),
        "a_w_qkv": f(inputs["a_w_qkv"]).reshape(NA * D, -1), "a_q_norm": f(inputs["a_q_norm"]), "a_k_norm": f(inputs["a_k_norm"]),
        "a_sink": f(inputs["a_sink"]), "a_w_o": f(inputs["a_w_o"]).reshape(NA * D, D),
        "b_w_qkv": f(inputs["b_w_qkv"]).reshape(NB * D, 3 * D), "b_q_norm": f(inputs["b_q_norm"]), "b_k_norm": f(inputs["b_k_norm"]),
        "b_lambda_q1": f(inputs["b_lambda_q1"]), "b_lambda_k1": f(inputs["b_lambda_k1"]),
        "b_lambda_q2": f(inputs["b_lambda_q2"]), "b_lambda_k2": f(inputs["b_lambda_k2"]),
        "b_subln": f(inputs["b_subln"]), "b_w_o": f(inputs["b_w_o"]).reshape(NB * D, D),
        "ffn_w_up": f(inputs["ffn_w_up"]).reshape(DEPTH * D, -1), "ffn_conv_w": f(inputs["ffn_conv_w"]).reshape(DEPTH * 3, -1),
        "ffn_conv_b": f(inputs["ffn_conv_b"]), "ffn_w_down": f(inputs["ffn_w_down"]).reshape(-1, D),
        "rope_cos": C, "rope_sin": S, "consts": _consts(),
    }
    xs = f(inputs["x_sample"]); xp = f(inputs["x_prompt"])
    cak = f(inputs["cache_a_k"]); cav = f(inputs["cache_a_v"]); cbk = f(inputs["cache_b_k"]); cbv = f(inputs["cache_b_v"])
    cc = f(inputs["c"]); cctx = f(inputs["c_ctx"])
    maps = []
    for i in range(ncores):
        m = dict(shared)
        m["xs"] = xs[i]
        m["xp"] = xp[i * NPS:(i + 1) * NPS].reshape(NPS * SEQ, D)
        m["cak"] = cak[i].reshape(NA * PAST, -1); m["cav"] = cav[i].reshape(NA * PAST, -1)
        m["cbk"] = cbk[i].reshape(NB * PAST, -1); m["cbv"] = cbv[i].reshape(NB * PAST, -1)
        m["cond"] = np.ascontiguousarray(np.stack([cc[i], cctx], axis=0))
        maps.append(m)
    return maps


def gather(cfg, results, ncores):
    D, DEPTH, NS, NPS, SEQ = cfg["D"], cfg["DEPTH"], cfg["NS"], cfg["NPS"], cfg["SEQ"]
    NA, NB = (DEPTH + 1) // 2, DEPTH // 2
    ys = np.stack([results[i]["ys"] for i in range(ncores)], 0)
    yp = np.concatenate([results[i]["yp"].reshape(NPS, SEQ, D) for i in range(ncores)], 0)
    sak = np.concatenate([results[i]["sak"].reshape(NPS, NA, SEQ, cfg["AKV"], 128) for i in range(ncores)], 0)
    sav = np.concatenate([results[i]["sav"].reshape(NPS, NA, SEQ, cfg["AKV"], 128) for i in range(ncores)], 0)
    sbk = np.concatenate([results[i]["sbk"].reshape(NPS, NB, SEQ, cfg["BH"], 2, 128) for i in range(ncores)], 0)
    sbv = np.concatenate([results[i]["sbv"].reshape(NPS, NB, SEQ, cfg["BH"], 256) for i in range(ncores)], 0)
    return (yp.astype(np.float32), ys.astype(np.float32), sak.astype(np.float32), sav.astype(np.float32),
            sbk.astype(np.float32), sbv.astype(np.float32))


def run(cfg, inputs, ncores=8):
    nc = build(cfg)
    maps = make_in_maps(cfg, inputs, ncores)
    res = run_bass_kernel_spmd(nc, maps, core_ids=list(range(ncores)))
    return gather(cfg, res.results, ncores)


def kernel(**inputs):
    return run(CFG_FULL, inputs, 8)
```

```python
import math
from contextlib import ExitStack
import numpy as np
import concourse.bass as bass
import concourse.mybir as mybir
from concourse.bass_utils import run_bass_kernel_spmd

F32 = mybir.dt.float32
BF16 = mybir.dt.bfloat16
AF = mybir.ActivationFunctionType
ALU = mybir.AluOpType
ENGS = ("pe", "act", "dve", "pool", "sp")
EPS = 1e-6

CFG_FULL = dict(D=2048, DEPTH=4, NS=4096, PAST=512, NPS=4, SEQ=256, AH=16, AKV=4, BH=8,
                DFF=5632, GRID_W=64)


class Buf:
    __slots__ = ("wc", "wd", "rc", "rd", "prc", "prd", "was_read", "sem", "lastdma")

    def __init__(self):
        self.prc = {}
        self.prd = []
        self.wc = {}
        self.wd = []
        self.rc = {}
        self.rd = []
        self.was_read = False
        self.sem = None
        self.lastdma = None


class DSem:
    __slots__ = ("h", "count")

    def __init__(self, h):
        self.h = h
        self.count = 0


class Op:
    __slots__ = ("eng", "fn", "deps", "is_dma", "sem", "dval", "signal", "count")

    def __init__(self, eng, fn, is_dma):
        self.eng = eng
        self.fn = fn
        self.deps = []
        self.is_dma = is_dma
        self.sem = None
        self.dval = 0
        self.signal = False
        self.count = 0


class Ctx:
    def __init__(self, nc, stack, ndsem=56):
        self.nc = nc
        self.esem = {e: stack.enter_context(nc.semaphore("es_" + e)) for e in ENGS}
        self.ecount = {e: 0 for e in ENGS}
        self.dsems = [DSem(stack.enter_context(nc.semaphore("ds%d" % i))) for i in range(ndsem)]
        self.sw = self.dsems[:14]
        self.nph = 0
        self.stop = None
        self.sub = None
        self.hw = self.dsems[14:]


class Phase:
    def __init__(self, ctx):
        self.ctx = ctx
        self.ops = {e: [] for e in ENGS}
        self.nsw = 0
        self.nhw = 0
        self.homesem = {}
        self.nrec = 0
        self.limit = ctx.sub if (ctx.stop is not None and ctx.nph + 1 == ctx.stop) else None

    def _rec(self, o, reads, writes):
        eng, is_dma = o.eng, o.is_dma
        deps = o.deps
        for r in reads:
            for e2, d in r.wc.items():
                if not (eng == "pe" and e2 == "pe" and not is_dma):
                    deps.append(d)
            deps.extend(r.wd)
        for w in writes:
            if w.was_read:
                w.prc = w.rc
                w.prd = w.rd
                w.wc = {}
                w.wd = []
                w.rc = {}
                w.rd = []
                w.was_read = False
            for e2, d in w.prc.items():
                if is_dma or e2 != eng:
                    deps.append(d)
            deps.extend(w.prd)
        for r in reads:
            r.was_read = True
            if is_dma:
                r.rd.append(o)
            else:
                r.rc[eng] = o
        for w in writes:
            if is_dma:
                w.wd.append(o)
            else:
                w.wc[eng] = o
        self.ops[eng].append(o)
        return o

    def op(self, eng, fn, reads=(), writes=()):
        self.nrec += 1
        if self.limit is not None and self.nrec > self.limit:
            return None
        return self._rec(Op(eng, fn, False), reads, writes)

    def dma(self, queue, fn, reads, writes, home):
        self.nrec += 1
        if self.limit is not None and self.nrec > self.limit:
            return None
        o = Op(queue, fn, True)
        key = (id(home), queue == "pool")
        if key not in self.homesem:
            if queue == "pool":
                assert self.nsw < len(self.ctx.sw)
                self.homesem[key] = self.ctx.sw[self.nsw]
                self.nsw += 1
            else:
                assert self.nhw < len(self.ctx.hw)
                self.homesem[key] = self.ctx.hw[self.nhw]
                self.nhw += 1
            home.lastdma = None
        home.sem = self.homesem[key]
        if home.lastdma is not None:
            o.deps.append(home.lastdma)
        home.lastdma = o
        o.sem = home.sem
        o.sem.count += 16
        o.dval = o.sem.count
        return self._rec(o, reads, writes)

    def emit(self):
        ctx = self.ctx
        nc = ctx.nc
        ctx.nph += 1
        if ctx.stop is not None and ctx.nph > ctx.stop:
            for e in ENGS:
                for o in self.ops[e]:
                    if o.is_dma:
                        o.sem.count -= 16
            return
        start_counts = dict(ctx.ecount)
        start_d = [(s, s.count) for s in ctx.dsems]
        for e in ENGS:
            lst = self.ops[e]
            for o in lst:
                for d in o.deps:
                    if not d.is_dma:
                        d.signal = True
            for o in reversed(lst):
                if not o.is_dma:
                    o.signal = True
                    break
        for e in ENGS:
            c = ctx.ecount[e]
            for o in self.ops[e]:
                if o.signal and not o.is_dma:
                    c += 1
                    o.count = c
            ctx.ecount[e] = c
        pre_d = {}
        for e in ENGS:
            for o in self.ops[e]:
                if o.is_dma and id(o.sem) not in pre_d:
                    pre_d[id(o.sem)] = o.dval - 16
        esem = ctx.esem

        def run(en, e):
            known = {}
            for e2 in ENGS:
                if start_counts[e2] > 0:
                    e.wait_ge(esem[e2], start_counts[e2])
                    known[id(esem[e2])] = start_counts[e2]
            for s, cnt in start_d:
                v = pre_d.get(id(s), cnt)
                if v > 0:
                    e.wait_ge(s.h, v)
                    known[id(s.h)] = v
            for o in self.ops[en]:
                for d in o.deps:
                    if d.is_dma:
                        sem, val = d.sem.h, d.dval
                    else:
                        sem, val = esem[d.eng], d.count
                    k = id(sem)
                    if known.get(k, 0) < val:
                        e.wait_ge(sem, val)
                        known[k] = val
                ins = o.fn(e)
                if o.is_dma:
                    ins.then_inc(o.sem.h, 16)
                elif o.signal:
                    ins.then_inc(esem[en], 1)

        with nc.Block() as block:
            @block.sync
            def _(e):
                run("sp", e)

            @block.tensor
            def _(e):
                run("pe", e)

            @block.scalar
            def _(e):
                run("act", e)

            @block.vector
            def _(e):
                run("dve", e)

            @block.gpsimd
            def _(e):
                run("pool", e)


def final_wait(ctx):
    nc = ctx.nc
    with nc.Block() as block:
        @block.sync
        def _(e):
            for e2 in ENGS:
                if ctx.ecount[e2] > 0:
                    e.wait_ge(ctx.esem[e2], ctx.ecount[e2])
            for s in ctx.dsems:
                if s.count > 0:
                    e.wait_ge(s.h, s.count)


class T:
    __slots__ = ("t", "b")

    def __init__(self, t):
        self.t = t
        self.b = Buf()


def build(cfg):
    D, DEPTH, NS, PAST = cfg["D"], cfg["DEPTH"], cfg["NS"], cfg["PAST"]
    NPS, SEQ, AH, AKV, BH, DFF = cfg["NPS"], cfg["SEQ"], cfg["AH"], cfg["AKV"], cfg["BH"], cfg["DFF"]
    KC = D // 128
    FC = DFF // 128
    NP = NPS * SEQ
    NTOK = NS + NP
    NKV = NS + PAST + NP
    NCH = NKV // 128
    NA, NB = (DEPTH + 1) // 2, DEPTH // 2
    QA = (AH + 2 * AKV) * 128
    AG = AH // AKV
    assert SEQ == 256 and AG == 4 and NS % 512 == 0 and NP % 512 == 0 and D % 512 == 0 and DFF % 512 == 0
    QCH = max(AH, 2 * BH)
    VDA, VDB = AKV * 128, BH * 256
    VD = max(VDA, VDB)

    nc = bass.Bass("TRN2", target_bir_lowering=False)

    def din(name, shape):
        return nc.dram_tensor(name, list(shape), F32, kind="ExternalInput").ap()

    def dout(name, shape):
        return nc.dram_tensor(name, list(shape), F32, kind="ExternalOutput").ap()

    xs = din("xs", [NS, D]); xp = din("xp", [NP, D])
    cak = din("cak", [NA * PAST, VDA]); cav = din("cav", [NA * PAST, VDA])
    cbk = din("cbk", [max(NB, 1) * PAST, VDB]); cbv = din("cbv", [max(NB, 1) * PAST, VDB])
    cond = din("cond", [2, D])
    ada_w = din("ada_w", [DEPTH * D, 6 * D]); ada_b = din("ada_b", [DEPTH, 6 * D])
    n1g = din("norm1_g", [DEPTH, D]); n2g = din("norm2_g", [DEPTH, D])
    a_wqkv = din("a_w_qkv", [NA * D, QA]); a_qn = din("a_q_norm", [NA, 128]); a_kn = din("a_k_norm", [NA, 128])
    a_sink = din("a_sink", [NA, AH]); a_wo = din("a_w_o", [NA * D, D])
    b_wqkv = din("b_w_qkv", [max(NB, 1) * D, 3 * D]); b_qn = din("b_q_norm", [max(NB, 1), 128])
    b_kn = din("b_k_norm", [max(NB, 1), 128])
    b_l = [din("b_lambda_" + n, [max(NB, 1), 128]) for n in ("q1", "k1", "q2", "k2")]
    b_sub = din("b_subln", [max(NB, 1), 256]); b_wo = din("b_w_o", [max(NB, 1) * D, D])
    w_up = din("ffn_w_up", [DEPTH * D, 2 * DFF]); cw = din("ffn_conv_w", [DEPTH * 3, DFF])
    cbias = din("ffn_conv_b", [DEPTH, DFF]); w_dn = din("ffn_w_down", [DEPTH * DFF, D])
    ropec = din("rope_cos", [128, NS]); ropes = din("rope_sin", [128, NS])
    consts = din("consts", [128, 640])

    ys = dout("ys", [NS, D]); yp = dout("yp", [NP, D])
    sak = dout("sak", [NPS * NA * SEQ, VDA]); sav = dout("sav", [NPS * NA * SEQ, VDA])
    sbk = dout("sbk", [NPS * max(NB, 1) * SEQ, VDB]); sbv = dout("sbv", [NPS * max(NB, 1) * SEQ, VDB])

    XT = nc.dram_tensor("XT", [KC, 128, NTOK], F32).ap()
    X1T = nc.dram_tensor("X1T", [KC, 128, NTOK], F32).ap()
    QT = nc.dram_tensor("QT", [QCH, 128, NTOK], BF16).ap()
    KT = nc.dram_tensor("KT", [QCH, 128, NKV], BF16).ap()
    VS = nc.dram_tensor("VS", [NKV, VD], BF16).ap()
    OT = nc.dram_tensor("OT", [KC, 128, NTOK], BF16).ap()
    XTv = XT.rearrange("k p t -> p k t"); X1Tv = X1T.rearrange("k p t -> p k t")
    QTv = QT.rearrange("k p t -> p k t"); KTv = KT.rearrange("k p t -> p k t")
    OTv = OT.rearrange("k p t -> p k t")

    groups = [(False, 512 * g, 512 * g, g) for g in range(NS // 512)]
    groups += [(True, NS + 512 * g, NS + PAST + 512 * g, g) for g in range(NP // 512)]
    dbuf = {}

    def DB(name, key):
        k = (name, key)
        if k not in dbuf:
            dbuf[k] = Buf()
        return dbuf[k]

    with ExitStack() as gst:
        ctx = Ctx(nc, gst)
        ctx.stop = cfg.get("STOP")
        ctx.sub = cfg.get("SUB")

        nmc = [0]

        def sb(st, name, shape, dt):
            nmc[0] += 1
            return T(st.enter_context(nc.sbuf_tensor("%s_%d" % (name, nmc[0]), list(shape), dt)))

        cst = sb(gst, "cst", [128, 640], BF16)
        identb = cst.t[:, 0:128]; rotm = cst.t[:, 128:256]
        trip = cst.t[:, 256:384]; trin = cst.t[:, 384:512]; onesb = cst.t[:, 512:640]
        identf = sb(gst, "identf", [128, 128], F32)
        mod = sb(gst, "mod", [128, DEPTH, 2, 6 * KC], F32)
        Am = sb(gst, "Am", [128, DEPTH, 2, 2, KC], F32)
        psA = T(gst.enter_context(nc.psum_tensor("psA", [128, 2, 512], F32)))
        psB = T(gst.enter_context(nc.psum_tensor("psB", [128, 2, 512], F32)))
        psC = T(gst.enter_context(nc.psum_tensor("psC", [128, 512], F32)))
        psD = T(gst.enter_context(nc.psum_tensor("psD", [128, 512], F32)))
        psE = T(gst.enter_context(nc.psum_tensor("psE", [128, 512], F32)))
        psT = T(gst.enter_context(nc.psum_tensor("psT", [128, 1024], BF16)))
        bA = [Buf(), Buf()]; bB = [Buf(), Buf()]

        def newbufs():
            for t in (psA, psB, psC, psD, psE, psT, cst, identf, mod, Am):
                pass

        with ExitStack() as st:
            ph = Phase(ctx)
            ph.dma("pool", lambda e: e.dma_start(out=cst.t[:], in_=consts), [], [cst.b], cst.b)
            ph.dma("sp", lambda e: e.dma_start(out=identf.t[:], in_=consts[:, 0:128]), [], [identf.b], identf.b)
            cT_ = sb(st, "condT", [128, KC, 2], F32)
            scT = sb(st, "scT", [128, KC, 2], BF16)
            for c_ in range(2):
                ph.dma("sp", lambda e, c_=c_: e.dma_start(out=cT_.t[:, :, c_], in_=cond[c_, :].rearrange("(k p) -> p k", p=128),
                                                          allow_slow_non_contiguous=True), [], [cT_.b], cT_.b)
            ph.op("act", lambda e: e.activation(out=scT.t[:], in_=cT_.t[:], func=AF.Silu), [cT_.b], [scT.b])
            wsl = [sb(st, "mw%d" % i, [128, KC, 512], BF16) for i in range(4)]
            abT = sb(st, "abT", [128, 6 * KC], F32)
            ngT = sb(st, "ngT", [128, 2, KC], F32)
            tmpm = sb(st, "tmpm", [128, 2, KC], F32)
            wi = 0
            for L in range(DEPTH):
                for j6 in range(6):
                    ph.dma("sp", lambda e, L=L, j6=j6: e.dma_start(
                        out=abT.t[:, j6 * KC:(j6 + 1) * KC], in_=ada_b[L, j6 * D:(j6 + 1) * D].rearrange("(j p) -> p j", p=128),
                        allow_slow_non_contiguous=True), [], [abT.b], abT.b)
                ph.dma("sp", lambda e, L=L: e.dma_start(out=ngT.t[:, 0, :], in_=n1g[L, :].rearrange("(j p) -> p j", p=128),
                                                        allow_slow_non_contiguous=True), [], [ngT.b], ngT.b)
                ph.dma("sp", lambda e, L=L: e.dma_start(out=ngT.t[:, 1, :], in_=n2g[L, :].rearrange("(j p) -> p j", p=128),
                                                        allow_slow_non_contiguous=True), [], [ngT.b], ngT.b)
                Wl = ada_w[L * D:(L + 1) * D, :].rearrange("(k p) n -> p k n", p=128)
                for cb_ in range(6 * D // 512):
                    w = wsl[wi % 4]; wi += 1
                    ph.dma("pool", lambda e, w=w, cb_=cb_, Wl=Wl: e.dma_start(out=w.t[:], in_=Wl[:, :, cb_ * 512:(cb_ + 1) * 512]),
                           [], [w.b], w.b)
                    for m in range(4):
                        col = cb_ * 4 + m
                        for k in range(KC):
                            ph.op("pe", lambda e, w=w, m=m, k=k, col=col: e.matmul(
                                psC.t[:, col * 2:col * 2 + 2], lhsT=w.t[:, k, m * 128:(m + 1) * 128], rhs=scT.t[:, k, :],
                                start=(k == 0), stop=(k == KC - 1)), [w.b, scT.b], [psC.b])
                for c in range(2):
                    ph.op("dve", lambda e, L=L, c=c: e.tensor_tensor(
                        out=mod.t[:, L, c, :], in0=psC.t[:, 0:12 * KC].rearrange("p (j c) -> p j c", c=2)[:, :, c],
                        in1=abT.t[:], op=ALU.add), [psC.b, abT.b], [mod.b])
                    for n in range(2):
                        ph.op("dve", lambda e, L=L, c=c, n=n: e.tensor_scalar(
                            out=tmpm.t[:, n, :], in0=mod.t[:, L, c, (3 * n + 1) * KC:(3 * n + 2) * KC], scalar1=1.0, scalar2=None,
                            op0=ALU.add), [mod.b], [tmpm.b])
                        ph.op("dve", lambda e, L=L, c=c, n=n: e.tensor_tensor(
                            out=Am.t[:, L, c, n, :], in0=tmpm.t[:, n, :], in1=ngT.t[:, n, :], op=ALU.mult),
                            [tmpm.b, ngT.b], [Am.b])
            ph.emit()

        with ExitStack() as st:
            ph = Phase(ctx)
            xin = [sb(st, "xin%d" % i, [128, D], F32) for i in range(2)]
            xst = [sb(st, "xst%d" % i, [128, KC, 512], F32) for i in range(2)]
            pss = [(psA.t[:, 0, :], bA[0]), (psA.t[:, 1, :], bA[1]), (psB.t[:, 0, :], bB[0]), (psB.t[:, 1, :], bB[1]),
                   (psC.t[:], psC.b), (psD.t[:], psD.b), (psE.t[:], psE.b)]
            pi = 0; ti = 0
            for gi, (isp, t0, kv0, g) in enumerate(groups):
                src = xp if isp else xs
                r0 = t0 - NS if isp else t0
                xo = xst[gi % 2]
                for tt in range(4):
                    xi = xin[ti % 2]; ti += 1
                    ph.dma("sp", lambda e, xi=xi, src=src, r=r0 + tt * 128: e.dma_start(out=xi.t[:], in_=src[r:r + 128, :]),
                           [], [xi.b], xi.b)
                    for k4 in range(KC // 4):
                        pt, pb = pss[pi % 7]; pi += 1
                        for q in range(4):
                            k = k4 * 4 + q
                            ph.op("pe", lambda e, pt=pt, xi=xi, k=k, q=q: e.transpose(
                                pt[:, q * 128:(q + 1) * 128], in_=xi.t[:, k * 128:(k + 1) * 128], identity=identf.t[:]),
                                [xi.b, identf.b], [pb])
                        eng = "act" if (pi % 2) else "dve"
                        if eng == "act":
                            ph.op("act", lambda e, pt=pt, xo=xo, k4=k4, tt=tt: e.activation(
                                out=xo.t[:, k4 * 4:k4 * 4 + 4, tt * 128:(tt + 1) * 128],
                                in_=pt.rearrange("p (q t) -> p q t", q=4), func=AF.Copy), [pb], [xo.b])
                        else:
                            ph.op("dve", lambda e, pt=pt, xo=xo, k4=k4, tt=tt: e.tensor_copy(
                                out=xo.t[:, k4 * 4:k4 * 4 + 4, tt * 128:(tt + 1) * 128],
                                in_=pt.rearrange("p (q t) -> p q t", q=4)), [pb], [xo.b])
                ph.dma("sp", lambda e, xo=xo, t0=t0: e.dma_start(out=XTv[:, :, t0:t0 + 512], in_=xo.t[:]),
                       [xo.b], [DB("XT", gi)], xo.b)
            ph.emit()

        def norm_mod(ph, x, W, segs, L, c, n, sq, hT, rstd, tmpf, psn):
            ph.op("act", lambda e: e.activation(out=sq.t[:, :, 0:W], in_=x.t[:, :, 0:W], func=AF.Square), [x.b], [sq.b])
            for (c0, n_) in segs:
                for k in range(KC):
                    ph.op("pe", lambda e, k=k, c0=c0, n_=n_: e.matmul(
                        psn.t[:, 0:n_], lhsT=onesb, rhs=sq.t[:, k, c0:c0 + n_], start=(k == 0), stop=(k == KC - 1)),
                        [sq.b, cst.b], [psn.b])
                ph.op("act", lambda e, c0=c0, n_=n_: e.activation(
                    out=tmpf.t[:, c0:c0 + n_], in_=psn.t[:, 0:n_], func=AF.Sqrt, scale=1.0 / D, bias=EPS), [psn.b], [tmpf.b])
            ph.op("dve", lambda e: e.reciprocal(out=rstd.t[:, 0:W], in_=tmpf.t[:, 0:W]), [tmpf.b], [rstd.b])
            for k in range(KC):
                ph.op("dve", lambda e, k=k: e.scalar_tensor_tensor(
                    out=hT.t[:, k, 0:W], in0=x.t[:, k, 0:W], scalar=Am.t[:, L, c, n, k:k + 1], in1=rstd.t[:, 0:W],
                    op0=ALU.mult, op1=ALU.mult), [x.b, Am.b, rstd.b], [hT.b])
            for k in range(KC):
                ph.op("act", lambda e, k=k: e.activation(
                    out=hT.t[:, k, 0:W], in_=hT.t[:, k, 0:W], func=AF.Identity,
                    bias=mod.t[:, L, c, 3 * n * KC + k:3 * n * KC + k + 1], scale=1.0), [hT.b, mod.b], [hT.b])

        for L in range(DEPTH):
            isA = (L % 2 == 0)
            j = L // 2
            lam_init = 0.8 - 0.6 * math.exp(-0.3 * L)
            if isA:
                Wqkv = a_wqkv[j * D:(j + 1) * D, :].rearrange("(k p) n -> p k n", p=128)
                Wo = a_wo[j * D:(j + 1) * D, :].rearrange("(k p) n -> p k n", p=128)
                nq, nk, nv = AH, AKV, AKV
                qn_ap, kn_ap = a_qn, a_kn
                ck, cv = cak[j * PAST:(j + 1) * PAST, :], cav[j * PAST:(j + 1) * PAST, :]
                sk, sv = sak, sav
                VDl = VDA
                NLs = NA
            else:
                Wqkv = b_wqkv[j * D:(j + 1) * D, :].rearrange("(k p) n -> p k n", p=128)
                Wo = b_wo[j * D:(j + 1) * D, :].rearrange("(k p) n -> p k n", p=128)
                nq, nk, nv = 2 * BH, 2 * BH, 2 * BH
                qn_ap, kn_ap = b_qn, b_kn
                ck, cv = cbk[j * PAST:(j + 1) * PAST, :], cbv[j * PAST:(j + 1) * PAST, :]
                sk, sv = sbk, sbv
                VDl = VDB
                NLs = NB
            nchunks = nq + nk + nv
            Wup = w_up[L * D:(L + 1) * D, :].rearrange("(k p) n -> p k n", p=128)
            Wdn = w_dn[L * DFF:(L + 1) * DFF, :].rearrange("(m p) f -> p m f", p=128)

            with ExitStack() as st:
                ph = Phase(ctx)
                gqk = sb(st, "gqk", [128, 2], F32)
                ph.dma("sp", lambda e: e.dma_start(out=gqk.t[:, 0:1], in_=qn_ap[j, :].rearrange("(p o) -> p o", o=1)), [], [gqk.b], gqk.b)
                ph.dma("sp", lambda e: e.dma_start(out=gqk.t[:, 1:2], in_=kn_ap[j, :].rearrange("(p o) -> p o", o=1)), [], [gqk.b], gqk.b)
                gs = sb(st, "gs", [128, 2], F32)
                ph.op("act", lambda e: e.mul(out=gs.t[:, 0:1], in_=gqk.t[:, 0:1], mul=128.0 ** -0.5), [gqk.b], [gs.b])
                ph.op("act", lambda e: e.copy(out=gs.t[:, 1:2], in_=gqk.t[:, 1:2]), [gqk.b], [gs.b])
                ph.dma("pool", lambda e: e.dma_start(out=VS[NS:NS + PAST, 0:VDl], in_=cv), [], [DB("VS", "ctx")], DB("VS", "ctx"))
                ckin = [sb(st, "ckin%d" % i, [128, VDl], F32) for i in range(2)]
                kst_c = [sb(st, "kstc%d" % i, [128, 4, 128], BF16) for i in range(2)]
                kci = 0
                for cch in range(PAST // 128):
                    ci_ = ckin[cch % 2]
                    ph.dma("sp", lambda e, ci_=ci_, cch=cch: e.dma_start(out=ci_.t[:], in_=ck[cch * 128:(cch + 1) * 128, :]),
                           [], [ci_.b], ci_.b)
                    for h4 in range((nk + 3) // 4):
                        nh = min(4, nk - h4 * 4)
                        for q in range(nh):
                            h = h4 * 4 + q
                            ph.op("pe", lambda e, ci_=ci_, h=h, q=q: e.transpose(
                                psD.t[:, q * 128:(q + 1) * 128], in_=ci_.t[:, h * 128:(h + 1) * 128], identity=identf.t[:]),
                                [ci_.b, identf.b], [psD.b])
                        ks_ = kst_c[kci % 2]; kci += 1
                        ph.op("dve", lambda e, ks_=ks_, nh=nh: e.tensor_copy(
                            out=ks_.t[:, 0:nh, :], in_=psD.t[:, 0:nh * 128].rearrange("p (q t) -> p q t", q=nh)), [psD.b], [ks_.b])
                        ph.dma("sp", lambda e, ks_=ks_, h4=h4, cch=cch, nh=nh: e.dma_start(
                            out=KTv[:, h4 * 4:h4 * 4 + nh, NS + cch * 128:NS + (cch + 1) * 128], in_=ks_.t[:, 0:nh, :]),
                            [ks_.b], [DB("KT", "ctx")], ks_.b)

                xt = sb(st, "xt", [128, KC, 512], F32)
                sq = sb(st, "sq", [128, KC, 512], BF16)
                hT = sb(st, "hT", [128, KC, 512], BF16)
                rstdn = sb(st, "rstdn", [128, 512], F32); tmpn = sb(st, "tmpn", [128, 512], F32)
                wsl = [sb(st, "w%d" % i, [128, KC, 512], BF16) for i in range(4)]
                cosT = sb(st, "cosT", [128, 512], F32); sinT = sb(st, "sinT", [128, 512], F32)
                sqc = [sb(st, "sqc%d" % i, [128, 512], BF16) for i in range(2)]
                tq = [sb(st, "tq%d" % i, [128, 512], F32) for i in range(2)]
                rq = [sb(st, "rq%d" % i, [128, 512], F32) for i in range(2)]
                qg = [sb(st, "qg%d" % i, [128, 512], BF16) for i in range(2)]
                t1 = [sb(st, "t1%d" % i, [128, 512], F32) for i in range(2)]
                t2 = [sb(st, "t2%d" % i, [128, 512], F32) for i in range(2)]
                cTs = [sb(st, "cT%d" % i, [128, 512], BF16) for i in range(3)]
                vtok = [sb(st, "vtok%d" % i, [128, 4, 128], BF16) for i in range(2)]
                stg = [sb(st, "stg%d" % i, [128, 4, 128], F32) for i in range(2)]
                wi = 0; cn = 0; vi = 0; si = 0
                for gi, (isp, t0, kv0, g) in enumerate(groups):
                    c = 1 if isp else 0
                    ph.dma("sp", lambda e, t0=t0: e.dma_start(out=xt.t[:], in_=XTv[:, :, t0:t0 + 512]),
                           [DB("XT", gi)], [xt.b], xt.b)
                    norm_mod(ph, xt, 512, [(0, 512)], L, c, 0, sq, hT, rstdn, tmpn, psC)
                    if not isp:
                        ph.dma("sp", lambda e, t0=t0: e.dma_start(out=cosT.t[:], in_=ropec[:, t0:t0 + 512]), [], [cosT.b], cosT.b)
                        ph.dma("sp", lambda e, t0=t0: e.dma_start(out=sinT.t[:], in_=ropes[:, t0:t0 + 512]), [], [sinT.b], sinT.b)
                    w = None
                    pend = None
                    for m in range(nchunks):
                        if m % 4 == 0:
                            w = wsl[wi % 4]; wi += 1
                            ncol = min(512, (nchunks - m) * 128)
                            ph.dma("pool", lambda e, w=w, m=m, ncol=ncol: e.dma_start(out=w.t[:, :, 0:ncol], in_=Wqkv[:, :, m * 128:m * 128 + ncol]),
                                   [], [w.b], w.b)
                        typ = "q" if m < nq else ("k" if m < nq + nk else "v")
                        hidx = m if typ == "q" else (m - nq if typ == "k" else m - nq - nk)
                        pm, pmb = (psA.t[:, cn % 2, :], bA[cn % 2])
                        x2 = cn % 2; cn += 1
                        for k in range(KC):
                            ph.op("pe", lambda e, w=w, m=m, k=k, pm=pm: e.matmul(
                                pm, lhsT=w.t[:, k, (m % 4) * 128:(m % 4 + 1) * 128], rhs=hT.t[:, k, :],
                                start=(k == 0), stop=(k == KC - 1)), [w.b, hT.b], [pmb])
                        cT = cTs[cn % 3]
                        if typ == "v":
                            ph.op("act", lambda e, cT=cT, pm=pm: e.activation(out=cT.t[:], in_=pm, func=AF.Copy), [pmb], [cT.b])
                        else:
                            gcol = 0 if typ == "q" else 1
                            ph.op("act", lambda e, x2=x2, pm=pm: e.activation(out=sqc[x2].t[:], in_=pm, func=AF.Square), [pmb], [sqc[x2].b])
                            ph.op("act", lambda e, x2=x2, pm=pm, gcol=gcol: e.activation(
                                out=qg[x2].t[:], in_=pm, func=AF.Copy, scale=gs.t[:, gcol:gcol + 1]), [pmb, gs.b], [qg[x2].b])
                        if pend is not None:
                            pend()

                        def tail(typ=typ, hidx=hidx, x2=x2, cT=cT, isp=isp, t0=t0, kv0=kv0, g=g, gi=gi):
                            nonlocal vi, si
                            if typ != "v":
                                ph.op("pe", lambda e: e.matmul(psB.t[:, 0, :], lhsT=onesb, rhs=sqc[x2].t[:], start=True, stop=True),
                                      [sqc[x2].b, cst.b], [bB[0]])
                                ph.op("act", lambda e: e.activation(out=tq[x2].t[:], in_=psB.t[:, 0, :], func=AF.Sqrt, scale=1.0 / 128, bias=EPS),
                                      [bB[0]], [tq[x2].b])
                                ph.op("dve", lambda e: e.reciprocal(out=rq[x2].t[:], in_=tq[x2].t[:]), [tq[x2].b], [rq[x2].b])
                                if isp:
                                    ph.op("dve", lambda e: e.tensor_tensor(out=cT.t[:], in0=qg[x2].t[:], in1=rq[x2].t[:], op=ALU.mult),
                                          [qg[x2].b, rq[x2].b], [cT.b])
                                else:
                                    ph.op("pe", lambda e: e.matmul(psB.t[:, 1, :], lhsT=rotm, rhs=qg[x2].t[:], start=True, stop=True),
                                          [qg[x2].b, cst.b], [bB[1]])
                                    ph.op("pool", lambda e: e.tensor_tensor(out=t1[x2].t[:], in0=qg[x2].t[:], in1=cosT.t[:], op=ALU.mult),
                                          [qg[x2].b, cosT.b], [t1[x2].b])
                                    ph.op("dve", lambda e: e.tensor_tensor(out=t2[x2].t[:], in0=psB.t[:, 1, :], in1=sinT.t[:], op=ALU.mult),
                                          [bB[1], sinT.b], [t2[x2].b])
                                    ph.op("pool", lambda e: e.tensor_tensor(out=t1[x2].t[:], in0=t1[x2].t[:], in1=t2[x2].t[:], op=ALU.add),
                                          [t1[x2].b, t2[x2].b], [t1[x2].b])
                                    ph.op("dve", lambda e: e.tensor_tensor(out=cT.t[:], in0=t1[x2].t[:], in1=rq[x2].t[:], op=ALU.mult),
                                          [t1[x2].b, rq[x2].b], [cT.b])
                            if typ == "q":
                                ph.dma("sp", lambda e: e.dma_start(out=QT[hidx, :, t0:t0 + 512], in_=cT.t[:]),
                                       [cT.b], [DB("QT", gi)], cT.b)
                            elif typ == "k":
                                ph.dma("sp", lambda e: e.dma_start(out=KT[hidx, :, kv0:kv0 + 512], in_=cT.t[:]),
                                       [cT.b], [DB("KT", gi)], cT.b)
                            if typ == "v" or (typ == "k" and isp):
                                for tt in range(4):
                                    ph.op("pe", lambda e, tt=tt: e.transpose(
                                        psT.t[:, tt * 128:(tt + 1) * 128], in_=cT.t[:, tt * 128:(tt + 1) * 128], identity=identb),
                                        [cT.b, cst.b], [psT.b])
                                vt = None
                                if typ == "v":
                                    vt = vtok[vi % 2]; vi += 1
                                    ph.op("dve", lambda e, vt=vt: e.tensor_copy(out=vt.t[:], in_=psT.t[:, 0:512].rearrange("p (t f) -> p t f", t=4)),
                                          [psT.b], [vt.b])
                                    ph.dma("sp", lambda e, vt=vt: e.dma_start(
                                        out=VS[kv0:kv0 + 512, hidx * 128:(hidx + 1) * 128].rearrange("(t p) f -> p t f", p=128), in_=vt.t[:]),
                                        [vt.b], [DB("VS", gi)], vt.b)
                                if isp:
                                    sg_ = stg[si % 2]; si += 1
                                    if typ == "v":
                                        ph.op("act", lambda e, sg_=sg_, vt=vt: e.activation(out=sg_.t[:], in_=vt.t[:], func=AF.Copy), [vt.b], [sg_.b])
                                    else:
                                        ph.op("act", lambda e, sg_=sg_: e.activation(out=sg_.t[:], in_=psT.t[:, 0:512].rearrange("p (t f) -> p t f", t=4), func=AF.Copy),
                                              [psT.b], [sg_.b])
                                    dst = sk if typ == "k" else sv
                                    for s2 in range(2):
                                        sq_ = 2 * g + s2
                                        r0 = (sq_ * NLs + j) * SEQ
                                        ph.dma("sp", lambda e, sg_=sg_, dst=dst, r0=r0, s2=s2: e.dma_start(
                                            out=dst[r0:r0 + SEQ, hidx * 128:(hidx + 1) * 128].rearrange("(t p) f -> p t f", p=128),
                                            in_=sg_.t[:, 2 * s2:2 * s2 + 2, :]), [sg_.b], [], sg_.b)
                        pend = tail
                    pend()
                ph.emit()

            with ExitStack() as st:
                ph = Phase(ctx)
                nhg = AKV if isA else BH
                ncmap = 1 if isA else 2
                dva = 129 if isA else 257
                kt = [sb(st, "kt%d" % i, [128, ncmap, NKV], BF16) for i in range(2)]
                vt_ = [sb(st, "vt%d" % i, [128, NCH, dva], BF16) for i in range(2)]
                for v in vt_:
                    ph.op("pool", lambda e, v=v: e.memset(v.t[:, :, dva - 1:dva], 1.0), [], [v.b])
                qt = [sb(st, "qt%d" % i, [128, 4 if isA else 2, 512 if isA else 256], BF16) for i in range(2)]
                pT = [sb(st, "pT%d" % i, [128, 512], BF16) for i in range(3)]
                den = sb(st, "den", [128, 8], F32); rden = sb(st, "rden", [128, 8], F32)
                otok = [sb(st, "otok%d" % i, [128, 512], BF16) for i in range(2)]
                otmp = [sb(st, "otmp%d" % i, [128, 256], F32) for i in range(2)]
                junk = sb(st, "junk", [128, 256], F32)
                osb = [sb(st, "osb%d" % i, [128, 4, 512] if isA else [128, 2, 256], BF16) for i in range(2)]
                allkv = [DB("KT", gi) for gi in range(len(groups))] + [DB("KT", "ctx")]
                allv = [DB("VS", gi) for gi in range(len(groups))] + [DB("VS", "ctx")]
                if isA:
                    psS = [(psA.t[:, 0, :], bA[0]), (psA.t[:, 1, :], bA[1]), (psE.t[:], psE.b)]
                else:
                    psS = [(psA.t[:, 0, :], bA[0]), (psA.t[:, 1, :], bA[1]), (psB.t[:, 0, :], bB[0])]
                if isA:
                    esk = sb(st, "esk", [128, AH], F32)
                    ph.dma("sp", lambda e: e.dma_start(out=esk.t[:], in_=a_sink[j:j + 1, :].partition_broadcast(128)), [], [esk.b], esk.b)
                    ph.op("act", lambda e: e.activation(out=esk.t[:], in_=esk.t[:], func=AF.Exp), [esk.b], [esk.b])
                    accs = [(psB.t[:, 0, 0:129], bB[0]), (psB.t[:, 1, 0:129], bB[1]), (psC.t[:, 0:129], psC.b), (psD.t[:, 0:129], psD.b)]
                else:
                    lv = sb(st, "lv", [128, 4, 128], F32)
                    for q in range(4):
                        ph.dma("sp", lambda e, q=q: e.dma_start(out=lv.t[:, q, :], in_=b_l[q][j:j + 1, :].partition_broadcast(128)), [], [lv.b], lv.b)
                    lp = sb(st, "lp", [128, 2, 128], F32); ls = sb(st, "ls", [128, 2], F32); nlam = sb(st, "nlam", [128, 1], F32)
                    ph.op("dve", lambda e: e.tensor_tensor(out=lp.t[:, 0, :], in0=lv.t[:, 0, :], in1=lv.t[:, 1, :], op=ALU.mult), [lv.b], [lp.b])
                    ph.op("dve", lambda e: e.tensor_tensor(out=lp.t[:, 1, :], in0=lv.t[:, 2, :], in1=lv.t[:, 3, :], op=ALU.mult), [lv.b], [lp.b])
                    ph.op("dve", lambda e: e.reduce_sum(out=ls.t[:], in_=lp.t[:], axis=mybir.AxisListType.X), [lp.b], [ls.b])
                    ph.op("act", lambda e: e.activation(out=ls.t[:], in_=ls.t[:], func=AF.Exp), [ls.b], [ls.b])
                    ph.op("dve", lambda e: e.tensor_tensor(out=nlam.t[:], in0=ls.t[:, 1:2], in1=ls.t[:, 0:1], op=ALU.subtract), [ls.b], [nlam.b])
                    ph.op("dve", lambda e: e.tensor_scalar(out=nlam.t[:], in0=nlam.t[:], scalar1=-lam_init, scalar2=None, op0=ALU.add), [nlam.b], [nlam.b])
                    subT = sb(st, "subT", [128, 2], F32)
                    ph.dma("sp", lambda e: e.dma_start(out=subT.t[:], in_=b_sub[j, :].rearrange("(c p) -> p c", p=128), allow_slow_non_contiguous=True),
                           [], [subT.b], subT.b)
                    ph.op("act", lambda e: e.mul(out=subT.t[:], in_=subT.t[:], mul=1.0 - lam_init), [subT.b], [subT.b])
                    accs = [(psB.t[:, 1, 0:257], bB[1]), (psC.t[:, 0:257], psC.b), (psD.t[:, 0:257], psD.b), (psE.t[:, 0:257], psE.b)]
                si = 0; pi = 0; oi = 0; qi = 0; ai = 0
                if not isA:
                    otokB = [[sb(st, "otkB%d%d" % (u, q), [128, 256], BF16) for q in range(2)] for u in range(2)]
                    otmpB = [[sb(st, "otmB%d%d" % (u, q), [128, 256], F32) for q in range(2)] for u in range(2)]
                dq = []

                def step():
                    due = []; keep = []
                    for it in dq:
                        it[0] -= 1
                        (due if it[0] <= 0 else keep).append(it)
                    dq[:] = keep
                    for it in due:
                        it[1]()

                def flush():
                    while dq:
                        step()

                for hg in range(nhg):
                    ktile = kt[hg % 2]; vtile = vt_[hg % 2]
                    ph.dma("sp", lambda e, ktile=ktile, hg=hg: e.dma_start(out=ktile.t[:], in_=KTv[:, hg * ncmap:(hg + 1) * ncmap, :]),
                           allkv, [ktile.b], ktile.b)
                    dvw = dva - 1
                    ph.dma("sp", lambda e, vtile=vtile, hg=hg, dvw=dvw: e.dma_start(
                        out=vtile.t[:, :, 0:dvw], in_=VS[:, hg * dvw:(hg + 1) * dvw].rearrange("(c p) f -> p c f", p=128)),
                        allv, [vtile.b], vtile.b)
                    units = []
                    if isA:
                        for gi, (isp, t0, kv0, g) in enumerate(groups):
                            blocks = []
                            for blk in range(4):
                                if isp:
                                    cb0 = (kv0 + (blk // 2) * 256) // 128
                                    chunks = [(cb0, None), (cb0 + 1, None)]
                                else:
                                    jb = g * 4 + blk
                                    chunks = []
                                    if jb > 0:
                                        chunks.append((jb - 1, trip))
                                    chunks.append((jb, None))
                                    if jb < NS // 128 - 1:
                                        chunks.append((jb + 1, trin))
                                    chunks += [(NS // 128 + c_, None) for c_ in range(PAST // 128)]
                                blocks.append(chunks)
                            units.append((gi, t0, blocks))
                    else:
                        for gi, (isp, t0, kv0, g) in enumerate(groups):
                            for hf in range(2):
                                if isp:
                                    cb0 = (kv0 + hf * 256) // 128
                                    chunks = [(cb0, None), (cb0 + 1, None)]
                                else:
                                    chunks = [(c_, None) for c_ in range((NS + PAST) // 128)]
                                units.append((gi, t0 + hf * 256, [chunks]))
                    for (gi, t0, blocks) in units:
                        qtile = qt[qi % 2]; qi += 1
                        ob = osb[oi % 2]; oi += 1
                        if isA:
                            ph.dma("sp", lambda e, qtile=qtile, hg=hg, t0=t0: e.dma_start(out=qtile.t[:], in_=QTv[:, hg * 4:hg * 4 + 4, t0:t0 + 512]),
                                   [DB("QT", gi)], [qtile.b], qtile.b)
                        else:
                            ph.dma("sp", lambda e, qtile=qtile, hg=hg, t0=t0: e.dma_start(out=qtile.t[:], in_=QTv[:, hg * 2:hg * 2 + 2, t0:t0 + 256]),
                                   [DB("QT", gi)], [qtile.b], qtile.b)
                        for blk, chunks in enumerate(blocks):
                            nchk = len(chunks)
                            accset = accs
                            ai += 1
                            for cidx, (ci, msk) in enumerate(chunks):
                                ps_, psb_ = psS[si % 3]; si += 1
                                p_ = pT[pi % 3]; pi += 1
                                if isA:
                                    ph.op("pe", lambda e, ps_=ps_, ktile=ktile, ci=ci, qtile=qtile, blk=blk: e.matmul(
                                        ps_.rearrange("p (g t) -> p g t", g=4), lhsT=ktile.t[:, 0, ci * 128:(ci + 1) * 128],
                                        rhs=qtile.t[:, :, blk * 128:(blk + 1) * 128], start=True, stop=True),
                                        [ktile.b, qtile.b], [psb_])
                                else:
                                    for c_ in range(2):
                                        ph.op("pe", lambda e, ps_=ps_, ktile=ktile, ci=ci, qtile=qtile, c_=c_: e.matmul(
                                            ps_[:, c_ * 256:(c_ + 1) * 256], lhsT=ktile.t[:, c_, ci * 128:(ci + 1) * 128],
                                            rhs=qtile.t[:, c_, :], start=True, stop=True), [ktile.b, qtile.b], [psb_])
                                ph.op("act", lambda e, ps_=ps_, p_=p_: e.activation(out=p_.t[:], in_=ps_, func=AF.Exp), [psb_], [p_.b])
                                if msk is not None:
                                    ph.op("dve", lambda e, p_=p_, msk=msk: e.tensor_tensor(
                                        out=p_.t[:].rearrange("p (g t) -> p g t", g=4), in0=p_.t[:].rearrange("p (g t) -> p g t", g=4),
                                        in1=msk.unsqueeze(1).to_broadcast([128, 4, 128]), op=ALU.mult), [p_.b, cst.b], [p_.b])
                                step()

                                def pv(p_=p_, ci=ci, cidx=cidx, nchk=nchk, accset=accset, vtile=vtile):
                                    for a_ in range(4):
                                        ac, acb = accset[a_]
                                        ph.op("pe", lambda e, ac=ac, a_=a_: e.matmul(
                                            ac, lhsT=p_.t[:, a_ * 128:(a_ + 1) * 128], rhs=vtile.t[:, ci, :],
                                            start=(cidx == 0), stop=(cidx == nchk - 1)), [p_.b, vtile.b], [acb])
                                dq.append([1, pv])
                            lastblk = (blk == len(blocks) - 1)
                            if isA:
                                ot_ = otok[ai % 2]

                                def fin_a(accset=accset, ot_=ot_, hg=hg):
                                    for a_ in range(4):
                                        ac, acb = accset[a_]
                                        ph.op("dve", lambda e, ac=ac, a_=a_: e.tensor_tensor(
                                            out=den.t[:, a_:a_ + 1], in0=ac[:, 128:129],
                                            in1=esk.t[:, hg * 4 + a_:hg * 4 + a_ + 1], op=ALU.add), [acb, esk.b], [den.b])
                                    ph.op("dve", lambda e: e.reciprocal(out=rden.t[:, 0:4], in_=den.t[:, 0:4]), [den.b], [rden.b])
                                    for a_ in range(4):
                                        ac, acb = accset[a_]
                                        ph.op("dve", lambda e, ac=ac, a_=a_: e.tensor_scalar(
                                            out=ot_.t[:, a_ * 128:(a_ + 1) * 128], in0=ac[:, 0:128], scalar1=rden.t[:, a_:a_ + 1], scalar2=None,
                                            op0=ALU.mult), [acb, rden.b], [ot_.b])

                                def fin_b(ot_=ot_, ob=ob, blk=blk, lastblk=lastblk, hg=hg, t0=t0, gi=gi):
                                    for a_ in range(4):
                                        ph.op("pe", lambda e, a_=a_: e.transpose(
                                            psT.t[:, a_ * 128:(a_ + 1) * 128], in_=ot_.t[:, a_ * 128:(a_ + 1) * 128], identity=identb),
                                            [ot_.b, cst.b], [psT.b])
                                    ph.op("act", lambda e: e.activation(
                                        out=ob.t[:, :, blk * 128:(blk + 1) * 128], in_=psT.t[:, 0:512].rearrange("p (g t) -> p g t", g=4), func=AF.Copy),
                                        [psT.b], [ob.b])
                                    if lastblk:
                                        ph.dma("sp", lambda e: e.dma_start(out=OTv[:, hg * 4:hg * 4 + 4, t0:t0 + 512], in_=ob.t[:]),
                                               [ob.b], [DB("OT", gi)], ob.b)
                            else:
                                otk = otokB[ai % 2]; otm = otmpB[ai % 2]

                                def fin_a(accset=accset, otk=otk, otm=otm):
                                    for a_ in range(4):
                                        ac, acb = accset[a_]
                                        ph.op("dve", lambda e, ac=ac, a_=a_: e.tensor_copy(out=den.t[:, a_:a_ + 1], in_=ac[:, 256:257]), [acb], [den.b])
                                    ph.op("dve", lambda e: e.reciprocal(out=rden.t[:, 0:4], in_=den.t[:, 0:4]), [den.b], [rden.b])
                                    ph.op("dve", lambda e: e.tensor_scalar(out=rden.t[:, 2:4], in0=rden.t[:, 2:4], scalar1=nlam.t[:, 0:1], scalar2=None, op0=ALU.mult),
                                          [rden.b, nlam.b], [rden.b])
                                    for qb in range(2):
                                        om = otm[qb]
                                        a0, a0b = accset[qb]; a1, a1b = accset[2 + qb]
                                        ph.op("dve", lambda e, om=om, a0=a0, qb=qb: e.tensor_scalar(
                                            out=om.t[:], in0=a0[:, 0:256], scalar1=rden.t[:, qb:qb + 1], scalar2=None, op0=ALU.mult), [a0b, rden.b], [om.b])
                                        ph.op("dve", lambda e, om=om, a1=a1, qb=qb: e.scalar_tensor_tensor(
                                            out=om.t[:], in0=a1[:, 0:256], scalar=rden.t[:, 2 + qb:3 + qb], in1=om.t[:], op0=ALU.mult, op1=ALU.add),
                                            [a1b, rden.b, om.b], [om.b])
                                        ph.op("act", lambda e, om=om, qb=qb: e.activation(out=junk.t[:], in_=om.t[:], func=AF.Square, accum_out=den.t[:, 4 + qb:5 + qb]),
                                              [om.b], [junk.b, den.b])
                                        ph.op("act", lambda e, qb=qb: e.activation(out=den.t[:, 6 + qb:7 + qb], in_=den.t[:, 4 + qb:5 + qb], func=AF.Sqrt, scale=1.0 / 256, bias=EPS),
                                              [den.b], [den.b])
                                        ph.op("dve", lambda e, qb=qb: e.reciprocal(out=rden.t[:, 6 + qb:7 + qb], in_=den.t[:, 6 + qb:7 + qb]), [den.b], [rden.b])
                                        ot_ = otk[qb]
                                        ph.op("dve", lambda e, om=om, qb=qb, ot_=ot_: e.tensor_scalar(
                                            out=ot_.t[:, 0:256], in0=om.t[:], scalar1=rden.t[:, 6 + qb:7 + qb], scalar2=None, op0=ALU.mult), [om.b, rden.b], [ot_.b])

                                def fin_b(otk=otk, ob=ob, hg=hg, t0=t0, gi=gi):
                                    for qb in range(2):
                                        ot_ = otk[qb]
                                        for cc in range(2):
                                            ph.op("pe", lambda e, ot_=ot_, cc=cc, qb=qb: e.transpose(
                                                psT.t[:, (qb * 2 + cc) * 128:(qb * 2 + cc + 1) * 128], in_=ot_.t[:, cc * 128:(cc + 1) * 128], identity=identb),
                                                [ot_.b, cst.b], [psT.b])
                                    for qb in range(2):
                                        for cc in range(2):
                                            ph.op("act", lambda e, cc=cc, qb=qb: e.activation(
                                                out=ob.t[:, cc, qb * 128:(qb + 1) * 128], in_=psT.t[:, (qb * 2 + cc) * 128:(qb * 2 + cc + 1) * 128],
                                                func=AF.Copy, scale=subT.t[:, cc:cc + 1]), [psT.b, subT.b], [ob.b])
                                    ph.dma("sp", lambda e: e.dma_start(out=OTv[:, hg * 2:hg * 2 + 2, t0:t0 + 256], in_=ob.t[:]),
                                           [ob.b], [DB("OT", gi)], ob.b)
                            dq.append([1, fin_a])
                            dq.append([3, fin_b])
                    flush()
                ph.emit()

            with ExitStack() as st:
                ph = Phase(ctx)
                xt2 = [sb(st, "xt%d" % i, [128, KC, 512], F32) for i in range(2)]
                ot2 = [sb(st, "ot%d" % i, [128, KC, 512], BF16) for i in range(2)]
                wsl = [sb(st, "w%d" % i, [128, KC, 512], BF16) for i in range(4)]
                pss = [(psA.t[:, 0, :], bA[0]), (psA.t[:, 1, :], bA[1]), (psB.t[:, 0, :], bB[0]), (psB.t[:, 1, :], bB[1])]
                wi = 0; pi = 0
                for gi, (isp, t0, kv0, g) in enumerate(groups):
                    c = 1 if isp else 0
                    xt = xt2[gi % 2]; ot = ot2[gi % 2]
                    ph.dma("sp", lambda e, xt=xt, t0=t0: e.dma_start(out=xt.t[:], in_=XTv[:, :, t0:t0 + 512]), [DB("XT", gi)], [xt.b], xt.b)
                    ph.dma("sp", lambda e, ot=ot, t0=t0: e.dma_start(out=ot.t[:], in_=OTv[:, :, t0:t0 + 512]), [DB("OT", gi)], [ot.b], ot.b)
                    for pc in range(D // 512):
                        w = wsl[wi % 4]; wi += 1
                        ph.dma("pool", lambda e, w=w, pc=pc: e.dma_start(out=w.t[:], in_=Wo[:, :, pc * 512:(pc + 1) * 512]), [], [w.b], w.b)
                        for m in range(4):
                            f = pc * 4 + m
                            pm, pmb = pss[pi % 4]; pi += 1
                            for k in range(KC):
                                ph.op("pe", lambda e, w=w, m=m, k=k, pm=pm, ot=ot: e.matmul(
                                    pm, lhsT=w.t[:, k, m * 128:(m + 1) * 128], rhs=ot.t[:, k, :], start=(k == 0), stop=(k == KC - 1)),
                                    [w.b, ot.b], [pmb])
                            ph.op("dve", lambda e, pm=pm, xt=xt, f=f, c=c: e.scalar_tensor_tensor(
                                out=xt.t[:, f, :], in0=pm, scalar=mod.t[:, L, c, 2 * KC + f:2 * KC + f + 1], in1=xt.t[:, f, :],
                                op0=ALU.mult, op1=ALU.add), [pmb, mod.b, xt.b], [xt.b])
                    ph.dma("sp", lambda e, xt=xt, t0=t0: e.dma_start(out=X1Tv[:, :, t0:t0 + 512], in_=xt.t[:]), [xt.b], [DB("X1T", gi)], xt.b)
                ph.emit()

            with ExitStack() as st:
                ph = Phase(ctx)
                x1 = sb(st, "x1", [128, KC, 516], F32)
                sq = sb(st, "sq", [128, KC, 516], BF16)
                hT = sb(st, "hT", [128, KC, 516], BF16)
                rstdn = sb(st, "rstdn", [128, 516], F32); tmpn = sb(st, "tmpn", [128, 516], F32)
                aT = sb(st, "aT", [128, FC, 512], BF16)
                wsl = [sb(st, "w%d" % i, [128, KC, 512], BF16) for i in range(4)]
                cwT = sb(st, "cwT", [128, 4, FC], F32)
                for m0 in range(0, FC, 11):
                    m1 = min(FC, m0 + 11)
                    for q in range(3):
                        ph.dma("sp", lambda e, q=q, m0=m0, m1=m1: e.dma_start(
                            out=cwT.t[:, q, m0:m1], in_=cw[L * 3 + q, m0 * 128:m1 * 128].rearrange("(m p) -> p m", p=128),
                            allow_slow_non_contiguous=True), [], [cwT.b], cwT.b)
                    ph.dma("sp", lambda e, m0=m0, m1=m1: e.dma_start(
                        out=cwT.t[:, 3, m0:m1], in_=cbias[L, m0 * 128:m1 * 128].rearrange("(m p) -> p m", p=128),
                        allow_slow_non_contiguous=True), [], [cwT.b], cwT.b)
                tmpc = [sb(st, "tmpc%d" % i, [128, 2, 256], F32) for i in range(2)]
                sgl = [sb(st, "sgl%d" % i, [128, 2, 256], F32) for i in range(2)]
                x1v = x1.t[:].rearrange("p k (s t) -> p k s t", s=2)
                hTv = hT.t[:].rearrange("p k (s t) -> p k s t", s=2)
                aTv = aT.t[:].rearrange("p m (s t) -> p m s t", s=2)
                wi = 0; ci_ = 0
                kpieces = [(k0, min(k0 + 16, FC)) for k0 in range(0, FC, 16)]
                for gi, (isp, t0, kv0, g) in enumerate(groups):
                    c = 1 if isp else 0
                    for s in range(2):
                        ts_ = t0 + s * 256
                        if isp:
                            left = right = False
                        else:
                            left = ts_ > 0
                            right = ts_ + 256 < NS
                        lo = ts_ - (1 if left else 0)
                        n_ = 256 + (1 if left else 0) + (1 if right else 0)
                        off = s * 258 + (0 if left else 1)
                        if not left:
                            ph.op("pool", lambda e, s=s: e.memset(x1.t[:, :, s * 258:s * 258 + 1], 0.0), [], [x1.b])
                        if not right:
                            ph.op("pool", lambda e, s=s: e.memset(x1.t[:, :, s * 258 + 257:s * 258 + 258], 0.0), [], [x1.b])
                        rb = [DB("X1T", gi)]
                        if left and (ts_ % 512 == 0):
                            rb.append(DB("X1T", gi - 1))
                        if right and ((ts_ + 256) % 512 == 0):
                            rb.append(DB("X1T", gi + 1))
                        ph.dma("sp", lambda e, lo=lo, n_=n_, off=off: e.dma_start(out=x1.t[:, :, off:off + n_], in_=X1Tv[:, :, lo:lo + n_]),
                               rb, [x1.b], x1.b)
                    norm_mod(ph, x1, 516, [(0, 258), (258, 258)], L, c, 1, sq, hT, rstdn, tmpn, psC)
                    for s in range(2):
                        ts_ = t0 + s * 256
                        left = (not isp) and ts_ > 0
                        right = (not isp) and (ts_ + 256 < NS)
                        if not left:
                            ph.op("pool", lambda e, s=s: e.memset(hT.t[:, :, s * 258:s * 258 + 1], 0.0), [hT.b], [hT.b])
                        if not right:
                            ph.op("pool", lambda e, s=s: e.memset(hT.t[:, :, s * 258 + 257:s * 258 + 258], 0.0), [hT.b], [hT.b])
                    for pc in range(DFF // 512):
                        wg = wsl[wi % 4]; wv = wsl[(wi + 1) % 4]; wi += 2
                        ph.dma("pool", lambda e, wg=wg, pc=pc: e.dma_start(out=wg.t[:], in_=Wup[:, :, pc * 512:(pc + 1) * 512]), [], [wg.b], wg.b)
                        ph.dma("pool", lambda e, wv=wv, pc=pc: e.dma_start(out=wv.t[:], in_=Wup[:, :, DFF + pc * 512:DFF + (pc + 1) * 512]), [], [wv.b], wv.b)
                        for m in range(4):
                            fm = pc * 4 + m
                            for (w_, ps_, pbs) in ((wg, psA, bA), (wv, psB, bB)):
                                for s in range(2):
                                    for k in range(KC):
                                        ph.op("pe", lambda e, w_=w_, ps_=ps_, s=s, k=k, m=m: e.matmul(
                                            ps_.t[:, s, 0:258], lhsT=w_.t[:, k, m * 128:(m + 1) * 128], rhs=hTv[:, k, s, :],
                                            start=(k == 0), stop=(k == KC - 1)), [w_.b, hT.b], [pbs[s]])
                            tc_ = tmpc[ci_ % 2]; sg_ = sgl[ci_ % 2]; ci_ += 1
                            ph.op("act", lambda e, tc_=tc_, fm=fm: e.activation(
                                out=tc_.t[:], in_=psA.t[:, :, 1:257], func=AF.Identity, scale=cwT.t[:, 1, fm:fm + 1], bias=cwT.t[:, 3, fm:fm + 1]),
                                [bA[0], bA[1], cwT.b], [tc_.b])
                            ph.op("dve", lambda e, tc_=tc_, fm=fm: e.scalar_tensor_tensor(
                                out=tc_.t[:], in0=psA.t[:, :, 0:256], scalar=cwT.t[:, 0, fm:fm + 1], in1=tc_.t[:], op0=ALU.mult, op1=ALU.add),
                                [bA[0], bA[1], cwT.b, tc_.b], [tc_.b])
                            ph.op("dve", lambda e, tc_=tc_, fm=fm: e.scalar_tensor_tensor(
                                out=tc_.t[:], in0=psA.t[:, :, 2:258], scalar=cwT.t[:, 2, fm:fm + 1], in1=tc_.t[:], op0=ALU.mult, op1=ALU.add),
                                [bA[0], bA[1], cwT.b, tc_.b], [tc_.b])
                            ph.op("act", lambda e, tc_=tc_, sg_=sg_: e.activation(out=sg_.t[:], in_=tc_.t[:], func=AF.Silu), [tc_.b], [sg_.b])
                            ph.op("dve", lambda e, sg_=sg_, fm=fm: e.tensor_tensor(
                                out=aTv[:, fm, :, :], in0=sg_.t[:], in1=psB.t[:, :, 1:257], op=ALU.mult), [sg_.b, bB[0], bB[1]], [aT.b])
                    pso = [(psA.t[:, 0, :], bA[0]), (psA.t[:, 1, :], bA[1]), (psB.t[:, 0, :], bB[0]), (psB.t[:, 1, :], bB[1])]
                    for pb_ in range(D // 512):
                        for (k0, k1) in kpieces:
                            w = wsl[wi % 4]; wi += 1
                            ph.dma("pool", lambda e, w=w, k0=k0, k1=k1, pb_=pb_: e.dma_start(
                                out=w.t[:, 0:k1 - k0, :], in_=Wdn[:, k0:k1, pb_ * 512:(pb_ + 1) * 512]), [], [w.b], w.b)
                            for o_ in range(4):
                                po, pob = pso[o_]
                                for k in range(k0, k1):
                                    ph.op("pe", lambda e, w=w, po=po, k=k, k0=k0, o_=o_: e.matmul(
                                        po, lhsT=w.t[:, k - k0, o_ * 128:(o_ + 1) * 128], rhs=aT.t[:, k, :],
                                        start=(k == 0), stop=(k == FC - 1)), [w.b, aT.b], [pob])
                        for o_ in range(4):
                            f = pb_ * 4 + o_
                            po, pob = pso[o_]
                            ph.op("dve", lambda e, po=po, f=f, c=c: e.scalar_tensor_tensor(
                                out=x1v[:, f, :, 1:257], in0=po.rearrange("p (s t) -> p s t", s=2), scalar=mod.t[:, L, c, 5 * KC + f:5 * KC + f + 1],
                                in1=x1v[:, f, :, 1:257], op0=ALU.mult, op1=ALU.add), [pob, mod.b, x1.b], [x1.b])
                    for s in range(2):
                        ph.dma("sp", lambda e, s=s, t0=t0: e.dma_start(out=XTv[:, :, t0 + s * 256:t0 + (s + 1) * 256], in_=x1v[:, :, s, 1:257]),
                               [x1.b], [DB("XT", gi)], x1.b)
                ph.emit()

        with ExitStack() as st:
            ph = Phase(ctx)
            xin = [sb(st, "fx%d" % i, [128, KC, 512], F32) for i in range(2)]
            yst = [sb(st, "yst%d" % i, [128, D], F32) for i in range(2)]
            pss = [(psA.t[:, 0, :], bA[0]), (psA.t[:, 1, :], bA[1]), (psB.t[:, 0, :], bB[0]), (psB.t[:, 1, :], bB[1]),
                   (psC.t[:], psC.b), (psD.t[:], psD.b), (psE.t[:], psE.b)]
            pi = 0; yi = 0
            for gi, (isp, t0, kv0, g) in enumerate(groups):
                xi = xin[gi % 2]
                dst = yp if isp else ys
                r0 = t0 - NS if isp else t0
                ph.dma("sp", lambda e, xi=xi, t0=t0: e.dma_start(out=xi.t[:], in_=XTv[:, :, t0:t0 + 512]), [DB("XT", gi)], [xi.b], xi.b)
                for tt in range(4):
                    yo = yst[yi % 2]; yi += 1
                    for k4 in range(KC // 4):
                        pt, pb = pss[pi % 7]; pi += 1
                        for q in range(4):
                            k = k4 * 4 + q
                            ph.op("pe", lambda e, pt=pt, xi=xi, k=k, q=q, tt=tt: e.transpose(
                                pt[:, q * 128:(q + 1) * 128], in_=xi.t[:, k, tt * 128:(tt + 1) * 128], identity=identf.t[:]),
                                [xi.b, identf.b], [pb])
                        if pi % 2:
                            ph.op("act", lambda e, pt=pt, yo=yo, k4=k4: e.activation(out=yo.t[:, k4 * 512:(k4 + 1) * 512], in_=pt, func=AF.Copy), [pb], [yo.b])
                        else:
                            ph.op("dve", lambda e, pt=pt, yo=yo, k4=k4: e.tensor_copy(out=yo.t[:, k4 * 512:(k4 + 1) * 512], in_=pt), [pb], [yo.b])
                    ph.dma("sp", lambda e, yo=yo, dst=dst, r=r0 + tt * 128: e.dma_start(out=dst[r:r + 128, :], in_=yo.t[:]), [yo.b], [], yo.b)
            ph.emit()
        final_wait(ctx)
    return nc


def _rope_tables(NS, GRID_W):
    half = 64
    t = np.arange(NS)
    row = (t // GRID_W).astype(np.float32)
    col = (t % GRID_W).astype(np.float32)
    inv = (10000.0 ** (-np.arange(0, half, 2, dtype=np.float32) / half)).astype(np.float32)
    ang = np.concatenate([row[:, None] * inv, col[:, None] * inv], axis=-1).astype(np.float32)
    cos, sin = np.cos(ang), np.sin(ang)
    C = np.zeros((128, NS), np.float32); S = np.zeros((128, NS), np.float32)
    for d in range(128):
        hf = d // 64; r = d % 64
        fi = hf * 32 + (r % 32)
        C[d] = cos[:, fi]
        S[d] = -sin[:, fi] if r < 32 else sin[:, fi]
    return C, S


def _consts():
    c = np.zeros((128, 640), np.float32)
    c[:, 0:128] = np.eye(128, dtype=np.float32)
    for dp in range(128):
        r = dp % 64
        partner = dp + 32 if r < 32 else dp - 32
        c[partner, 128 + dp] = 1.0
    tk = np.arange(128)[:, None]; tq = np.arange(128)[None, :]
    c[:, 256:384] = (tq <= tk).astype(np.float32)
    c[:, 384:512] = (tk <= tq).astype(np.float32)
    c[:, 512:640] = 1.0
    return c


def make_in_maps(cfg, inputs, ncores):
    D, DEPTH, NS, PAST, NPS, SEQ = cfg["D"], cfg["DEPTH"], cfg["NS"], cfg["PAST"], cfg["NPS"], cfg["SEQ"]
    NA, NB = (DEPTH + 1) // 2, DEPTH // 2
    f = lambda a: np.ascontiguousarray(np.asarray(a, dtype=np.float32))
    C, S = _rope_tables(NS, cfg["GRID_W"])
    shared = {
        "ada_w": f(inputs["ada_w"]).reshape(DEPTH * D, 6 * D), "ada_b": f(inputs["ada_b"]),
        "norm1_g": f(inputs["norm1_g"]), "norm2_g": f(inputs["norm2_g"]),
        "a_w_qkv": f(inputs["a_w_qkv"]).reshape(NA * D, -1), "a_q_norm": f(inputs["a_q_norm"]), "a_k_norm": f(inputs["a_k_norm"]),
        "a_sink": f(inputs["a_sink"]), "a_w_o": f(inputs["a_w_o"]).reshape(NA * D, D),
        "b_w_qkv": f(inputs["b_w_qkv"]).reshape(NB * D, 3 * D), "b_q_norm": f(inputs["b_q_norm"]), "b_k_norm": f(inputs["b_k_norm"]),
        "b_lambda_q1": f(inputs["b_lambda_q1"]), "b_lambda_k1": f(inputs["b_lambda_k1"]),
        "b_lambda_q2": f(inputs["b_lambda_q2"]), "b_lambda_k2": f(inputs["b_lambda_k2"]),
        "b_subln": f(inputs["b_subln"]), "b_w_o": f(inputs["b_w_o"]).reshape(NB * D, D),
        "ffn_w_up": f(inputs["ffn_w_up"]).reshape(DEPTH * D, -1), "ffn_conv_w": f(inputs["ffn_conv_w"]).reshape(DEPTH * 3, -1),
        "ffn_conv_b": f(inputs["ffn_conv_b"]), "ffn_w_down": f(inputs["ffn_w_down"]).reshape(-1, D),
        "rope_cos": C, "rope_sin": S, "consts": _consts(),
    }
    xs = f(inputs["x_sample"]); xp = f(inputs["x_prompt"])
    cak = f(inputs["cache_a_k"]); cav = f(inputs["cache_a_v"]); cbk = f(inputs["cache_b_k"]); cbv = f(inputs["cache_b_v"])
    cc = f(inputs["c"]); cctx = f(inputs["c_ctx"])
    maps = []
    for i in range(ncores):
        m = dict(shared)
        m["xs"] = xs[i]
        m["xp"] = xp[i * NPS:(i + 1) * NPS].reshape(NPS * SEQ, D)
        m["cak"] = cak[i].reshape(NA * PAST, -1); m["cav"] = cav[i].reshape(NA * PAST, -1)
        m["cbk"] = cbk[i].reshape(NB * PAST, -1); m["cbv"] = cbv[i].reshape(NB * PAST, -1)
        m["cond"] = np.ascontiguousarray(np.stack([cc[i], cctx], axis=0))
        maps.append(m)
    return maps


def gather(cfg, results, ncores):
    D, DEPTH, NS, NPS, SEQ = cfg["D"], cfg["DEPTH"], cfg["NS"], cfg["NPS"], cfg["SEQ"]
    NA, NB = (DEPTH + 1) // 2, DEPTH // 2
    ys = np.stack([results[i]["ys"] for i in range(ncores)], 0)
    yp = np.concatenate([results[i]["yp"].reshape(NPS, SEQ, D) for i in range(ncores)], 0)
    sak = np.concatenate([results[i]["sak"].reshape(NPS, NA, SEQ, cfg["AKV"], 128) for i in range(ncores)], 0)
    sav = np.concatenate([results[i]["sav"].reshape(NPS, NA, SEQ, cfg["AKV"], 128) for i in range(ncores)], 0)
    sbk = np.concatenate([results[i]["sbk"].reshape(NPS, NB, SEQ, cfg["BH"], 2, 128) for i in range(ncores)], 0)
    sbv = np.concatenate([results[i]["sbv"].reshape(NPS, NB, SEQ, cfg["BH"], 256) for i in range(ncores)], 0)
    return (yp.astype(np.float32), ys.astype(np.float32), sak.astype(np.float32), sav.astype(np.float32),
            sbk.astype(np.float32), sbv.astype(np.float32))


def run(cfg, inputs, ncores=8):
    nc = build(cfg)
    maps = make_in_maps(cfg, inputs, ncores)
    res = run_bass_kernel_spmd(nc, maps, core_ids=list(range(ncores)))
    return gather(cfg, res.results, ncores)


def kernel(**inputs):
    return run(CFG_FULL, inputs, 8)
```
